# Optimizing a Trainium2 kernel written in Bass

```python
import math
import jax, jax.numpy as jnp
from jax import lax
import numpy as np

D_MODEL = 2048
BATCH = 4
SEQ = 4096
DEPTH = 4

N_MIXERS = 2
N_ATTN_LAYERS = (DEPTH + 1) // 2
N_RWKV_LAYERS = DEPTH // 2
N_VRES_LAYERS = max(N_RWKV_LAYERS - 1, 0)

DA_HEAD_DIM = 128
DA_VALUE_DIM = 2 * DA_HEAD_DIM
DA_HEADS = D_MODEL // DA_VALUE_DIM
DA_WIDTH = DA_HEADS * DA_VALUE_DIM
DA_IN_COLS = 4 * DA_WIDTH
ROPE_THETA = 10000.0
Q_BLOCK = 128

RW_HEAD_DIM = 64
RW_HEADS = D_MODEL // RW_HEAD_DIM
RW_WIDTH = RW_HEADS * RW_HEAD_DIM
DECAY_RANK = max(32, int(round(1.8 * D_MODEL ** 0.5 / 32)) * 32)
ICLR_RANK = max(32, int(round(1.8 * D_MODEL ** 0.5 / 32)) * 32)
VRES_RANK = max(32, int(round(1.3 * D_MODEL ** 0.5 / 32)) * 32)
RW_IN_COLS = 4 * RW_WIDTH + DECAY_RANK + ICLR_RANK
RW_SPLITS = [RW_WIDTH, 2 * RW_WIDTH, 3 * RW_WIDTH, 4 * RW_WIDTH, 4 * RW_WIDTH + DECAY_RANK]
GN_EPS = 64e-5
NORM_EPS = 1e-6

kernel_name = "hybrid_diffattn_rwkv7_interleaved"

F32 = jnp.float32


def rms_norm(x, gain, eps=NORM_EPS):
    xf = x.astype(F32)
    y = xf * lax.rsqrt(jnp.mean(xf * xf, axis=-1, keepdims=True) + eps)
    return (y * gain.astype(F32)).astype(x.dtype)


def rope_tables(positions):
    inv = ROPE_THETA ** (-jnp.arange(0, DA_HEAD_DIM, 2, dtype=F32) / DA_HEAD_DIM)
    ang = positions.astype(F32)[..., None] * inv
    ang = jnp.concatenate([ang, ang], axis=-1)
    return jnp.cos(ang)[:, :, None, None, :], jnp.sin(ang)[:, :, None, None, :]


def apply_rope(x, cos, sin):
    xf = x.astype(F32)
    x1, x2 = jnp.split(xf, 2, axis=-1)
    rot = jnp.concatenate([-x2, x1], axis=-1)
    return (xf * cos + rot * sin).astype(x.dtype)


def causal_diff_attention(q, k, v, lam):
    B, T, H, _, dh = q.shape
    dv = v.shape[-1]
    nb = T // Q_BLOCK
    scale = dh ** -0.5
    qb = jnp.moveaxis(q.reshape(B, nb, Q_BLOCK, H, 2, dh), 1, 0)
    key_pos = jnp.arange(T)
    vf = v.astype(F32)
    neg = jnp.finfo(F32).min

    def one_block(args):
        j, qj = args
        s = jnp.einsum('bqhcd,bkhcd->bhcqk', qj, k).astype(F32) * scale
        q_pos = j * Q_BLOCK + jnp.arange(Q_BLOCK)
        mask = key_pos[None, :] <= q_pos[:, None]
        p = jax.nn.softmax(jnp.where(mask, s, neg), axis=-1)
        a = p[:, :, 0] - lam * p[:, :, 1]
        return jnp.einsum('bhqk,bkhd->bqhd', a, vf)

    out = lax.map(one_block, (jnp.arange(nb), qb))
    return jnp.moveaxis(out, 0, 1).reshape(B, T, H, dv)


def diff_attention_mixer(h, cos, sin, w_in, q_gain, k_gain, lam_q1, lam_k1, lam_q2, lam_k2,
                         subln_w, w_out, lambda_init):
    B, T, _ = h.shape
    proj = h @ w_in
    q, k, v, gate = jnp.split(proj, 4, axis=-1)
    q = q.reshape(B, T, DA_HEADS, 2, DA_HEAD_DIM)
    k = k.reshape(B, T, DA_HEADS, 2, DA_HEAD_DIM)
    v = v.reshape(B, T, DA_HEADS, DA_VALUE_DIM)
    q = apply_rope(rms_norm(q, q_gain), cos, sin)
    k = apply_rope(rms_norm(k, k_gain), cos, sin)
    lam = (jnp.exp(jnp.sum(lam_q1.astype(F32) * lam_k1.astype(F32)))
           - jnp.exp(jnp.sum(lam_q2.astype(F32) * lam_k2.astype(F32))) + lambda_init)
    o = causal_diff_attention(q, k, v, lam)
    o = rms_norm(o, subln_w) * (1.0 - lambda_init)
    o = o.reshape(B, T, DA_WIDTH).astype(h.dtype) * jax.nn.silu(gate)
    return o @ w_out


def token_shift(p):
    return jnp.pad(p, ((0, 0), (1, 0), (0, 0)))[:, :-1]


def wkv7_scan(r, w, k, v, a, b):
    B, T, H, N = r.shape
    xs = tuple(jnp.moveaxis(t, 1, 0) for t in (r, w, k, v, a, b))

    def step(S, inp):
        r_t, w_t, k_t, v_t, a_t, b_t = inp
        sa = jnp.einsum('bhvk,bhk->bhv', S, a_t)
        S = S * w_t[:, :, None, :] + sa[..., None] * b_t[:, :, None, :] + v_t[..., None] * k_t[:, :, None, :]
        y = jnp.einsum('bhvk,bhk->bhv', S, r_t)
        return S, y

    _, ys = lax.scan(step, jnp.zeros((B, H, N, N), F32), xs)
    return jnp.moveaxis(ys, 0, 1)


def rwkv7_mixer(h, v_first, w_in, mu, w0, decay_up, a0, iclr_up, k_k, k_a, r_k, gn_w, gn_b,
                w_out, vres):
    B, T, _ = h.shape
    p = h @ w_in
    p = p + (token_shift(p) - p) * mu
    r, k, v, g, dw, da = jnp.split(p, RW_SPLITS, axis=-1)
    r, k, v = r.astype(F32), k.astype(F32), v.astype(F32)
    w_log = -jax.nn.softplus(-(w0.astype(F32) + jnp.tanh(dw.astype(F32)) @ decay_up.astype(F32))) - 0.5
    decay = jnp.exp(-jnp.exp(w_log))
    a = jax.nn.sigmoid(a0.astype(F32) + da.astype(F32) @ iclr_up.astype(F32))
    if vres is None:
        v_first = v
    else:
        vd_w, vd_mu, v0, vu_w = vres
        pv = h @ vd_w
        pv = pv + (token_shift(pv) - pv) * vd_mu
        v = v + (v_first - v) * jax.nn.sigmoid(v0.astype(F32) + pv.astype(F32) @ vu_w.astype(F32))
    heads = lambda t: t.reshape(B, T, RW_HEADS, RW_HEAD_DIM)
    kk = heads(k * k_k.astype(F32))
    kk = kk / jnp.maximum(jnp.sqrt(jnp.sum(kk * kk, axis=-1, keepdims=True)), 1e-12)
    k = k * (1.0 + (a - 1.0) * k_a.astype(F32))
    rh, kh, vh, ah = heads(r), heads(k), heads(v), heads(a)
    y = wkv7_scan(rh, heads(decay), kh, vh, -kk, kk * ah)
    mean = jnp.mean(y, axis=-1, keepdims=True)
    var = jnp.mean(jnp.square(y - mean), axis=-1, keepdims=True)
    y = (y - mean) * lax.rsqrt(var + GN_EPS)
    y = y * gn_w.astype(F32).reshape(RW_HEADS, RW_HEAD_DIM) + gn_b.astype(F32).reshape(RW_HEADS, RW_HEAD_DIM)
    y = y + jnp.sum(rh * kh * r_k.astype(F32), axis=-1, keepdims=True) * vh
    y = y.reshape(B, T, RW_WIDTH).astype(h.dtype) * jax.nn.silu(g)
    return y @ w_out, v_first


def setup_inputs(seed: int = 0) -> dict:
    key = jax.random.key(seed)
    ks = jax.random.split(key, 32)
    nrm = lambda k, s, sc: jax.random.normal(k, s, F32) * sc
    NA, NR, NV = N_ATTN_LAYERS, N_RWKV_LAYERS, N_VRES_LAYERS
    x = jax.random.normal(ks[0], (BATCH, SEQ, D_MODEL), F32)
    offs = jax.random.randint(ks[1], (BATCH, 1), 0, 1024, dtype=jnp.int32)
    positions = (offs + jnp.arange(SEQ, dtype=jnp.int32)[None, :]).astype(jnp.int32)
    return {
        "x": x,
        "positions": positions,
        "norm_w": 1.0 + nrm(ks[2], (DEPTH, D_MODEL), 0.02),
        "da_w_in": nrm(ks[3], (NA, D_MODEL, DA_IN_COLS), D_MODEL ** -0.5),
        "da_q_gain": 1.0 + nrm(ks[4], (NA, DA_HEAD_DIM), 0.02),
        "da_k_gain": 1.0 + nrm(ks[5], (NA, DA_HEAD_DIM), 0.02),
        "da_lam_q1": nrm(ks[6], (NA, DA_HEAD_DIM), 0.1),
        "da_lam_k1": nrm(ks[7], (NA, DA_HEAD_DIM), 0.1),
        "da_lam_q2": nrm(ks[8], (NA, DA_HEAD_DIM), 0.1),
        "da_lam_k2": nrm(ks[9], (NA, DA_HEAD_DIM), 0.1),
        "da_subln_w": 1.0 + nrm(ks[10], (NA, DA_VALUE_DIM), 0.02),
        "da_w_out": nrm(ks[11], (NA, DA_WIDTH, D_MODEL), DA_WIDTH ** -0.5),
        "rw_w_in": nrm(ks[12], (NR, D_MODEL, RW_IN_COLS), D_MODEL ** -0.5),
        "rw_mu": jax.random.uniform(ks[13], (NR, RW_IN_COLS), F32),
        "rw_w0": jax.random.uniform(ks[14], (NR, RW_WIDTH), F32, -6.0, 1.0),
        "rw_decay_up": nrm(ks[15], (NR, DECAY_RANK, RW_WIDTH), 0.5 * DECAY_RANK ** -0.5),
        "rw_a0": nrm(ks[16], (NR, RW_WIDTH), 0.5),
        "rw_iclr_up": nrm(ks[17], (NR, ICLR_RANK, RW_WIDTH), 0.5 * ICLR_RANK ** -0.5),
        "rw_k_k": 0.85 + nrm(ks[18], (NR, RW_WIDTH), 0.05),
        "rw_k_a": 1.0 + nrm(ks[19], (NR, RW_WIDTH), 0.05),
        "rw_r_k": nrm(ks[20], (NR, RW_HEADS, RW_HEAD_DIM), 0.1),
        "rw_gn_w": 1.0 + nrm(ks[21], (NR, RW_WIDTH), 0.02),
        "rw_gn_b": nrm(ks[22], (NR, RW_WIDTH), 0.01),
        "rw_w_out": nrm(ks[23], (NR, RW_WIDTH, D_MODEL), RW_WIDTH ** -0.5),
        "rw_vres_down": nrm(ks[24], (NV, D_MODEL, VRES_RANK), D_MODEL ** -0.5),
        "rw_vres_mu": jax.random.uniform(ks[25], (NV, VRES_RANK), F32),
        "rw_v0": 1.0 + nrm(ks[26], (NV, RW_WIDTH), 0.2),
        "rw_vres_up": nrm(ks[27], (NV, VRES_RANK, RW_WIDTH), VRES_RANK ** -0.5),
    }


def reference(x, positions, norm_w, da_w_in, da_q_gain, da_k_gain, da_lam_q1, da_lam_k1,
              da_lam_q2, da_lam_k2, da_subln_w, da_w_out, rw_w_in, rw_mu, rw_w0, rw_decay_up,
              rw_a0, rw_iclr_up, rw_k_k, rw_k_a, rw_r_k, rw_gn_w, rw_gn_b, rw_w_out,
              rw_vres_down, rw_vres_mu, rw_v0, rw_vres_up):
    cos, sin = rope_tables(positions)
    v_first = None
    for i in range(DEPTH):
        hn = rms_norm(x, norm_w[i])
        j = i // N_MIXERS
        if i % N_MIXERS == 0:
            lambda_init = 0.8 - 0.6 * math.exp(-0.3 * i)
            out = diff_attention_mixer(hn, cos, sin, da_w_in[j], da_q_gain[j], da_k_gain[j],
                                       da_lam_q1[j], da_lam_k1[j], da_lam_q2[j], da_lam_k2[j],
                                       da_subln_w[j], da_w_out[j], lambda_init)
        else:
            vres = None if j == 0 else (rw_vres_down[j - 1], rw_vres_mu[j - 1], rw_v0[j - 1], rw_vres_up[j - 1])
            out, v_first = rwkv7_mixer(hn, v_first, rw_w_in[j], rw_mu[j], rw_w0[j], rw_decay_up[j],
                                       rw_a0[j], rw_iclr_up[j], rw_k_k[j], rw_k_a[j], rw_r_k[j],
                                       rw_gn_w[j], rw_gn_b[j], rw_w_out[j], vres)
        x = x + out.astype(x.dtype)
    return x
```

```python
import math
import numpy as np
import concourse.bass as bass
import concourse.mybir as mybir
from concourse.bass_utils import run_bass_kernel_spmd

F32 = mybir.dt.float32
BF16 = mybir.dt.bfloat16
I32 = mybir.dt.int32
AF = mybir.ActivationFunctionType
ALU = mybir.AluOpType
AX = mybir.AxisListType

SEM_EPOCH = 30000
D = 2048
NORM_EPS = 1e-6
GN_EPS = 64e-5
NCORES = 8


class Buf:
    __slots__ = ("name", "writer", "readers")

    def __init__(self, name=""):
        self.name = name
        self.writer = None
        self.readers = {}


class KB:
    ENGS = ("pe", "act", "dve", "pool", "sp")

    def __init__(self, nc, n_dma_sems=40):
        self.nc = nc
        self.ops = {e: [] for e in self.ENGS}
        self.cnt = {e: 0 for e in self.ENGS}
        self.sems = {}
        self.seen = {e: {} for e in self.ENGS}
        self.n_dma_sems = n_dma_sems
        self.dma_next = 0
        self.dma_cum = [0] * n_dma_sems
        self._stack = []
        self.ninst = 0

    def _sem(self, key):
        if key not in self.sems:
            cm = self.nc.semaphore("s_%s_%s" % key)
            h = cm.__enter__()
            self._stack.append(cm)
            self.sems[key] = h
        return self.sems[key]

    def _wait(self, eng, dep):
        key, val = dep
        if self.seen[eng].get(key, 0) >= val:
            return
        self.seen[eng][key] = val
        h = self._sem(key)
        self.ops[eng].append(lambda e, h=h, val=val: e.wait_ge(h, val))

    def _deps(self, eng, reads, writes):
        deps = []
        for r in reads:
            if r.writer is not None:
                deps.append(r.writer)
        for w in writes:
            if w.writer is not None and w.writer[0][0] != eng:
                deps.append(w.writer)
            for k, v in w.readers.items():
                if k[0] != eng:
                    deps.append((k, v))
        return deps

    def op(self, eng, fn, reads=(), writes=()):
        for dep in self._deps(eng, reads, writes):
            if eng == "pe" and dep[0][0] == "pe":
                continue
            self._wait(eng, dep)
        self.cnt[eng] += 1
        n = self.cnt[eng]
        key = (eng, (n - 1) // SEM_EPOCH)
        val = (n - 1) % SEM_EPOCH + 1
        h = self._sem(key)
        self.ops[eng].append(lambda e, fn=fn, h=h: fn(e).then_inc(h, 1))
        tok = (key, val)
        for r in reads:
            r.readers[key] = val
        for w in writes:
            w.writer = tok
            w.readers = {}
        self.ninst += 1
        return tok

    def dma(self, q, out, in_, reads=(), writes=(), **kw):
        j = self.dma_next
        self.dma_next = (j + 1) % self.n_dma_sems
        key = ("dma", j)
        if self.dma_cum[j] > 0:
            self._wait(q, (key, self.dma_cum[j]))
        for dep in self._deps(q, reads, writes):
            self._wait(q, dep)
        self.dma_cum[j] += 16
        val = self.dma_cum[j]
        h = self._sem(key)
        self.ops[q].append(lambda e, h=h, out=out, in_=in_, kw=kw:
                           e.dma_start(out=out, in_=in_, **kw).then_inc(h, 16))
        tok = (key, val)
        for r in reads:
            r.readers[key] = val
        for w in writes:
            w.writer = tok
            w.readers = {}
        self.ninst += 1
        return tok

    def barrier(self):
        toks = []
        for f in ("pe", "act", "dve", "pool"):
            n = self.cnt[f]
            if n > 0:
                toks.append(((f, (n - 1) // SEM_EPOCH), (n - 1) % SEM_EPOCH + 1))
        for j in range(self.n_dma_sems):
            if self.dma_cum[j] > 0:
                toks.append((("dma", j), self.dma_cum[j]))
        for e in self.ENGS:
            for t in toks:
                if t[0][0] == e:
                    continue
                self._wait(e, t)

    def finish(self):
        for j in range(self.n_dma_sems):
            if self.dma_cum[j] > 0:
                self._wait("sp", (("dma", j), self.dma_cum[j]))

    def emit(self):
        nc = self.nc
        with nc.Block() as block:
            @block.tensor
            def _(e):
                for f in self.ops["pe"]:
                    f(e)

            @block.scalar
            def _(e):
                for f in self.ops["act"]:
                    f(e)

            @block.vector
            def _(e):
                for f in self.ops["dve"]:
                    f(e)

            @block.gpsimd
            def _(e):
                for f in self.ops["pool"]:
                    f(e)

            @block.sync
            def _(e):
                for f in self.ops["sp"]:
                    f(e)
        for cm in reversed(self._stack):
            cm.__exit__(None, None, None)


class TT:
    __slots__ = ("t", "b")

    def __init__(self, t, name=""):
        self.t = t
        self.b = Buf(name)

    def __getitem__(self, k):
        return self.t[k]


_DT_BYTES = {F32: 4, BF16: 2, I32: 4}


class Prog:
    def __init__(self, T):
        self.T = T
        self.nc = bass.Bass("TRN2", target_bir_lowering=False)
        self.kb = KB(self.nc)
        self.off = 16640
        self.uid = 0
        self.ps = [TT(self.nc.alloc_psum_tensor("psb%d" % i, [128, 512], F32), "ps%d" % i) for i in range(8)]
        self.rr = 0
        self.SB_LIMIT = 229376

    def sb(self, shape, dt=F32, name="t"):
        n = 1
        for s in shape[1:]:
            n *= s
        nbytes = n * _DT_BYTES[dt]
        nbytes = (nbytes + 63) // 64 * 64
        self.uid += 1
        assert self.off + nbytes <= self.SB_LIMIT, ("SBUF overflow", name, self.off, nbytes)
        t = self.nc.alloc_sbuf_tensor_at("%s_%d" % (name, self.uid), list(shape), dt, offset=self.off)
        self.off += nbytes
        return TT(t, name)

    def mark(self):
        return self.off

    def release(self, m):
        self.kb.barrier()
        self.off = m

    def din(self, name, shape, dt=F32):
        return self.nc.dram_tensor(name, list(shape), dt, kind="ExternalInput").ap()

    def dout(self, name, shape, dt=F32):
        return self.nc.dram_tensor(name, list(shape), dt, kind="ExternalOutput").ap()

    def dscratch(self, name, shape, dt=F32):
        return self.nc.dram_tensor(name, list(shape), dt, kind="Internal").ap()

    def mm(self, out, o_tt, lhsT, l_tt, rhs, r_tt, start=True, stop=True):
        self.kb.op("pe", lambda e: e.matmul(out, lhsT, rhs, start=start, stop=stop),
                   reads=[l_tt.b, r_tt.b], writes=[o_tt.b])

    def tr(self, out, o_tt, in_, i_tt, ident):
        self.kb.op("pe", lambda e: e.transpose(out, in_, ident[:]), reads=[i_tt.b, ident.b], writes=[o_tt.b])

    def ev(self, eng, fn, reads, writes):
        self.kb.op(eng, fn, reads=[r.b for r in reads], writes=[w.b for w in writes])

    def alt(self):
        self.rr ^= 1
        return "act" if self.rr else "dve"

    def copy(self, eng, out, o_tt, in_, i_tt):
        if eng == "act":
            self.ev("act", lambda e: e.activation(out, in_, AF.Copy), [i_tt], [o_tt])
        else:
            self.ev(eng, lambda e: e.tensor_copy(out, in_), [i_tt], [o_tt])

    def load(self, out, o_tt, in_, q="sp", dbuf=None):
        self.kb.dma(q, out, in_, reads=[dbuf] if dbuf is not None else [], writes=[o_tt.b])

    def store(self, out, in_, i_tt, q="pool", dbuf=None):
        self.kb.dma(q, out, in_, reads=[i_tt.b], writes=[dbuf] if dbuf is not None else [])


def stage_norm_T(P, x_d, x_buf, normw_row_d, hnT, ident):
    T = P.T
    m = P.mark()
    nw = P.sb([128, D], F32, "nw")
    P.load(nw[:], nw, normw_row_d.partition_broadcast(128))
    xb = [P.sb([128, D], F32, "xb") for _ in range(2)]
    xs = [P.sb([128, D], F32, "xs") for _ in range(2)]
    junk = P.sb([128, D], BF16, "junk")
    ss = [P.sb([128, 1], F32, "ss") for _ in range(2)]
    for tb in range(T // 128):
        xt = xb[tb % 2]
        st = ss[tb % 2]
        xst = xs[tb % 2]
        P.load(xt[:], xt, x_d[tb * 128:(tb + 1) * 128, :], dbuf=x_buf)
        P.ev("act", lambda e, xt=xt, st=st: e.activation(junk[:], xt[:], AF.Square, accum_out=st[:]), [xt], [junk, st])
        P.ev("act", lambda e, st=st: e.activation(st[:], st[:], AF.Sqrt, bias=NORM_EPS, scale=1.0 / D), [st], [st])
        P.ev("dve", lambda e, st=st: e.reciprocal(st[:], st[:]), [st], [st])
        P.ev("dve", lambda e, xt=xt, st=st, xst=xst: e.scalar_tensor_tensor(xst[:], xt[:], st[:, 0:1], nw[:], ALU.mult, ALU.mult),
             [xt, st, nw], [xst])
        for g in range(4):
            pb = P.ps[g % 2]
            for j in range(4):
                c = g * 4 + j
                P.tr(pb[:, j * 128:(j + 1) * 128], pb, xst[:, c * 128:(c + 1) * 128], xst, ident)
            eng = P.alt()
            P.copy(eng, hnT[:, g * 4:(g + 1) * 4, tb * 128:(tb + 1) * 128],
                   hnT, pb[:].rearrange("p (a b) -> p a b", a=4), pb)
    P.release(m)


def stage_outproj(P, ogT_d, ogT_buf, wout_d, xin_d, xin_buf, xout_d, xout_buf):
    T = P.T
    m = P.mark()
    wo = P.sb([128, 16, D], BF16, "wo")
    wv = wout_d.rearrange("(kc p) n -> p kc n", p=128)
    for i in range(4):
        P.load(wo[:, i * 4:(i + 1) * 4, :], wo, wv[:, i * 4:(i + 1) * 4, :], q="pool")
    ogb = [P.sb([128, 16, 128], BF16, "ogb") for _ in range(2)]
    xb = [P.sb([128, D], F32, "xb") for _ in range(2)]
    ob = [P.sb([128, D], F32, "ob") for _ in range(2)]
    ogv = ogT_d.rearrange("(kc p) t -> p kc t", p=128)
    for tb in range(T // 128):
        og = ogb[tb % 2]
        xt = xb[tb % 2]
        ot = ob[tb % 2]
        P.load(og[:], og, ogv[:, :, tb * 128:(tb + 1) * 128], dbuf=ogT_buf)
        P.load(xt[:], xt, xin_d[tb * 128:(tb + 1) * 128, :], dbuf=xin_buf)
        for cg in range(4):
            pb = P.ps[(tb * 4 + cg) % 4]
            for kc in range(16):
                P.mm(pb[:], pb, og[:, kc, :], og, wo[:, kc, cg * 512:(cg + 1) * 512], wo, start=(kc == 0), stop=(kc == 15))
            P.ev("dve", lambda e, ot=ot, pb=pb, xt=xt, cg=cg: e.tensor_tensor(ot[:, cg * 512:(cg + 1) * 512], pb[:],
                                                                              xt[:, cg * 512:(cg + 1) * 512], ALU.add), [pb, xt], [ot])
        P.store(xout_d[tb * 128:(tb + 1) * 128, :], ot[:], ot, q="sp", dbuf=xout_buf)
    P.release(m)


def attn_layer(P, li, j, x_d, x_buf, xo_d, xo_buf, A, C):
    T = P.T
    NT = T // 128
    lambda_init = 0.8 - 0.6 * math.exp(-0.3 * li)
    ident = C["ident"]
    QT, KT, V, G, OGT = C["QT"], C["KT"], C["V"], C["G"], C["OGT"]
    bQT, bKT, bV, bG, bOGT = C["bQT"], C["bKT"], C["bV"], C["bG"], C["bOGT"]

    m0 = P.mark()
    setup_rope(P, A, C)
    hnT = P.sb([128, 16, T], BF16, "hnT")
    stage_norm_T(P, x_d, x_buf, A["norm_w"][li:li + 1, :], hnT, ident)

    m1 = P.mark()
    cos, sin = C["cos"], C["sin"]
    gq = P.sb([128, 128], F32, "gq")
    gk = P.sb([128, 128], F32, "gk")
    P.load(gq[:], gq, A["da_q_gain"][j:j + 1, :].partition_broadcast(128))
    P.load(gk[:], gk, A["da_k_gain"][j:j + 1, :].partition_broadcast(128))
    wb = [P.sb([128, 16, 512], BF16, "wb") for _ in range(2)]
    sq = P.sb([128, 512], F32, "sq")
    s4 = [P.sb([128, 4], F32, "s4") for _ in range(2)]
    qn = [P.sb([128, 4, 128], F32, "qn") for _ in range(2)]
    qr = [P.sb([128, 4, 128], F32, "qr") for _ in range(2)]
    tmp = [P.sb([128, 4, 64], F32, "tmp") for _ in range(2)]
    qT = [P.sb([128, 4, 128], BF16, "qT") for _ in range(2)]
    vb = [P.sb([128, 512], BF16, "vb") for _ in range(2)]
    gb = [P.sb([128, 512], F32, "gb") for _ in range(2)]
    wv = A["da_w_in"][j].rearrange("(kc p) n -> p kc n", p=128)
    it = 0
    for cg in range(16):
        w = wb[cg % 2]
        for h2 in range(2):
            P.load(w[:, h2 * 8:(h2 + 1) * 8, :], w, wv[:, h2 * 8:(h2 + 1) * 8, cg * 512:(cg + 1) * 512], q="pool")
        for tb in range(NT):
            pb = P.ps[it % 2]
            for kc in range(16):
                P.mm(pb[:], pb, hnT[:, kc, tb * 128:(tb + 1) * 128], hnT, w[:, kc, :], w, start=(kc == 0), stop=(kc == 15))
            if cg < 8:
                gain = gq if cg < 4 else gk
                s4t, qnt, qrt, tmpt, qTt = s4[it % 2], qn[it % 2], qr[it % 2], tmp[it % 2], qT[it % 2]
                pv = pb[:].rearrange("p (a b) -> p a b", a=4)
                P.ev("act", lambda e, pb=pb: e.activation(sq[:], pb[:], AF.Square), [pb], [sq])
                P.ev("dve", lambda e, s4t=s4t: e.reduce_sum(s4t[:], sq[:].rearrange("p (a b) -> p a b", a=4), AX.X), [sq], [s4t])
                P.ev("act", lambda e, s4t=s4t: e.activation(s4t[:], s4t[:], AF.Sqrt, bias=NORM_EPS, scale=1.0 / 128), [s4t], [s4t])
                P.ev("dve", lambda e, s4t=s4t: e.reciprocal(s4t[:], s4t[:]), [s4t], [s4t])
                P.ev("dve", lambda e, qnt=qnt, pv=pv, s4t=s4t: e.tensor_tensor(qnt[:], pv, s4t[:].unsqueeze(2).broadcast_to([128, 4, 128]), ALU.mult),
                     [pb, s4t], [qnt])
                P.ev("pool", lambda e, qnt=qnt, gain=gain: e.tensor_tensor(qnt[:], qnt[:], gain[:].unsqueeze(1).broadcast_to([128, 4, 128]), ALU.mult),
                     [qnt, gain], [qnt])
                cb = cos[:, tb, :].unsqueeze(1).broadcast_to([128, 4, 64])
                sb_ = sin[:, tb, :].unsqueeze(1).broadcast_to([128, 4, 64])
                P.ev("dve", lambda e, qrt=qrt, qnt=qnt, cb=cb: e.tensor_tensor(qrt[:, :, 0:64], qnt[:, :, 0:64], cb, ALU.mult), [qnt, cos], [qrt])
                P.ev("pool", lambda e, tmpt=tmpt, qnt=qnt, sb_=sb_: e.tensor_tensor(tmpt[:], qnt[:, :, 64:128], sb_, ALU.mult), [qnt, sin], [tmpt])
                P.ev("dve", lambda e, qrt=qrt, tmpt=tmpt: e.tensor_tensor(qrt[:, :, 0:64], qrt[:, :, 0:64], tmpt[:], ALU.subtract), [qrt, tmpt], [qrt])
                P.ev("pool", lambda e, qrt=qrt, qnt=qnt, cb=cb: e.tensor_tensor(qrt[:, :, 64:128], qnt[:, :, 64:128], cb, ALU.mult), [qnt, cos], [qrt])
                P.ev("dve", lambda e, tmpt=tmpt, qnt=qnt, sb_=sb_: e.tensor_tensor(tmpt[:], qnt[:, :, 0:64], sb_, ALU.mult), [qnt, sin], [tmpt])
                P.ev("dve", lambda e, qrt=qrt, tmpt=tmpt: e.tensor_tensor(qrt[:, :, 64:128], qrt[:, :, 64:128], tmpt[:], ALU.add), [qrt, tmpt], [qrt])
                pt = P.ps[2 + it % 2]
                for a in range(4):
                    P.tr(pt[:, a * 128:(a + 1) * 128], pt, qrt[:, a, :], qrt, ident)
                P.ev("act", lambda e, qTt=qTt, pt=pt: e.activation(qTt[:], pt[:].rearrange("p (a b) -> p a b", a=4), AF.Copy), [pt], [qTt])
                dst = QT if cg < 4 else KT
                dbuf = bQT if cg < 4 else bKT
                hc0 = (cg % 4) * 4
                P.store(dst[hc0:hc0 + 4, :, tb * 128:(tb + 1) * 128].rearrange("a p t -> p a t"), qTt[:], qTt, dbuf=dbuf)
            elif cg < 12:
                vt = vb[it % 2]
                P.copy(P.alt(), vt[:], vt, pb[:], pb)
                c0 = (cg - 8) * 512
                P.store(V[tb * 128:(tb + 1) * 128, c0:c0 + 512], vt[:], vt, dbuf=bV)
            else:
                gt = gb[it % 2]
                P.ev("act", lambda e, gt=gt, pb=pb: e.activation(gt[:], pb[:], AF.Silu), [pb], [gt])
                c0 = (cg - 12) * 512
                P.store(G[tb * 128:(tb + 1) * 128, c0:c0 + 512], gt[:], gt, dbuf=bG)
            it += 1
    P.release(m0)

    m2 = P.mark()
    lv = P.sb([1, 4, 128], F32, "lv")
    for i, nm in enumerate(["da_lam_q1", "da_lam_k1", "da_lam_q2", "da_lam_k2"]):
        P.load(lv[:, i, :], lv, A[nm][j:j + 1, :])
    lp = P.sb([1, 2, 128], F32, "lp")
    l2 = P.sb([1, 2], F32, "l2")
    P.ev("dve", lambda e: e.tensor_tensor(lp[:, 0, :], lv[:, 0, :], lv[:, 1, :], ALU.mult), [lv], [lp])
    P.ev("dve", lambda e: e.tensor_tensor(lp[:, 1, :], lv[:, 2, :], lv[:, 3, :], ALU.mult), [lv], [lp])
    P.ev("dve", lambda e: e.reduce_sum(l2[:], lp[:], AX.X), [lp], [l2])
    P.ev("act", lambda e: e.activation(l2[:], l2[:], AF.Exp), [l2], [l2])
    l1 = P.sb([1, 1], F32, "l1")
    P.ev("dve", lambda e: e.scalar_tensor_tensor(l1[:], l2[:, 1:2], -lambda_init, l2[:, 0:1], ALU.add, ALU.subtract), [l2], [l1])
    ones1 = C["ones1"]
    pb = P.ps[7]
    P.mm(pb[:, 0:1], pb, ones1[0:1, :], ones1, l1[:], l1)
    nlam = P.sb([128, 1], F32, "nlam")
    P.copy("dve", nlam[:], nlam, pb[:, 0:1], pb)
    subw = P.sb([128, 256], F32, "subw")
    P.load(subw[:], subw, A["da_subln_w"][j:j + 1, :].partition_broadcast(128))
    P.ev("dve", lambda e: e.tensor_scalar(subw[:], subw[:], 1.0 - lambda_init, None, ALU.mult), [subw], [subw])
    mask = C["mask"]

    qkb = [[P.sb([128, T], BF16, "qk") for _ in range(4)] for _ in range(2)]
    vh = [P.sb([128, NT, 257], BF16, "vh") for _ in range(2)]
    for v_ in vh:
        P.ev("pool", lambda e, v_=v_: e.memset(v_[:, :, 256:257], 1.0), [], [v_])
    ptb = [P.sb([128, 512], BF16, "pt") for _ in range(3)]
    gt2 = [P.sb([128, 256], F32, "gt2") for _ in range(2)]
    o1 = [P.sb([128, 256], F32, "o1") for _ in range(2)]
    dd = [P.sb([128, 256], F32, "dd") for _ in range(2)]
    rs = [P.sb([128, 4], F32, "rs") for _ in range(2)]
    jk = P.sb([128, 256], BF16, "jk")
    ogT = [P.sb([128, 2, 128], BF16, "ogT") for _ in range(2)]
    scale = 128 ** -0.5
    nsb = T // 256
    ipair = 0
    ifin = 0
    for h in range(8):
        qk = qkb[h % 2]
        vt = vh[h % 2]
        P.load(qk[0][:], qk[0], QT[2 * h, :, :], dbuf=bQT)
        P.load(qk[1][:], qk[1], QT[2 * h + 1, :, :], dbuf=bQT)
        P.load(qk[2][:], qk[2], KT[2 * h, :, :], dbuf=bKT)
        P.load(qk[3][:], qk[3], KT[2 * h + 1, :, :], dbuf=bKT)
        P.load(vt[:, :, 0:256], vt, V[:, h * 256:(h + 1) * 256].rearrange("(n p) c -> p n c", p=128), dbuf=bV)
        for qs in range(nsb):
            acc = [[P.ps[2 + c * 2 + s] for s in range(2)] for c in range(2)]
            nkb = 2 * qs + 2
            for kb_ in range(nkb):
                last = (kb_ == nkb - 1)
                psb = P.ps[ipair % 2]
                pt = ptb[ipair % 3]
                q0 = 128 if last else 0
                for c in range(2):
                    P.mm(psb[:, c * 256 + q0:(c + 1) * 256], psb, qk[2 + c][:, kb_ * 128:(kb_ + 1) * 128], qk[2 + c],
                         qk[c][:, qs * 256 + q0:(qs + 1) * 256], qk[c])
                if last:
                    pin = psb[:].rearrange("p (c q) -> p c q", c=2)[:, :, 128:256]
                    pout = pt[:].rearrange("p (c q) -> p c q", c=2)[:, :, 128:256]
                else:
                    pin = psb[:]
                    pout = pt[:]
                P.ev("act", lambda e, pin=pin, pout=pout: e.activation(pout, pin, AF.Exp, scale=scale), [psb], [pt])
                if kb_ >= nkb - 2:
                    sub = kb_ - (nkb - 2)
                    pm = pt[:].rearrange("p (c q) -> p c q", c=2)[:, :, sub * 128:(sub + 1) * 128]
                    P.ev("pool", lambda e, pm=pm: e.tensor_tensor(pm, pm, mask[:].unsqueeze(1).broadcast_to([128, 2, 128]), ALU.mult),
                         [pt, mask], [pt])
                for s in range(2):
                    if kb_ > 2 * qs + s:
                        continue
                    for c in range(2):
                        a_ = acc[c][s]
                        P.mm(a_[:, 0:257], a_, pt[:, c * 256 + s * 128:c * 256 + (s + 1) * 128], pt, vt[:, kb_, :], vt,
                             start=(kb_ == 0), stop=(kb_ == 2 * qs + s))
                ipair += 1
            for s in range(2):
                tb = qs * 2 + s
                a1, a2 = acc[0][s], acc[1][s]
                g_, o_, d_, r_, og_ = gt2[ifin % 2], o1[ifin % 2], dd[ifin % 2], rs[ifin % 2], ogT[ifin % 2]
                P.load(g_[:], g_, G[tb * 128:(tb + 1) * 128, h * 256:(h + 1) * 256], dbuf=bG)
                P.ev("dve", lambda e, r_=r_, a1=a1: e.reciprocal(r_[:, 0:1], a1[:, 256:257]), [a1], [r_])
                P.ev("dve", lambda e, r_=r_, a2=a2: e.reciprocal(r_[:, 1:2], a2[:, 256:257]), [a2], [r_])
                P.ev("dve", lambda e, r_=r_: e.tensor_tensor(r_[:, 1:2], r_[:, 1:2], nlam[:], ALU.mult), [r_, nlam], [r_])
                P.ev("act", lambda e, o_=o_, a1=a1, r_=r_: e.activation(o_[:], a1[:, 0:256], AF.Copy, scale=r_[:, 0:1]), [a1, r_], [o_])
                P.ev("dve", lambda e, d_=d_, a2=a2, r_=r_, o_=o_: e.scalar_tensor_tensor(d_[:], a2[:, 0:256], r_[:, 1:2], o_[:], ALU.mult, ALU.add),
                     [a2, r_, o_], [d_])
                P.ev("act", lambda e, d_=d_, r_=r_: e.activation(jk[:], d_[:], AF.Square, accum_out=r_[:, 2:3]), [d_], [jk, r_])
                P.ev("act", lambda e, r_=r_: e.activation(r_[:, 2:3], r_[:, 2:3], AF.Sqrt, bias=NORM_EPS, scale=1.0 / 256), [r_], [r_])
                P.ev("dve", lambda e, r_=r_: e.reciprocal(r_[:, 2:3], r_[:, 2:3]), [r_], [r_])
                P.ev("dve", lambda e, d_=d_, r_=r_: e.scalar_tensor_tensor(d_[:], d_[:], r_[:, 2:3], subw[:], ALU.mult, ALU.mult), [d_, r_, subw], [d_])
                P.ev("pool", lambda e, d_=d_, g_=g_: e.tensor_tensor(d_[:], d_[:], g_[:], ALU.mult), [d_, g_], [d_])
                pt_ = P.ps[6 + ifin % 2]
                for a in range(2):
                    P.tr(pt_[:, a * 128:(a + 1) * 128], pt_, d_[:, a * 128:(a + 1) * 128], d_, ident)
                P.ev("act", lambda e, og_=og_, pt_=pt_: e.activation(og_[:], pt_[:, 0:256].rearrange("p (a b) -> p a b", a=2), AF.Copy), [pt_], [og_])
                P.store(OGT[h * 256:(h + 1) * 256, tb * 128:(tb + 1) * 128].rearrange("(a p) t -> p a t", p=128), og_[:], og_, dbuf=bOGT)
                ifin += 1
    P.release(m2)

    stage_outproj(P, OGT, bOGT, A["da_w_out"][j], x_d, x_buf, xo_d, xo_buf)


def setup_consts(P, A, C):
    T = P.T
    NT = T // 128
    ident = P.sb([128, 128], F32, "ident")
    P.load(ident[:], ident, A["c_ident"])
    C["ident"] = ident
    ones1 = P.sb([1, 128], F32, "ones1")
    P.ev("dve", lambda e: e.memset(ones1[:], 1.0), [], [ones1])
    C["ones1"] = ones1
    mask = P.sb([128, 128], BF16, "mask")
    P.load(mask[:], mask, A["c_mask"], q="pool")
    C["mask"] = mask
    for nm, shp in (("c_m4", [128, 4, 128]), ("c_ml", [128, 128]), ("c_sel", [128, 64]), ("c_bones", [128, 128]), ("c_bdm", [128, 2])):
        t = P.sb(shp, F32, nm)
        P.load(t[:], t, A[nm])
        C[nm] = t
    onesc = P.sb([128, 1], F32, "onesc")
    P.ev("dve", lambda e: e.memset(onesc[:], 1.0), [], [onesc])
    C["onesc"] = onesc


def setup_rope(P, A, C):
    T = P.T
    NT = T // 128
    cos = P.sb([128, NT, 64], F32, "cos")
    sin = P.sb([128, NT, 64], F32, "sin")
    m = P.mark()
    posi = P.sb([128, NT], I32, "posi")
    posf = P.sb([128, NT], F32, "posf")
    inv = P.sb([128, 64], F32, "inv")
    ang = P.sb([128, NT, 64], F32, "ang")
    n_ = P.sb([128, NT, 64], F32, "n_")
    P.load(posi[:], posi, A["pos"])
    P.load(inv[:], inv, A["c_inv"])
    P.ev("dve", lambda e: e.tensor_copy(posf[:], posi[:]), [posi], [posf])
    P.ev("dve", lambda e: e.tensor_tensor(ang[:], posf[:].unsqueeze(2).broadcast_to([128, NT, 64]),
                                          inv[:].unsqueeze(1).broadcast_to([128, NT, 64]), ALU.mult), [posf, inv], [ang])
    TWO_PI = 2.0 * math.pi
    MAGIC = 12582912.0
    for tab, shift in ((sin, 0.0), (cos, math.pi / 2)):
        P.ev("dve", lambda e, shift=shift: e.tensor_scalar(n_[:], ang[:], shift, 1.0 / TWO_PI, ALU.add, ALU.mult), [ang], [n_])
        P.ev("dve", lambda e: e.tensor_scalar(n_[:], n_[:], MAGIC, -MAGIC, ALU.add, ALU.add), [n_], [n_])
        P.ev("dve", lambda e, tab=tab: e.scalar_tensor_tensor(tab[:], n_[:], -TWO_PI, ang[:], ALU.mult, ALU.add), [n_, ang], [tab])
        P.ev("dve", lambda e, tab=tab, shift=shift: e.tensor_scalar(tab[:], tab[:], shift, 3.14159, ALU.add, ALU.min), [tab], [tab])
        P.ev("dve", lambda e, tab=tab: e.tensor_scalar(tab[:], tab[:], -3.14159, None, ALU.max), [tab], [tab])
        P.ev("act", lambda e, tab=tab: e.activation(tab[:], tab[:], AF.Sin), [tab], [tab])
    P.release(m)
    C["cos"], C["sin"] = cos, sin


RWKV_W = ["rw_w_in", "rw_decay_up", "rw_iclr_up", "rw_gn_w", "rw_gn_b", "rw_w_out", "rw_vres_down", "rw_vres_up"]
ATTN_W = ["norm_w", "da_w_in", "da_q_gain", "da_k_gain", "da_lam_q1", "da_lam_k1", "da_lam_q2", "da_lam_k2",
          "da_subln_w", "da_w_out"]


def build_program(T, layers, shapes):
    P = Prog(T)
    A = {}
    for nm, (shp, dt) in shapes.items():
        A[nm] = P.din(nm, shp, dt)
    out_d = P.dout("out", [T, D], F32)
    C = {}
    C["QT"] = P.dscratch("QT", [16, 128, T], BF16)
    C["KT"] = P.dscratch("KT", [16, 128, T], BF16)
    C["V"] = P.dscratch("V", [T, D], BF16)
    C["G"] = P.dscratch("G", [T, D], F32)
    C["OGT"] = P.dscratch("OGT", [D, T], BF16)
    for nm in ["QT", "KT", "V", "G", "OGT"]:
        C["b" + nm] = Buf(nm)
    C["PT"] = [P.dscratch("PT%d" % i, [67 * 128, T], F32) for i in range(2)]
    C["bPT"] = [Buf("PT0"), Buf("PT1")]
    xs = [P.dscratch("XA", [T, D], F32), P.dscratch("XB", [T, D], F32)]
    xbufs = [Buf("XA"), Buf("XB")]
    setup_consts(P, A, C)
    cur, cur_b = A["x"], Buf("x")
    for n, li in enumerate(layers):
        if n == len(layers) - 1:
            nxt, nxt_b = out_d, Buf("out")
        else:
            nxt, nxt_b = xs[n % 2], xbufs[n % 2]
        if li % 2 == 0:
            attn_layer(P, li, li // 2, cur, cur_b, nxt, nxt_b, A, C)
        else:
            rwkv_layer(P, li, li // 2, cur, cur_b, nxt, nxt_b, A, C)
        cur, cur_b = nxt, nxt_b
    P.kb.finish()
    P.kb.emit()
    return P


def rwkv_layer(P, li, j, x_d, x_buf, xo_d, xo_buf, A, C):
    T = P.T
    ident = C["ident"]
    PT, bPT = C["PT"][j], C["bPT"][j]
    has_vres = j > 0
    OGT, bOGT = C["OGT"], C["bOGT"]
    m0 = P.mark()
    hnT = P.sb([128, 16, T], BF16, "hnT")
    stage_norm_T(P, x_d, x_buf, A["norm_w"][li:li + 1, :], hnT, ident)

    TH = min(2048, T)
    NHALF = T // TH
    mucol = P.sb([128, 66], F32, "mucol")
    P.load(mucol[:], mucol, A["rw_mucol"][j])
    if has_vres:
        vmucol = P.sb([128, 1], F32, "vmucol")
        P.load(vmucol[:], vmucol, A["rw_vmucol"][j - 1])
    wb = [P.sb([128, 16, 512], BF16, "wb") for _ in range(2)]
    pT = [P.sb([128, TH + 1], F32, "pT") for _ in range(2)]
    dT = P.sb([128, TH], F32, "dT")
    res = [P.sb([128, TH], F32, "res") for _ in range(2)]
    wv = A["rw_w_in"][j].rearrange("(kc p) n -> p kc n", p=128)
    groups = []
    for g in range(16):
        groups.append((wv, g * 512, 512, [(g * 4 + i, 128, i * 128, mucol[:, g * 4 + i:g * 4 + i + 1]) for i in range(4)]))
    groups.append((wv, 8192, 192, [(64, 96, 0, mucol[0:96, 64:65]), (65, 96, 96, mucol[0:96, 65:66])]))
    if has_vres:
        groups.append((A["rw_vres_down"][j - 1].rearrange("(kc p) n -> p kc n", p=128), 0, 64, [(66, 64, 0, vmucol[0:64, 0:1])]))
    ib = it = ir = 0
    for gi, (src, c0, ncg, blks) in enumerate(groups):
        w = wb[gi % 2]
        for h2 in range(2):
            P.load(w[:, h2 * 8:(h2 + 1) * 8, 0:ncg], w, src[:, h2 * 8:(h2 + 1) * 8, c0:c0 + ncg], q="pool")
        for (cb, ncols, wc, mu_ap) in blks:
            p_ = pT[ib % 2]
            ib += 1
            for half in range(NHALF):
                if half == 0:
                    P.ev("pool", lambda e, p_=p_: e.memset(p_[:, 0:1], 0.0), [], [p_])
                else:
                    P.ev("pool", lambda e, p_=p_: e.tensor_copy(p_[:, 0:1], p_[:, TH:TH + 1]), [p_], [p_])
                for tg in range(TH // 512):
                    t0 = half * TH + tg * 512
                    pb = P.ps[it % 2]
                    it += 1
                    for kc in range(16):
                        P.mm(pb[0:ncols, :], pb, w[:, kc, wc:wc + ncols], w, hnT[:, kc, t0:t0 + 512], hnT, start=(kc == 0), stop=(kc == 15))
                    P.ev("act", lambda e, p_=p_, pb=pb, tg=tg, ncols=ncols: e.activation(p_[0:ncols, 1 + tg * 512:1 + (tg + 1) * 512], pb[0:ncols, :], AF.Copy), [pb], [p_])
                r_ = res[ir % 2]
                ir += 1
                P.ev("dve", lambda e, p_=p_, ncols=ncols: e.tensor_tensor(dT[0:ncols, :], p_[0:ncols, 0:TH], p_[0:ncols, 1:TH + 1], ALU.subtract), [p_], [dT])
                P.ev("dve", lambda e, p_=p_, r_=r_, ncols=ncols, mu_ap=mu_ap: e.scalar_tensor_tensor(r_[0:ncols, :], dT[0:ncols, :], mu_ap, p_[0:ncols, 1:TH + 1], ALU.mult, ALU.add),
                     [dT, p_, mucol], [r_])
                P.store(PT[cb * 128:cb * 128 + ncols, half * TH:(half + 1) * TH], r_[0:ncols, :], r_, dbuf=bPT)
    P.release(m0)

    m3 = P.mark()
    TG = 256
    NC_ = 4
    NTG = T // TG
    dup = P.sb([96, D], F32, "dup")
    iup = P.sb([96, D], F32, "iup")
    P.load(dup[:], dup, A["rw_decay_up"][j])
    P.load(iup[:], iup, A["rw_iclr_up"][j])
    vecs = P.sb([128, 5, 16], F32, "vecs")
    P.load(vecs[:], vecs, A["rw_vecs"][j])
    omka = P.sb([128, 16], F32, "omka")
    P.ev("dve", lambda e: e.tensor_scalar(omka[:], vecs[:, 3, :], -1.0, 1.0, ALU.mult, ALU.add), [vecs], [omka])
    gnw = P.sb([128, 16, 64], F32, "gnw")
    gnb = P.sb([128, 16, 64], F32, "gnb")
    for h in range(2):
        for (dst, nm) in ((gnw, "rw_gn_w"), (gnb, "rw_gn_b")):
            srcv = A[nm][j:j + 1, :].rearrange("o (hp h v) -> o hp h v", h=2, v=64)[:, :, h, :]
            P.load(dst[h * 64:(h + 1) * 64, :, :], dst, srcv.partition_broadcast(64))
    if has_vres:
        vup = P.sb([64, D], F32, "vup")
        P.load(vup[:], vup, A["rw_vres_up"][j - 1])
        v0c = P.sb([128, 16], F32, "v0c")
        P.load(v0c[:], v0c, A["rw_v0col"][j - 1])
    H = [P.sb([128, 64], F32, "H") for _ in range(16)]
    for h_ in H:
        P.ev("pool", lambda e, h_=h_: e.memset(h_[:], 0.0), [], [h_])
    m4, ml, sel, bones, bdm, onesc = C["c_m4"], C["c_ml"], C["c_sel"], C["c_bones"], C["c_bdm"], C["onesc"]
    mu4 = P.sb([128, NC_, 128], F32, "mu4")
    mui4 = P.sb([128, NC_, 128], F32, "mui4")
    ml4 = P.sb([128, NC_, 128], F32, "ml4")
    for n in range(NC_):
        P.ev("pool", lambda e, n=n: e.tensor_copy(mu4[:, n, :], m4[:, 0, :]), [m4], [mu4])
        P.ev("pool", lambda e, n=n: e.tensor_copy(mui4[:, n, :], m4[:, 1, :]), [m4], [mui4])
        P.ev("pool", lambda e, n=n: e.tensor_copy(ml4[:, n, :], ml[:]), [ml], [ml4])

    def pool2(shape, dt=F32, name="t", n=2):
        return [P.sb(shape, dt, name) for _ in range(n)]

    W = TG
    ld = {k: pool2([128, W], F32, k) for k in (["rT", "kT", "vT", "gT"] + (["vf"] if has_vres else []))}
    tdw = pool2([96, W], F32, "tdw")
    tda = pool2([96, W], F32, "tda")
    tpv = pool2([64, W], F32, "tpv") if has_vres else None
    tmpn = ["lw", "a", "kk", "sqk", "nrm", "kkn", "tm", "k2", "bb", "dcs", "eneg", "eprev", "at", "rt", "bt", "kt", "rk"]
    tm = {k: P.sb([128, W], F32, k) for k in tmpn}
    lwtm = P.sb([128, W // 128, 128], F32, "lwtm")
    epos = pool2([128, W], F32, "epos")
    sg = pool2([128, W], F32, "sg")
    bdAR = pool2([128, NC_, 2, 2, 64], F32, "bdAR")
    bdB = P.sb([128, NC_, 2, 64], F32, "bdB")
    bdK = P.sb([128, NC_, 2, 64], F32, "bdK")
    bdV = P.sb([128, NC_, 2, 64], F32, "bdV")
    bdRK = P.sb([128, NC_, 2, 64], F32, "bdRK")
    Nst = [P.sb([128, NC_, 128], F32, "Nst") for _ in range(2)]
    Ast = [P.sb([128, NC_, 128], F32, "Ast") for _ in range(2)]
    Nk = P.sb([128, NC_, 128], F32, "Nk")
    Nrb = pool2([128, NC_, 128], F32, "Nrb")
    Nrk = pool2([128, NC_, 128], F32, "Nrk")
    Vtm = pool2([128, NC_, 64], F32, "Vtm")
    sbon = pool2([128, NC_], F32, "sbon")
    Z = [P.sb([128, NC_, 128], F32, "Z") for _ in range(3)]
    bdW = P.sb([128, NC_, 2, 64], F32, "bdW")
    bdWT = pool2([128, NC_, 128], F32, "bdWT")
    bdBtm = pool2([128, NC_, 128], F32, "bdBtm")
    bdKtm = pool2([128, NC_, 128], F32, "bdKtm")
    U = pool2([128, 64], F32, "U")
    Ht = P.sb([128, 64], F32, "Ht")
    Ysb = P.sb([128, NC_, 64], F32, "Ysb")
    Ysq = P.sb([128, NC_, 64], F32, "Ysq")
    st = P.sb([128, 4, NC_], F32, "st")
    bdY = P.sb([128, NC_, 2, 64], F32, "bdY")
    ycm = P.sb([128, W], F32, "ycm")
    ogT = pool2([128, W], BF16, "ogT")
    PTv = C["PT"][0]
    bPTv = C["bPT"][0]
    EXPM05 = math.exp(-0.5)

    def v3(t):
        return t[:].rearrange("p (n t) -> p n t", n=NC_)

    def expand(eng, out4, src_tt, reads):
        in0 = v3(src_tt).unsqueeze(2).broadcast_to([128, NC_, 2, 64])
        in1 = bdm[:].unsqueeze(1).unsqueeze(3).broadcast_to([128, NC_, 2, 64])
        return lambda e: e.tensor_tensor(out4, in0, in1, ALU.mult)

    u = 0
    for tg in range(NTG):
        c0 = tg * TG
        dwt, dat = tdw[tg % 2], tda[tg % 2]
        P.load(dwt[:], dwt, PT[64 * 128:64 * 128 + 96, c0:c0 + W], dbuf=bPT)
        P.load(dat[:], dat, PT[65 * 128:65 * 128 + 96, c0:c0 + W], dbuf=bPT)
        P.ev("act", lambda e, dwt=dwt: e.activation(dwt[:], dwt[:], AF.Tanh), [dwt], [dwt])
        if has_vres:
            pvt = tpv[tg % 2]
            P.load(pvt[:], pvt, PT[66 * 128:66 * 128 + 64, c0:c0 + W], dbuf=bPT)
        for hp in range(16):
            k2i = u % 2
            u += 1
            rT, kT, vT, gT = ld["rT"][k2i], ld["kT"][k2i], ld["vT"][k2i], ld["gT"][k2i]
            for sec, t_ in enumerate((rT, kT, vT, gT)):
                r0 = (sec * 16 + hp) * 128
                P.load(t_[:], t_, PT[r0:r0 + 128, c0:c0 + W], dbuf=bPT)
            hs = slice(hp * 128, (hp + 1) * 128)
            col = lambda i: vecs[:, i, hp:hp + 1]
            lw, a_, kk, sqk, nrm, kkn, tmq, k2, bb, dcs, eneg, eprev, at, rt, bt, kt, rk = [tm[k] for k in tmpn]
            ep, sg_ = epos[k2i], sg[k2i]
            AR, nrb, nrk, vtm, sbn = bdAR[k2i], Nrb[k2i], Nrk[k2i], Vtm[k2i], sbon[k2i]
            wT, btm, ktm = bdWT[k2i], bdBtm[k2i], bdKtm[k2i]
            og_ = ogT[k2i]
            p0, p1, p2 = P.ps[0], P.ps[1], P.ps[2]
            P.mm(p0[:, 0:W], p0, dup[0:96, hs], dup, dwt[0:96, :], dwt)
            P.ev("act", lambda e, lw=lw, p0=p0, b=col(0): e.activation(lw[:], p0[:, 0:W], AF.Sigmoid, bias=b), [p0, vecs], [lw])
            P.ev("dve", lambda e, lw=lw: e.tensor_scalar(lw[:], lw[:], -EXPM05, None, ALU.mult), [lw], [lw])
            P.mm(p1[:, 0:W], p1, iup[0:96, hs], iup, dat[0:96, :], dat)
            P.ev("act", lambda e, a_=a_, p1=p1, b=col(1): e.activation(a_[:], p1[:, 0:W], AF.Sigmoid, bias=b), [p1, vecs], [a_])
            P.ev("dve", lambda e, kk=kk, kT=kT, s=col(2): e.tensor_scalar(kk[:], kT[:], s, None, ALU.mult), [kT, vecs], [kk])
            P.ev("act", lambda e, sqk=sqk, kk=kk: e.activation(sqk[:], kk[:], AF.Square), [kk], [sqk])
            P.mm(p2[:, 0:W], p2, bones[:], bones, sqk[:], sqk)
            P.ev("act", lambda e, nrm=nrm, p2=p2: e.activation(nrm[:], p2[:, 0:W], AF.Sqrt), [p2], [nrm])
            P.ev("dve", lambda e, nrm=nrm: e.tensor_scalar(nrm[:], nrm[:], 1e-12, None, ALU.max), [nrm], [nrm])
            P.ev("dve", lambda e, nrm=nrm: e.reciprocal(nrm[:], nrm[:]), [nrm], [nrm])
            P.ev("pool", lambda e, kkn=kkn, kk=kk, nrm=nrm: e.tensor_tensor(kkn[:], kk[:], nrm[:], ALU.mult), [kk, nrm], [kkn])
            P.ev("dve", lambda e, tmq=tmq, a_=a_, s1=col(3), s2=omka[:, hp:hp + 1]: e.tensor_scalar(tmq[:], a_[:], s1, s2, ALU.mult, ALU.add), [a_, vecs, omka], [tmq])
            P.ev("pool", lambda e, k2=k2, kT=kT, tmq=tmq: e.tensor_tensor(k2[:], kT[:], tmq[:], ALU.mult), [kT, tmq], [k2])
            P.ev("dve", lambda e, bb=bb, kkn=kkn, a_=a_: e.tensor_tensor(bb[:], kkn[:], a_[:], ALU.mult), [kkn, a_], [bb])
            if has_vres:
                vf = ld["vf"][k2i]
                r0 = (32 + hp) * 128
                P.load(vf[:], vf, PTv[r0:r0 + 128, c0:c0 + W], dbuf=bPTv)
                P.mm(p0[:, 256:256 + W], p0, vup[0:64, hs], vup, pvt[0:64, :], pvt)
                P.ev("act", lambda e, tmq=tmq, p0=p0, b=v0c[:, hp:hp + 1]: e.activation(tmq[:], p0[:, 256:256 + W], AF.Sigmoid, bias=b), [p0, v0c], [tmq])
                P.ev("pool", lambda e, vf=vf, vT=vT: e.tensor_tensor(vf[:], vf[:], vT[:], ALU.subtract), [vf, vT], [vf])
                P.ev("dve", lambda e, vf=vf, tmq=tmq: e.tensor_tensor(vf[:], vf[:], tmq[:], ALU.mult), [vf, tmq], [vf])
                P.ev("pool", lambda e, vf=vf, vT=vT: e.tensor_tensor(vT[:], vT[:], vf[:], ALU.add), [vf, vT], [vT])
            for i in range(W // 128):
                P.tr(p1[:, 256 + i * 128:256 + (i + 1) * 128], p1, lw[:, i * 128:(i + 1) * 128], lw, ident)
            P.ev("act", lambda e, p1=p1: e.activation(lwtm[:], p1[:, 256:256 + W].rearrange("p (a b) -> p a b", b=128), AF.Copy), [p1], [lwtm])
            for i in range(W // 128):
                P.mm(p2[:, 256 + i * 128:256 + (i + 1) * 128], p2, lwtm[:, i, :], lwtm, m4[:, 1, :], m4)
            cs = p2[:, 256:256 + W]
            P.ev("act", lambda e, ep=ep, cs=cs: e.activation(ep[:], cs, AF.Exp), [p2], [ep])
            P.ev("act", lambda e, eneg=eneg, cs=cs: e.activation(eneg[:], cs, AF.Exp, scale=-1.0), [p2], [eneg])
            P.ev("dve", lambda e, dcs=dcs, cs=cs, lw=lw: e.tensor_tensor(dcs[:], cs, lw[:], ALU.subtract), [p2, lw], [dcs])
            P.ev("act", lambda e, eprev=eprev, dcs=dcs: e.activation(eprev[:], dcs[:], AF.Exp), [dcs], [eprev])
            P.ev("act", lambda e, sg_=sg_, gT=gT: e.activation(sg_[:], gT[:], AF.Silu), [gT], [sg_])
            P.ev("dve", lambda e, at=at, kkn=kkn, eprev=eprev: e.scalar_tensor_tensor(at[:], kkn[:], -1.0, eprev[:], ALU.mult, ALU.mult), [kkn, eprev], [at])
            P.ev("pool", lambda e, rt=rt, rT=rT, ep=ep: e.tensor_tensor(rt[:], rT[:], ep[:], ALU.mult), [rT, ep], [rt])
            P.ev("dve", lambda e, bt=bt, bb=bb, eneg=eneg: e.tensor_tensor(bt[:], bb[:], eneg[:], ALU.mult), [bb, eneg], [bt])
            P.ev("pool", lambda e, kt=kt, k2=k2, eneg=eneg: e.tensor_tensor(kt[:], k2[:], eneg[:], ALU.mult), [k2, eneg], [kt])
            P.ev("dve", lambda e, rk=rk, rT=rT, k2=k2, s=col(4): e.scalar_tensor_tensor(rk[:], rT[:], s, k2[:], ALU.mult, ALU.mult), [rT, k2, vecs], [rk])
            P.ev("dve", expand("dve", AR[:, :, 0, :, :], at, None), [at, bdm], [AR])
            P.ev("pool", expand("pool", AR[:, :, 1, :, :], rt, None), [rt, bdm], [AR])
            P.ev("dve", expand("dve", bdB[:], bt, None), [bt, bdm], [bdB])
            P.ev("pool", expand("pool", bdK[:], kt, None), [kt, bdm], [bdK])
            P.ev("dve", expand("dve", bdV[:], vT, None), [vT, bdm], [bdV])
            P.ev("pool", expand("pool", bdRK[:], rk, None), [rk, bdm], [bdRK])
            f2 = lambda t, n: t[:, n, :, :].rearrange("p a b -> p (a b)")
            bA = lambda n: AR[:, n, 0, :, :].rearrange("p a b -> p (a b)")
            bR = lambda n: AR[:, n, 1, :, :].rearrange("p a b -> p (a b)")
            q3, q4, q5, q6, q7 = P.ps[3], P.ps[4], P.ps[5], P.ps[6], P.ps[7]
            def ncols(n):
                return slice(n * 128, (n + 1) * 128)
            N0, A0 = Nst[0], Ast[0]
            for n in range(NC_):
                P.mm(q3[:, ncols(n)], q3, f2(bdB, n), bdB, bA(n), AR)
            P.ev("dve", lambda e, N0=N0, q3=q3: e.tensor_tensor(N0[:], q3[:].rearrange("p (n t) -> p n t", n=NC_), mu4[:], ALU.mult), [q3, mu4], [N0])
            for n in range(NC_):
                P.mm(q4[:, ncols(n)], q4, bA(n), AR, f2(bdB, n), bdB)
            P.ev("dve", lambda e, A0=A0, q4=q4: e.tensor_tensor(A0[:], q4[:].rearrange("p (n t) -> p n t", n=NC_), ml4[:], ALU.mult), [q4, ml4], [A0])
            for n in range(NC_):
                P.mm(q5[:, ncols(n)], q5, f2(bdK, n), bdK, bA(n), AR)
            P.ev("dve", lambda e, q5=q5: e.tensor_tensor(Nk[:], q5[:].rearrange("p (n t) -> p n t", n=NC_), mu4[:], ALU.mult), [q5, mu4], [Nk])
            for n in range(NC_):
                P.mm(q6[:, ncols(n)], q6, f2(bdB, n), bdB, bR(n), AR)
            P.ev("dve", lambda e, nrb=nrb, q6=q6: e.tensor_tensor(nrb[:], q6[:].rearrange("p (n t) -> p n t", n=NC_), mui4[:], ALU.mult), [q6, mui4], [nrb])
            for n in range(NC_):
                P.mm(q7[:, ncols(n)], q7, f2(bdK, n), bdK, bR(n), AR)
            P.ev("dve", lambda e, nrk=nrk, q7=q7: e.tensor_tensor(nrk[:], q7[:].rearrange("p (n t) -> p n t", n=NC_), mui4[:], ALU.mult), [q7, mui4], [nrk])
            Zc = Z[0]
            for n in range(NC_):
                P.mm(q3[:, n * 64:(n + 1) * 64], q3, f2(bdV, n), bdV, sel[:], sel)
            for n in range(NC_):
                P.mm(q3[:, 256 + n * 64:256 + (n + 1) * 64], q3, bA(n), AR, sel[:], sel)
            P.ev("act", lambda e, vtm=vtm, q3=q3: e.activation(vtm[:], q3[:, 0:256].rearrange("p (n v) -> p n v", n=NC_), AF.Copy), [q3], [vtm])
            P.ev("act", lambda e, Zc=Zc, q3=q3: e.activation(Zc[:, :, 0:64], q3[:, 256:512].rearrange("p (n v) -> p n v", n=NC_), AF.Copy), [q3], [Zc])
            for n in range(NC_):
                P.mm(q4[:, n:n + 1], q4, f2(bdRK, n), bdRK, onesc[:], onesc)
            P.ev("act", lambda e, sbn=sbn, q4=q4: e.activation(sbn[:], q4[:, 0:NC_], AF.Copy), [q4], [sbn])
            for n in range(NC_):
                P.mm(q5[:, n * 64:(n + 1) * 64], q5, Nk[:, n, :], Nk, vtm[:, n, :], vtm)
            P.ev("act", lambda e, Zc=Zc, q5=q5: e.activation(Zc[:, :, 64:128], q5[:, 0:256].rearrange("p (n v) -> p n v", n=NC_), AF.Copy), [q5], [Zc])
            Ncur, Acur = N0, A0
            zi = 0
            for k in range(6):
                qa = P.ps[3 + (k % 2)]
                Zn = Z[(zi + 1) % 3]
                for n in range(NC_):
                    P.mm(qa[:, ncols(n)], qa, Ncur[:, n, :], Ncur, Zc[:, n, :], Zc)
                P.ev("dve", lambda e, Zn=Zn, qa=qa, Zc=Zc: e.tensor_tensor(Zn[:], qa[:].rearrange("p (n t) -> p n t", n=NC_), Zc[:], ALU.add), [qa, Zc], [Zn])
                Zc = Zn
                zi += 1
                if k < 5:
                    Nn = Nst[(k + 1) % 2]
                    qn_ = P.ps[5 + (k % 2)]
                    for n in range(NC_):
                        P.mm(qn_[:, ncols(n)], qn_, Acur[:, n, :], Acur, Ncur[:, n, :], Ncur)
                    if k < 4:
                        An = Ast[(k + 1) % 2]
                        qb = P.ps[7]
                        for n in range(NC_):
                            P.mm(qb[:, ncols(n)], qb, Ncur[:, n, :], Ncur, Acur[:, n, :], Acur)
                    P.ev("act", lambda e, Nn=Nn, qn_=qn_: e.activation(Nn[:], qn_[:].rearrange("p (n t) -> p n t", n=NC_), AF.Copy), [qn_], [Nn])
                    if k < 4:
                        P.ev("act", lambda e, An=An, qb=qb: e.activation(An[:], qb[:].rearrange("p (n t) -> p n t", n=NC_), AF.Copy), [qb], [An])
                        Acur = An
                    Ncur = Nn
            Zf = Zc
            P.ev("dve", lambda e, Zf=Zf: e.tensor_tensor(bdW[:], Zf[:, :, 0:64].unsqueeze(2).broadcast_to([128, NC_, 2, 64]),
                                                        bdm[:].unsqueeze(1).unsqueeze(3).broadcast_to([128, NC_, 2, 64]), ALU.mult), [Zf, bdm], [bdW])
            for n in range(NC_):
                P.tr(q3[:, ncols(n)], q3, f2(bdW, n), bdW, ident)
            P.ev("act", lambda e, wT=wT, q3=q3: e.activation(wT[:], q3[:].rearrange("p (n t) -> p n t", n=NC_), AF.Copy), [q3], [wT])
            for n in range(NC_):
                P.tr(q4[:, ncols(n)], q4, f2(bdB, n), bdB, ident)
            P.ev("act", lambda e, btm=btm, q4=q4: e.activation(btm[:], q4[:].rearrange("p (n t) -> p n t", n=NC_), AF.Copy), [q4], [btm])
            for n in range(NC_):
                P.tr(q5[:, ncols(n)], q5, f2(bdK, n), bdK, ident)
            P.ev("act", lambda e, ktm=ktm, q5=q5: e.activation(ktm[:], q5[:].rearrange("p (n t) -> p n t", n=NC_), AF.Copy), [q5], [ktm])
            Hh = H[hp]
            qy = P.ps[6]
            for n in range(NC_):
                qu = P.ps[7]
                U_ = U[n % 2]
                P.mm(qu[:, 0:64], qu, wT[:, n, :], wT, Hh[:], Hh)
                P.ev("dve", lambda e, U_=U_, qu=qu, Zf=Zf, n=n: e.tensor_tensor(U_[:], qu[:, 0:64], Zf[:, n, 64:128], ALU.add), [qu, Zf], [U_])
                ys = qy[:, n * 64:(n + 1) * 64]
                P.mm(ys, qy, bR(n), AR, Hh[:], Hh, start=True, stop=False)
                P.mm(ys, qy, nrb[:, n, :], nrb, U_[:], U_, start=False, stop=False)
                P.mm(ys, qy, nrk[:, n, :], nrk, vtm[:, n, :], vtm, start=False, stop=True)
                P.mm(qu[:, 64:128], qu, btm[:, n, :], btm, U_[:], U_, start=True, stop=False)
                P.mm(qu[:, 64:128], qu, ktm[:, n, :], ktm, vtm[:, n, :], vtm, start=False, stop=True)
                P.ev("dve", lambda e, qu=qu, Hh=Hh: e.tensor_tensor(Ht[:], qu[:, 64:128], Hh[:], ALU.add), [qu, Hh], [Ht])
                pc = ep[:, n * 64 + 63:n * 64 + 64]
                P.ev("dve", lambda e, Hh=Hh, pc=pc: e.tensor_scalar(Hh[:], Ht[:], pc, None, ALU.mult), [Ht, ep], [Hh])
            yv = qy[:, 0:NC_ * 64].rearrange("p (n v) -> p n v", n=NC_)
            P.ev("act", lambda e, yv=yv: e.activation(Ysb[:], yv, AF.Copy), [qy], [Ysb])
            P.ev("act", lambda e, yv=yv: e.activation(Ysq[:], yv, AF.Square), [qy], [Ysq])
            P.ev("dve", lambda e: e.reduce_sum(st[:, 0, :], Ysb[:], AX.X), [Ysb], [st])
            P.ev("dve", lambda e: e.reduce_sum(st[:, 1, :], Ysq[:], AX.X), [Ysq], [st])
            P.ev("dve", lambda e: e.tensor_scalar(st[:, 0, :], st[:, 0, :], 1.0 / 64, None, ALU.mult), [st], [st])
            P.ev("dve", lambda e: e.tensor_tensor(st[:, 2, :], st[:, 0, :], st[:, 0, :], ALU.mult), [st], [st])
            P.ev("dve", lambda e: e.scalar_tensor_tensor(st[:, 1, :], st[:, 1, :], 1.0 / 64, st[:, 2, :], ALU.mult, ALU.subtract), [st], [st])
            P.ev("act", lambda e: e.activation(st[:, 1, :], st[:, 1, :], AF.Sqrt, bias=GN_EPS, scale=1.0), [st], [st])
            P.ev("dve", lambda e: e.reciprocal(st[:, 1, :], st[:, 1, :]), [st], [st])
            P.ev("dve", lambda e: e.tensor_tensor(Ysb[:], Ysb[:], st[:, 0, :].unsqueeze(2).broadcast_to([128, NC_, 64]), ALU.subtract), [Ysb, st], [Ysb])
            P.ev("dve", lambda e: e.tensor_tensor(Ysb[:], Ysb[:], st[:, 1, :].unsqueeze(2).broadcast_to([128, NC_, 64]), ALU.mult), [Ysb, st], [Ysb])
            P.ev("pool", lambda e, hp=hp: e.tensor_tensor(Ysb[:], Ysb[:], gnw[:, hp, :].unsqueeze(1).broadcast_to([128, NC_, 64]), ALU.mult), [Ysb, gnw], [Ysb])
            P.ev("pool", lambda e, hp=hp: e.tensor_tensor(Ysb[:], Ysb[:], gnb[:, hp, :].unsqueeze(1).broadcast_to([128, NC_, 64]), ALU.add), [Ysb, gnb], [Ysb])
            P.ev("dve", lambda e, vtm=vtm, sbn=sbn: e.tensor_tensor(Ysq[:], vtm[:], sbn[:].unsqueeze(2).broadcast_to([128, NC_, 64]), ALU.mult), [vtm, sbn], [Ysq])
            P.ev("dve", lambda e: e.tensor_tensor(Ysb[:], Ysb[:], Ysq[:], ALU.add), [Ysb, Ysq], [Ysb])
            P.ev("pool", lambda e: e.tensor_tensor(bdY[:], Ysb[:].unsqueeze(2).broadcast_to([128, NC_, 2, 64]),
                                                   bdm[:].unsqueeze(1).unsqueeze(3).broadcast_to([128, NC_, 2, 64]), ALU.mult), [Ysb, bdm], [bdY])
            for n in range(NC_):
                P.tr(q3[:, ncols(n)], q3, f2(bdY, n), bdY, ident)
            q3v = q3[:].rearrange("p (n h t) -> p n h t", n=NC_, h=2)
            P.ev("dve", lambda e, q3v=q3v: e.tensor_copy(v3(ycm), q3v[:, :, 0, :]), [q3], [ycm])
            P.ev("dve", lambda e, q3v=q3v: e.tensor_tensor(v3(ycm), v3(ycm), q3v[:, :, 1, :], ALU.add), [q3, ycm], [ycm])
            P.ev("pool", lambda e, og_=og_, sg_=sg_: e.tensor_tensor(og_[:], ycm[:], sg_[:], ALU.mult), [ycm, sg_], [og_])
            P.store(OGT[hp * 128:(hp + 1) * 128, c0:c0 + W], og_[:], og_, dbuf=bOGT)
    P.release(m3)
    stage_outproj(P, OGT, bOGT, A["rw_w_out"][j], x_d, x_buf, xo_d, xo_buf)


def host_consts(T):
    c = {}
    c["c_ident"] = np.eye(128, dtype=np.float32)
    import ml_dtypes
    k = np.arange(128)[:, None]
    q = np.arange(128)[None, :]
    c["c_mask"] = (k <= q).astype(np.float32)
    inv = (10000.0 ** (-np.arange(0, 128, 2, dtype=np.float32) / np.float32(128))).astype(np.float32)
    c["c_inv"] = np.ascontiguousarray(np.broadcast_to(inv[None, :], (128, 64))).astype(np.float32)
    hh = np.arange(128) // 64
    tt = np.arange(128) % 64
    same = (hh[:, None] == hh[None, :])
    mu_ = (same & (tt[:, None] < tt[None, :])).astype(np.float32)
    mui = (same & (tt[:, None] <= tt[None, :])).astype(np.float32)
    ml_ = (same & (tt[:, None] > tt[None, :])).astype(np.float32)
    c["c_m4"] = np.ascontiguousarray(np.stack([mu_, mui, mu_, mui], axis=1))
    c["c_ml"] = ml_
    c["c_sel"] = np.concatenate([np.eye(64), np.eye(64)], axis=0).astype(np.float32)
    c["c_bones"] = same.astype(np.float32)
    c["c_bdm"] = (hh[:, None] == np.arange(2)[None, :]).astype(np.float32)
    return c


def core_inputs(inputs, b, T):
    m = {}
    m["x"] = np.ascontiguousarray(inputs["x"][b, :T])
    pos = np.asarray(inputs["positions"][b, :T]).astype(np.int32)
    m["pos"] = np.ascontiguousarray(pos.reshape(T // 128, 128).T)
    for nm in ATTN_W + RWKV_W:
        m[nm] = np.ascontiguousarray(np.asarray(inputs[nm]))
    NR = inputs["rw_mu"].shape[0]
    mu = np.asarray(inputs["rw_mu"])
    mucol = np.zeros((NR, 128, 66), np.float32)
    mucol[:, :, :64] = mu[:, :8192].reshape(NR, 64, 128).transpose(0, 2, 1)
    mucol[:, :96, 64] = mu[:, 8192:8288]
    mucol[:, :96, 65] = mu[:, 8288:8384]
    m["rw_mucol"] = mucol
    vecs = np.stack([np.asarray(inputs[n]).reshape(NR, 16, 128).transpose(0, 2, 1)
                     for n in ["rw_w0", "rw_a0", "rw_k_k", "rw_k_a", "rw_r_k"]], axis=2)
    m["rw_vecs"] = np.ascontiguousarray(vecs.astype(np.float32))
    NV = inputs["rw_v0"].shape[0]
    m["rw_v0col"] = np.ascontiguousarray(np.asarray(inputs["rw_v0"]).reshape(NV, 16, 128).transpose(0, 2, 1))
    vm = np.zeros((NV, 128, 1), np.float32)
    vm[:, :64, 0] = np.asarray(inputs["rw_vres_mu"])
    m["rw_vmucol"] = vm
    m.update(host_consts(T))
    return m


def kernel(**inputs):
    T = inputs["x"].shape[1]
    B = inputs["x"].shape[0]
    maps = [core_inputs(inputs, c % B, T) for c in range(NCORES)]
    shapes = {k: (list(v.shape), I32 if v.dtype == np.int32 else F32) for k, v in maps[0].items()}
    P = build_program(T, [0, 1, 2, 3], shapes)
    res = run_bass_kernel_spmd(P.nc, maps, core_ids=list(range(NCORES)))
    out = np.stack([res.results[b]["out"] for b in range(B)], axis=0)
    return out.astype(np.float32)
```

```python
import math
import numpy as np
import concourse.bass as bass
import concourse.mybir as mybir
from concourse.bass_utils import run_bass_kernel_spmd

F32 = mybir.dt.float32
BF16 = mybir.dt.bfloat16
I32 = mybir.dt.int32
AF = mybir.ActivationFunctionType
ALU = mybir.AluOpType
AX = mybir.AxisListType

SEM_EPOCH = 30000
D = 2048
NORM_EPS = 1e-6
GN_EPS = 64e-5
NCORES = 8


class Buf:
    __slots__ = ("name", "writer", "readers")

    def __init__(self, name=""):
        self.name = name
        self.writer = None
        self.readers = {}


class KB:
    ENGS = ("pe", "act", "dve", "pool", "sp")

    def __init__(self, nc, n_dma_sems=40):
        self.nc = nc
        self.ops = {e: [] for e in self.ENGS}
        self.cnt = {e: 0 for e in self.ENGS}
        self.sems = {}
        self.seen = {e: {} for e in self.ENGS}
        self.n_dma_sems = n_dma_sems
        self.dma_next = 0
        self.dma_cum = [0] * n_dma_sems
        self._stack = []
        self.ninst = 0

    def _sem(self, key):
        if key not in self.sems:
            cm = self.nc.semaphore("s_%s_%s" % key)
            h = cm.__enter__()
            self._stack.append(cm)
            self.sems[key] = h
        return self.sems[key]

    def _wait(self, eng, dep):
        key, val = dep
        if self.seen[eng].get(key, 0) >= val:
            return
        self.seen[eng][key] = val
        h = self._sem(key)
        self.ops[eng].append(lambda e, h=h, val=val: e.wait_ge(h, val))

    def _deps(self, eng, reads, writes):
        deps = []
        for r in reads:
            if r.writer is not None:
                deps.append(r.writer)
        for w in writes:
            if w.writer is not None and w.writer[0][0] != eng:
                deps.append(w.writer)
            for k, v in w.readers.items():
                if k[0] != eng:
                    deps.append((k, v))
        return deps

    def op(self, eng, fn, reads=(), writes=()):
        for dep in self._deps(eng, reads, writes):
            if eng == "pe" and dep[0][0] == "pe":
                continue
            self._wait(eng, dep)
        self.cnt[eng] += 1
        n = self.cnt[eng]
        key = (eng, (n - 1) // SEM_EPOCH)
        val = (n - 1) % SEM_EPOCH + 1
        h = self._sem(key)
        self.ops[eng].append(lambda e, fn=fn, h=h: fn(e).then_inc(h, 1))
        tok = (key, val)
        for r in reads:
            r.readers[key] = val
        for w in writes:
            w.writer = tok
            w.readers = {}
        self.ninst += 1
        return tok

    def dma(self, q, out, in_, reads=(), writes=(), **kw):
        j = self.dma_next
        self.dma_next = (j + 1) % self.n_dma_sems
        key = ("dma", j)
        if self.dma_cum[j] > 0:
            self._wait(q, (key, self.dma_cum[j]))
        for dep in self._deps(q, reads, writes):
            self._wait(q, dep)
        self.dma_cum[j] += 16
        val = self.dma_cum[j]
        h = self._sem(key)
        self.ops[q].append(lambda e, h=h, out=out, in_=in_, kw=kw:
                           e.dma_start(out=out, in_=in_, **kw).then_inc(h, 16))
        tok = (key, val)
        for r in reads:
            r.readers[key] = val
        for w in writes:
            w.writer = tok
            w.readers = {}
        self.ninst += 1
        return tok

    def collective(self, kind, groups, in_ap, out_ap, reads=(), writes=()):
        q = "pool"
        j = self.dma_next
        self.dma_next = (j + 1) % self.n_dma_sems
        key = ("dma", j)
        if self.dma_cum[j] > 0:
            self._wait(q, (key, self.dma_cum[j]))
        for dep in self._deps(q, reads, writes):
            self._wait(q, dep)
        self.dma_cum[j] += 16
        val = self.dma_cum[j]
        h = self._sem(key)
        self.ops[q].append(lambda e, h=h: e.collective_compute(kind, ALU.bypass, replica_groups=groups,
                                                               ins=[in_ap], outs=[out_ap]).then_inc(h, 16))
        tok = (key, val)
        for r in reads:
            r.readers[key] = val
        for w in writes:
            w.writer = tok
            w.readers = {}
        return tok

    def barrier(self):
        toks = []
        for f in ("pe", "act", "dve", "pool"):
            n = self.cnt[f]
            if n > 0:
                toks.append(((f, (n - 1) // SEM_EPOCH), (n - 1) % SEM_EPOCH + 1))
        for j in range(self.n_dma_sems):
            if self.dma_cum[j] > 0:
                toks.append((("dma", j), self.dma_cum[j]))
        for e in self.ENGS:
            for t in toks:
                if t[0][0] == e:
                    continue
                self._wait(e, t)

    def finish(self):
        for j in range(self.n_dma_sems):
            if self.dma_cum[j] > 0:
                self._wait("sp", (("dma", j), self.dma_cum[j]))

    def emit(self):
        nc = self.nc
        with nc.Block() as block:
            @block.tensor
            def _(e):
                for f in self.ops["pe"]:
                    f(e)

            @block.scalar
            def _(e):
                for f in self.ops["act"]:
                    f(e)

            @block.vector
            def _(e):
                for f in self.ops["dve"]:
                    f(e)

            @block.gpsimd
            def _(e):
                for f in self.ops["pool"]:
                    f(e)

            @block.sync
            def _(e):
                for f in self.ops["sp"]:
                    f(e)
        for cm in reversed(self._stack):
            cm.__exit__(None, None, None)


class TT:
    __slots__ = ("t", "b")

    def __init__(self, t, name=""):
        self.t = t
        self.b = Buf(name)

    def __getitem__(self, k):
        return self.t[k]


_DT_BYTES = {F32: 4, BF16: 2, I32: 4}


class Prog:
    def __init__(self, T):
        self.T = T
        self.nc = bass.Bass("TRN2", target_bir_lowering=False)
        self.kb = KB(self.nc)
        self.off = 16640
        self.uid = 0
        self.ps = [TT(self.nc.alloc_psum_tensor("psb%d" % i, [128, 512], F32), "ps%d" % i) for i in range(8)]
        self.rr = 0
        self.SB_LIMIT = 229376

    def sb(self, shape, dt=F32, name="t"):
        n = 1
        for s in shape[1:]:
            n *= s
        nbytes = n * _DT_BYTES[dt]
        nbytes = (nbytes + 63) // 64 * 64
        self.uid += 1
        assert self.off + nbytes <= self.SB_LIMIT, ("SBUF overflow", name, self.off, nbytes)
        t = self.nc.alloc_sbuf_tensor_at("%s_%d" % (name, self.uid), list(shape), dt, offset=self.off)
        self.off += nbytes
        return TT(t, name)

    def mark(self):
        return self.off

    def release(self, m):
        self.kb.barrier()
        self.off = m

    def din(self, name, shape, dt=F32):
        return self.nc.dram_tensor(name, list(shape), dt, kind="ExternalInput").ap()

    def dout(self, name, shape, dt=F32):
        return self.nc.dram_tensor(name, list(shape), dt, kind="ExternalOutput").ap()

    def dscratch(self, name, shape, dt=F32):
        return self.nc.dram_tensor(name, list(shape), dt, kind="Internal").ap()

    def mm(self, out, o_tt, lhsT, l_tt, rhs, r_tt, start=True, stop=True):
        self.kb.op("pe", lambda e: e.matmul(out, lhsT, rhs, start=start, stop=stop),
                   reads=[l_tt.b, r_tt.b], writes=[o_tt.b])

    def tr(self, out, o_tt, in_, i_tt, ident):
        self.kb.op("pe", lambda e: e.transpose(out, in_, ident[:]), reads=[i_tt.b, ident.b], writes=[o_tt.b])

    def ev(self, eng, fn, reads, writes):
        self.kb.op(eng, fn, reads=[r.b for r in reads], writes=[w.b for w in writes])

    def alt(self):
        self.rr ^= 1
        return "act" if self.rr else "dve"

    def copy(self, eng, out, o_tt, in_, i_tt):
        if eng == "act":
            self.ev("act", lambda e: e.activation(out, in_, AF.Copy), [i_tt], [o_tt])
        else:
            self.ev(eng, lambda e: e.tensor_copy(out, in_), [i_tt], [o_tt])

    def load(self, out, o_tt, in_, q="sp", dbuf=None):
        self.kb.dma(q, out, in_, reads=[dbuf] if dbuf is not None else [], writes=[o_tt.b])

    def store(self, out, in_, i_tt, q="sp", dbuf=None):
        self.kb.dma(q, out, in_, reads=[i_tt.b], writes=[dbuf] if dbuf is not None else [])


def stage_norm_T(P, x_d, x_buf, normw_row_d, hnT, ident):
    T = P.T
    m = P.mark()
    nw = P.sb([128, D], F32, "nw")
    P.load(nw[:], nw, normw_row_d.partition_broadcast(128))
    xb = [P.sb([128, D], F32, "xb") for _ in range(2)]
    xs = [P.sb([128, D], F32, "xs") for _ in range(2)]
    junk = P.sb([128, D], BF16, "junk")
    ss = [P.sb([128, 1], F32, "ss") for _ in range(2)]
    for tb in range(T // 128):
        xt = xb[tb % 2]
        st = ss[tb % 2]
        xst = xs[tb % 2]
        P.load(xt[:], xt, x_d[tb * 128:(tb + 1) * 128, :], dbuf=x_buf)
        P.ev("act", lambda e, xt=xt, st=st: e.activation(junk[:], xt[:], AF.Square, accum_out=st[:]), [xt], [junk, st])
        P.ev("act", lambda e, st=st: e.activation(st[:], st[:], AF.Sqrt, bias=NORM_EPS, scale=1.0 / D), [st], [st])
        P.ev("dve", lambda e, st=st: e.reciprocal(st[:], st[:]), [st], [st])
        P.ev("dve", lambda e, xt=xt, st=st, xst=xst: e.scalar_tensor_tensor(xst[:], xt[:], st[:, 0:1], nw[:], ALU.mult, ALU.mult),
             [xt, st, nw], [xst])
        for g in range(4):
            pb = P.ps[g % 2]
            for j in range(4):
                c = g * 4 + j
                P.tr(pb[:, j * 128:(j + 1) * 128], pb, xst[:, c * 128:(c + 1) * 128], xst, ident)
            eng = P.alt()
            P.copy(eng, hnT[:, g * 4:(g + 1) * 4, tb * 128:(tb + 1) * 128],
                   hnT, pb[:].rearrange("p (a b) -> p a b", a=4), pb)
    P.release(m)


def stage_outproj(P, ogT_d, ogT_buf, wout_d, xin_d, xin_buf, xout_d, xout_buf):
    T = P.T
    m = P.mark()
    wo = P.sb([128, 16, D], BF16, "wo")
    wv = wout_d.rearrange("(kc p) n -> p kc n", p=128)
    for i in range(4):
        P.load(wo[:, i * 4:(i + 1) * 4, :], wo, wv[:, i * 4:(i + 1) * 4, :], q="pool")
    ogb = [P.sb([128, 16, 128], BF16, "ogb") for _ in range(2)]
    xb = [P.sb([128, D], F32, "xb") for _ in range(2)]
    ob = [P.sb([128, D], F32, "ob") for _ in range(2)]
    ogv = ogT_d.rearrange("(kc p) t -> p kc t", p=128)
    for tb in range(T // 128):
        og = ogb[tb % 2]
        xt = xb[tb % 2]
        ot = ob[tb % 2]
        P.load(og[:], og, ogv[:, :, tb * 128:(tb + 1) * 128], dbuf=ogT_buf)
        P.load(xt[:], xt, xin_d[tb * 128:(tb + 1) * 128, :], dbuf=xin_buf)
        for cg in range(4):
            pb = P.ps[(tb * 4 + cg) % 4]
            for kc in range(16):
                P.mm(pb[:], pb, og[:, kc, :], og, wo[:, kc, cg * 512:(cg + 1) * 512], wo, start=(kc == 0), stop=(kc == 15))
            P.ev("dve", lambda e, ot=ot, pb=pb, xt=xt, cg=cg: e.tensor_tensor(ot[:, cg * 512:(cg + 1) * 512], pb[:],
                                                                              xt[:, cg * 512:(cg + 1) * 512], ALU.add), [pb, xt], [ot])
        P.store(xout_d[tb * 128:(tb + 1) * 128, :], ot[:], ot, q="sp", dbuf=xout_buf)
    P.release(m)


def attn_layer(P, li, j, x_d, x_buf, xo_d, xo_buf, A, C):
    T = P.T
    NT = T // 128
    lambda_init = 0.8 - 0.6 * math.exp(-0.3 * li)
    ident = C["ident"]
    QT, KT, V, G, OGT = C["QT"], C["KT"], C["V"], C["G"], C["OGT"]
    bQT, bKT, bV, bG, bOGT = C["bQT"], C["bKT"], C["bV"], C["bG"], C["bOGT"]

    m0 = P.mark()
    setup_rope(P, A, C)
    hnT = P.sb([128, 16, T], BF16, "hnT")
    stage_norm_T(P, x_d, x_buf, A["norm_w"][li:li + 1, :], hnT, ident)

    m1 = P.mark()
    cos, sin = C["cos"], C["sin"]
    gq = P.sb([128, 128], F32, "gq")
    gk = P.sb([128, 128], F32, "gk")
    P.load(gq[:], gq, A["da_q_gain"][j:j + 1, :].partition_broadcast(128))
    P.load(gk[:], gk, A["da_k_gain"][j:j + 1, :].partition_broadcast(128))
    wb = [P.sb([128, 16, 512], BF16, "wb") for _ in range(2)]
    ND = 3
    sqs = [P.sb([128, 512], F32, "sq") for _ in range(2)]
    s4 = [P.sb([128, 4], F32, "s4") for _ in range(ND)]
    qn = [P.sb([128, 4, 128], F32, "qn") for _ in range(ND)]
    qr = [P.sb([128, 4, 128], F32, "qr") for _ in range(ND)]
    tmp = [P.sb([128, 4, 64], F32, "tmp") for _ in range(ND)]
    qT = [P.sb([128, 4, 128], BF16, "qT") for _ in range(ND)]
    wv = A["da_w_in"][j].rearrange("(kc p) n -> p kc n", p=128)
    it = 0

    def load_w(cg):
        w = wb[cg % 2]
        for h2 in range(2):
            P.load(w[:, h2 * 8:(h2 + 1) * 8, :], w, wv[:, h2 * 8:(h2 + 1) * 8, cg * 512:(cg + 1) * 512], q="pool")

    load_w(0)
    for cg in range(16):
        w = wb[cg % 2]
        if cg + 1 < 16:
            load_w(cg + 1)
        for tb in range(NT):
            pb = P.ps[it % 4]
            for kc in range(16):
                P.mm(pb[:], pb, hnT[:, kc, tb * 128:(tb + 1) * 128], hnT, w[:, kc, :], w, start=(kc == 0), stop=(kc == 15))
            if cg < 8:
                gain = gq if cg < 4 else gk
                sq = sqs[it % 2]
                s4t, qnt, qrt, tmpt, qTt = s4[it % ND], qn[it % ND], qr[it % ND], tmp[it % ND], qT[it % ND]
                pv = pb[:].rearrange("p (a b) -> p a b", a=4)
                P.ev("act", lambda e, pb=pb, sq=sq: e.activation(sq[:], pb[:], AF.Square), [pb], [sq])
                P.ev("dve", lambda e, s4t=s4t, sq=sq: e.reduce_sum(s4t[:], sq[:].rearrange("p (a b) -> p a b", a=4), AX.X), [sq], [s4t])
                P.ev("act", lambda e, s4t=s4t: e.activation(s4t[:], s4t[:], AF.Sqrt, bias=NORM_EPS, scale=1.0 / 128), [s4t], [s4t])
                P.ev("dve", lambda e, s4t=s4t: e.reciprocal(s4t[:], s4t[:]), [s4t], [s4t])
                P.ev("dve", lambda e, qnt=qnt, pv=pv, s4t=s4t: e.tensor_tensor(qnt[:], pv, s4t[:].unsqueeze(2).broadcast_to([128, 4, 128]), ALU.mult),
                     [pb, s4t], [qnt])
                P.ev("pool", lambda e, qnt=qnt, gain=gain: e.tensor_tensor(qnt[:], qnt[:], gain[:].unsqueeze(1).broadcast_to([128, 4, 128]), ALU.mult),
                     [qnt, gain], [qnt])
                cb = cos[:, tb, :].unsqueeze(1).broadcast_to([128, 4, 64])
                sb_ = sin[:, tb, :].unsqueeze(1).broadcast_to([128, 4, 64])
                P.ev("dve", lambda e, qrt=qrt, qnt=qnt, cb=cb: e.tensor_tensor(qrt[:, :, 0:64], qnt[:, :, 0:64], cb, ALU.mult), [qnt, cos], [qrt])
                P.ev("pool", lambda e, tmpt=tmpt, qnt=qnt, sb_=sb_: e.tensor_tensor(tmpt[:], qnt[:, :, 64:128], sb_, ALU.mult), [qnt, sin], [tmpt])
                P.ev("dve", lambda e, qrt=qrt, tmpt=tmpt: e.tensor_tensor(qrt[:, :, 0:64], qrt[:, :, 0:64], tmpt[:], ALU.subtract), [qrt, tmpt], [qrt])
                P.ev("pool", lambda e, qrt=qrt, qnt=qnt, cb=cb: e.tensor_tensor(qrt[:, :, 64:128], qnt[:, :, 64:128], cb, ALU.mult), [qnt, cos], [qrt])
                P.ev("dve", lambda e, tmpt=tmpt, qnt=qnt, sb_=sb_: e.tensor_tensor(tmpt[:], qnt[:, :, 0:64], sb_, ALU.mult), [qnt, sin], [tmpt])
                P.ev("dve", lambda e, qrt=qrt, tmpt=tmpt: e.tensor_tensor(qrt[:, :, 64:128], qrt[:, :, 64:128], tmpt[:], ALU.add), [qrt, tmpt], [qrt])
                pt = P.ps[4 + it % 2]
                for a in range(4):
                    P.tr(pt[:, a * 128:(a + 1) * 128], pt, qrt[:, a, :], qrt, ident)
                P.ev("act", lambda e, qTt=qTt, pt=pt: e.activation(qTt[:], pt[:].rearrange("p (a b) -> p a b", a=4), AF.Copy), [pt], [qTt])
                dst = QT if cg < 4 else KT
                dbuf = bQT if cg < 4 else bKT
                hc0 = (cg % 4) * 4
                P.store(dst[hc0:hc0 + 4, :, tb * 128:(tb + 1) * 128].rearrange("a p t -> p a t"), qTt[:], qTt, dbuf=dbuf)
            elif cg < 12:
                vt = qT[it % ND]
                vflat = vt[:].rearrange("p a b -> p (a b)")
                P.copy(P.alt(), vflat, vt, pb[:], pb)
                c0 = (cg - 8) * 512
                P.store(V[tb * 128:(tb + 1) * 128, c0:c0 + 512], vflat, vt, dbuf=bV)
            else:
                gt = qn[it % ND]
                gflat = gt[:].rearrange("p a b -> p (a b)")
                P.ev("act", lambda e, gflat=gflat, pb=pb: e.activation(gflat, pb[:], AF.Silu), [pb], [gt])
                c0 = (cg - 12) * 512
                P.store(G[tb * 128:(tb + 1) * 128, c0:c0 + 512], gflat, gt, dbuf=bG)
            it += 1
    P.release(m0)

    m2 = P.mark()
    lv = P.sb([1, 4, 128], F32, "lv")
    for i, nm in enumerate(["da_lam_q1", "da_lam_k1", "da_lam_q2", "da_lam_k2"]):
        P.load(lv[:, i, :], lv, A[nm][j:j + 1, :])
    lp = P.sb([1, 2, 128], F32, "lp")
    l2 = P.sb([1, 2], F32, "l2")
    P.ev("dve", lambda e: e.tensor_tensor(lp[:, 0, :], lv[:, 0, :], lv[:, 1, :], ALU.mult), [lv], [lp])
    P.ev("dve", lambda e: e.tensor_tensor(lp[:, 1, :], lv[:, 2, :], lv[:, 3, :], ALU.mult), [lv], [lp])
    P.ev("dve", lambda e: e.reduce_sum(l2[:], lp[:], AX.X), [lp], [l2])
    P.ev("act", lambda e: e.activation(l2[:], l2[:], AF.Exp), [l2], [l2])
    l1 = P.sb([1, 1], F32, "l1")
    P.ev("dve", lambda e: e.scalar_tensor_tensor(l1[:], l2[:, 1:2], -lambda_init, l2[:, 0:1], ALU.add, ALU.subtract), [l2], [l1])
    ones1 = C["ones1"]
    pb = P.ps[7]
    P.mm(pb[:, 0:1], pb, ones1[0:1, :], ones1, l1[:], l1)
    nlam = P.sb([128, 1], F32, "nlam")
    P.copy("dve", nlam[:], nlam, pb[:, 0:1], pb)
    subw = P.sb([128, 256], F32, "subw")
    P.load(subw[:], subw, A["da_subln_w"][j:j + 1, :].partition_broadcast(128))
    P.ev("dve", lambda e: e.tensor_scalar(subw[:], subw[:], 1.0 - lambda_init, None, ALU.mult), [subw], [subw])
    mask = C["mask"]

    qkb = [[P.sb([128, T], BF16, "qk") for _ in range(4)] for _ in range(2)]
    vh = [P.sb([128, NT, 257], BF16, "vh") for _ in range(2)]
    for v_ in vh:
        P.ev("pool", lambda e, v_=v_: e.memset(v_[:, :, 256:257], 1.0), [], [v_])
    ptb = [P.sb([128, 512], BF16, "pt") for _ in range(3)]
    gt2 = [P.sb([128, 256], F32, "gt2") for _ in range(2)]
    o1 = [P.sb([128, 256], F32, "o1") for _ in range(2)]
    dd = [P.sb([128, 256], F32, "dd") for _ in range(2)]
    rs = [P.sb([128, 4], F32, "rs") for _ in range(2)]
    jk = P.sb([128, 256], BF16, "jk")
    ogT = [P.sb([128, 2, 128], BF16, "ogT") for _ in range(2)]
    scale = 128 ** -0.5
    nsb = T // 256
    ipair = 0
    ifin = 0
    for h in range(8):
        qk = qkb[h % 2]
        vt = vh[h % 2]
        P.load(qk[0][:], qk[0], QT[2 * h, :, :], dbuf=bQT)
        P.load(qk[1][:], qk[1], QT[2 * h + 1, :, :], dbuf=bQT)
        P.load(qk[2][:], qk[2], KT[2 * h, :, :], dbuf=bKT)
        P.load(qk[3][:], qk[3], KT[2 * h + 1, :, :], dbuf=bKT)
        P.load(vt[:, :, 0:256], vt, V[:, h * 256:(h + 1) * 256].rearrange("(n p) c -> p n c", p=128), dbuf=bV)
        acc = [[P.ps[2 + c * 2 + s_] for s_ in range(2)] for c in range(2)]
        pairs = [(qs, kb_) for qs in range(nsb) for kb_ in range(2 * qs + 2)]
        slots = {}

        def emit_S(idx):
            nonlocal ipair
            qs, kb_ = pairs[idx]
            nkb = 2 * qs + 2
            last = (kb_ == nkb - 1)
            psb = P.ps[ipair % 2]
            pt = ptb[ipair % 3]
            ipair += 1
            slots[idx] = pt
            q0 = 128 if last else 0
            for c in range(2):
                P.mm(psb[:, c * 256 + q0:(c + 1) * 256], psb, qk[2 + c][:, kb_ * 128:(kb_ + 1) * 128], qk[2 + c],
                     qk[c][:, qs * 256 + q0:(qs + 1) * 256], qk[c])
            if last:
                pin = psb[:].rearrange("p (c q) -> p c q", c=2)[:, :, 128:256]
                pout = pt[:].rearrange("p (c q) -> p c q", c=2)[:, :, 128:256]
            else:
                pin = psb[:]
                pout = pt[:]
            P.ev("act", lambda e, pin=pin, pout=pout: e.activation(pout, pin, AF.Exp, scale=scale), [psb], [pt])
            if kb_ >= nkb - 2:
                sub = kb_ - (nkb - 2)
                pm = pt[:].rearrange("p (c q) -> p c q", c=2)[:, :, sub * 128:(sub + 1) * 128]
                P.ev("pool", lambda e, pm=pm: e.tensor_tensor(pm, pm, mask[:].unsqueeze(1).broadcast_to([128, 2, 128]), ALU.mult),
                     [pt, mask], [pt])

        def emit_AV(idx):
            qs, kb_ = pairs[idx]
            pt = slots.pop(idx)
            for s_ in range(2):
                if kb_ > 2 * qs + s_:
                    continue
                for c in range(2):
                    a_ = acc[c][s_]
                    P.mm(a_[:, 0:257], a_, pt[:, c * 256 + s_ * 128:c * 256 + (s_ + 1) * 128], pt, vt[:, kb_, :], vt,
                         start=(kb_ == 0), stop=(kb_ == 2 * qs + s_))

        def finalize(qs):
            nonlocal ifin
            for s_ in range(2):
                tb = qs * 2 + s_
                a1, a2 = acc[0][s_], acc[1][s_]
                g_, o_, d_, r_, og_ = gt2[ifin % 2], o1[ifin % 2], dd[ifin % 2], rs[ifin % 2], ogT[ifin % 2]
                P.load(g_[:], g_, G[tb * 128:(tb + 1) * 128, h * 256:(h + 1) * 256], dbuf=bG)
                P.ev("dve", lambda e, r_=r_, a1=a1: e.reciprocal(r_[:, 0:1], a1[:, 256:257]), [a1], [r_])
                P.ev("dve", lambda e, r_=r_, a2=a2: e.reciprocal(r_[:, 1:2], a2[:, 256:257]), [a2], [r_])
                P.ev("dve", lambda e, r_=r_: e.tensor_tensor(r_[:, 1:2], r_[:, 1:2], nlam[:], ALU.mult), [r_, nlam], [r_])
                P.ev("act", lambda e, o_=o_, a1=a1, r_=r_: e.activation(o_[:], a1[:, 0:256], AF.Copy, scale=r_[:, 0:1]), [a1, r_], [o_])
                P.ev("dve", lambda e, d_=d_, a2=a2, r_=r_, o_=o_: e.scalar_tensor_tensor(d_[:], a2[:, 0:256], r_[:, 1:2], o_[:], ALU.mult, ALU.add),
                     [a2, r_, o_], [d_])
                P.ev("act", lambda e, d_=d_, r_=r_: e.activation(jk[:], d_[:], AF.Square, accum_out=r_[:, 2:3]), [d_], [jk, r_])
                P.ev("act", lambda e, r_=r_: e.activation(r_[:, 2:3], r_[:, 2:3], AF.Sqrt, bias=NORM_EPS, scale=1.0 / 256), [r_], [r_])
                P.ev("dve", lambda e, r_=r_: e.reciprocal(r_[:, 2:3], r_[:, 2:3]), [r_], [r_])
                P.ev("dve", lambda e, d_=d_, r_=r_: e.scalar_tensor_tensor(d_[:], d_[:], r_[:, 2:3], subw[:], ALU.mult, ALU.mult), [d_, r_, subw], [d_])
                P.ev("pool", lambda e, d_=d_, g_=g_: e.tensor_tensor(d_[:], d_[:], g_[:], ALU.mult), [d_, g_], [d_])
                pt_ = P.ps[6 + ifin % 2]
                for a in range(2):
                    P.tr(pt_[:, a * 128:(a + 1) * 128], pt_, d_[:, a * 128:(a + 1) * 128], d_, ident)
                P.ev("act", lambda e, og_=og_, pt_=pt_: e.activation(og_[:], pt_[:, 0:256].rearrange("p (a b) -> p a b", a=2), AF.Copy), [pt_], [og_])
                P.store(OGT[h * 256:(h + 1) * 256, tb * 128:(tb + 1) * 128].rearrange("(a p) t -> p a t", p=128), og_[:], og_, dbuf=bOGT)
                ifin += 1

        npairs = len(pairs)
        emit_S(0)
        for idx in range(npairs):
            if idx + 1 < npairs:
                emit_S(idx + 1)
            emit_AV(idx)
            qs, kb_ = pairs[idx]
            if kb_ == 2 * qs + 1:
                finalize(qs)
    P.release(m2)

    stage_outproj(P, OGT, bOGT, A["da_w_out"][j], x_d, x_buf, xo_d, xo_buf)


def setup_consts(P, A, C):
    T = P.T
    NT = T // 128
    ident = P.sb([128, 128], F32, "ident")
    P.load(ident[:], ident, A["c_ident"])
    C["ident"] = ident
    ones1 = P.sb([1, 128], F32, "ones1")
    P.ev("dve", lambda e: e.memset(ones1[:], 1.0), [], [ones1])
    C["ones1"] = ones1
    mask = P.sb([128, 128], BF16, "mask")
    P.load(mask[:], mask, A["c_mask"], q="pool")
    C["mask"] = mask
    for nm, shp in (("c_m4", [128, 4, 128]), ("c_ml", [128, 128]), ("c_sel", [128, 64]), ("c_bones", [128, 128]), ("c_bdm", [128, 2])):
        t = P.sb(shp, F32, nm)
        P.load(t[:], t, A[nm])
        C[nm] = t
    onesc = P.sb([128, 1], F32, "onesc")
    P.ev("dve", lambda e: e.memset(onesc[:], 1.0), [], [onesc])
    C["onesc"] = onesc


def setup_rope(P, A, C):
    T = P.T
    NT = T // 128
    cos = P.sb([128, NT, 64], F32, "cos")
    sin = P.sb([128, NT, 64], F32, "sin")
    m = P.mark()
    posi = P.sb([128, NT], I32, "posi")
    posf = P.sb([128, NT], F32, "posf")
    inv = P.sb([128, 64], F32, "inv")
    ang = P.sb([128, NT, 64], F32, "ang")
    n_ = P.sb([128, NT, 64], F32, "n_")
    P.load(posi[:], posi, A["pos"])
    P.load(inv[:], inv, A["c_inv"])
    P.ev("dve", lambda e: e.tensor_copy(posf[:], posi[:]), [posi], [posf])
    P.ev("dve", lambda e: e.tensor_tensor(ang[:], posf[:].unsqueeze(2).broadcast_to([128, NT, 64]),
                                          inv[:].unsqueeze(1).broadcast_to([128, NT, 64]), ALU.mult), [posf, inv], [ang])
    TWO_PI = 2.0 * math.pi
    MAGIC = 12582912.0
    for tab, shift in ((sin, 0.0), (cos, math.pi / 2)):
        P.ev("dve", lambda e, shift=shift: e.tensor_scalar(n_[:], ang[:], shift, 1.0 / TWO_PI, ALU.add, ALU.mult), [ang], [n_])
        P.ev("dve", lambda e: e.tensor_scalar(n_[:], n_[:], MAGIC, -MAGIC, ALU.add, ALU.add), [n_], [n_])
        P.ev("dve", lambda e, tab=tab: e.scalar_tensor_tensor(tab[:], n_[:], -TWO_PI, ang[:], ALU.mult, ALU.add), [n_, ang], [tab])
        P.ev("dve", lambda e, tab=tab, shift=shift: e.tensor_scalar(tab[:], tab[:], shift, 3.14159, ALU.add, ALU.min), [tab], [tab])
        P.ev("dve", lambda e, tab=tab: e.tensor_scalar(tab[:], tab[:], -3.14159, None, ALU.max), [tab], [tab])
        P.ev("act", lambda e, tab=tab: e.activation(tab[:], tab[:], AF.Sin), [tab], [tab])
    P.release(m)
    C["cos"], C["sin"] = cos, sin


RWKV_W = ["rw_w_in", "rw_decay_up", "rw_iclr_up", "rw_gn_w", "rw_gn_b", "rw_w_out", "rw_vres_down", "rw_vres_up"]
ATTN_W = ["norm_w", "da_w_in", "da_q_gain", "da_k_gain", "da_lam_q1", "da_lam_k1", "da_lam_q2", "da_lam_k2",
          "da_subln_w", "da_w_out"]


def build_program(T, layers, shapes):
    P = Prog(T)
    A = {}
    for nm, (shp, dt) in shapes.items():
        A[nm] = P.din(nm, shp, dt)
    out_d = P.dout("out", [T, D], F32)
    C = {}
    C["QT"] = P.dscratch("QT", [16, 128, T], BF16)
    C["KT"] = P.dscratch("KT", [16, 128, T], BF16)
    C["V"] = P.dscratch("V", [T, D], BF16)
    C["G"] = P.dscratch("G", [T, D], F32)
    C["OGT"] = P.dscratch("OGT", [D, T], BF16)
    for nm in ["QT", "KT", "V", "G", "OGT"]:
        C["b" + nm] = Buf(nm)
    C["PT"] = [P.dscratch("PT%d" % i, [67 * 128, T], F32) for i in range(2)]
    C["bPT"] = [Buf("PT0"), Buf("PT1")]
    xs = [P.dscratch("XA", [T, D], F32), P.dscratch("XB", [T, D], F32)]
    xbufs = [Buf("XA"), Buf("XB")]
    setup_consts(P, A, C)
    cur, cur_b = A["x"], Buf("x")
    for n, li in enumerate(layers):
        if n == len(layers) - 1:
            nxt, nxt_b = out_d, Buf("out")
        else:
            nxt, nxt_b = xs[n % 2], xbufs[n % 2]
        if li % 2 == 0:
            attn_layer(P, li, li // 2, cur, cur_b, nxt, nxt_b, A, C)
        else:
            rwkv_layer(P, li, li // 2, cur, cur_b, nxt, nxt_b, A, C)
        cur, cur_b = nxt, nxt_b
    P.kb.finish()
    P.kb.emit()
    return P


def rwkv_layer(P, li, j, x_d, x_buf, xo_d, xo_buf, A, C):
    T = P.T
    ident = C["ident"]
    PT, bPT = C["PT"][j], C["bPT"][j]
    has_vres = j > 0
    OGT, bOGT = C["OGT"], C["bOGT"]
    m0 = P.mark()
    hnT = P.sb([128, 16, T], BF16, "hnT")
    stage_norm_T(P, x_d, x_buf, A["norm_w"][li:li + 1, :], hnT, ident)

    TH = min(2048, T)
    NHALF = T // TH
    mucol = P.sb([128, 66], F32, "mucol")
    P.load(mucol[:], mucol, A["rw_mucol"][j])
    if has_vres:
        vmucol = P.sb([128, 1], F32, "vmucol")
        P.load(vmucol[:], vmucol, A["rw_vmucol"][j - 1])
    wb = [P.sb([128, 16, 512], BF16, "wb") for _ in range(2)]
    pT = [P.sb([128, TH + 1], F32, "pT") for _ in range(2)]
    dT = P.sb([128, TH], F32, "dT")
    res = [P.sb([128, TH], F32, "res") for _ in range(2)]
    wv = A["rw_w_in"][j].rearrange("(kc p) n -> p kc n", p=128)
    groups = []
    for g in range(16):
        groups.append((wv, g * 512, 512, [(g * 4 + i, 128, i * 128, mucol[:, g * 4 + i:g * 4 + i + 1]) for i in range(4)]))
    groups.append((wv, 8192, 192, [(64, 96, 0, mucol[0:96, 64:65]), (65, 96, 96, mucol[0:96, 65:66])]))
    if has_vres:
        groups.append((A["rw_vres_down"][j - 1].rearrange("(kc p) n -> p kc n", p=128), 0, 64, [(66, 64, 0, vmucol[0:64, 0:1])]))
    ib = it = ir = 0

    def load_wg(gi):
        src, c0, ncg, blks = groups[gi]
        w = wb[gi % 2]
        for h2 in range(2):
            P.load(w[:, h2 * 8:(h2 + 1) * 8, 0:ncg], w, src[:, h2 * 8:(h2 + 1) * 8, c0:c0 + ncg], q="pool")

    load_wg(0)
    for gi, (src, c0, ncg, blks) in enumerate(groups):
        w = wb[gi % 2]
        if gi + 1 < len(groups):
            load_wg(gi + 1)
        for (cb, ncols, wc, mu_ap) in blks:
            p_ = pT[ib % 2]
            ib += 1
            for half in range(NHALF):
                if half == 0:
                    P.ev("pool", lambda e, p_=p_: e.memset(p_[:, 0:1], 0.0), [], [p_])
                else:
                    P.ev("pool", lambda e, p_=p_: e.tensor_copy(p_[:, 0:1], p_[:, TH:TH + 1]), [p_], [p_])
                for tg in range(TH // 512):
                    t0 = half * TH + tg * 512
                    pb = P.ps[it % 2]
                    it += 1
                    for kc in range(16):
                        P.mm(pb[0:ncols, :], pb, w[:, kc, wc:wc + ncols], w, hnT[:, kc, t0:t0 + 512], hnT, start=(kc == 0), stop=(kc == 15))
                    P.ev("act", lambda e, p_=p_, pb=pb, tg=tg, ncols=ncols: e.activation(p_[0:ncols, 1 + tg * 512:1 + (tg + 1) * 512], pb[0:ncols, :], AF.Copy), [pb], [p_])
                r_ = res[ir % 2]
                ir += 1
                P.ev("dve", lambda e, p_=p_, ncols=ncols: e.tensor_tensor(dT[0:ncols, :], p_[0:ncols, 0:TH], p_[0:ncols, 1:TH + 1], ALU.subtract), [p_], [dT])
                P.ev("dve", lambda e, p_=p_, r_=r_, ncols=ncols, mu_ap=mu_ap: e.scalar_tensor_tensor(r_[0:ncols, :], dT[0:ncols, :], mu_ap, p_[0:ncols, 1:TH + 1], ALU.mult, ALU.add),
                     [dT, p_, mucol], [r_])
                P.store(PT[cb * 128:cb * 128 + ncols, half * TH:(half + 1) * TH], r_[0:ncols, :], r_, dbuf=bPT)
    P.release(m0)

    m3 = P.mark()
    TG = 256
    NC_ = 4
    NTG = T // TG
    dup = P.sb([96, D], F32, "dup")
    iup = P.sb([96, D], F32, "iup")
    P.load(dup[:], dup, A["rw_decay_up"][j])
    P.load(iup[:], iup, A["rw_iclr_up"][j])
    vecs = P.sb([128, 5, 16], F32, "vecs")
    P.load(vecs[:], vecs, A["rw_vecs"][j])
    omka = P.sb([128, 16], F32, "omka")
    P.ev("dve", lambda e: e.tensor_scalar(omka[:], vecs[:, 3, :], -1.0, 1.0, ALU.mult, ALU.add), [vecs], [omka])
    gnw = P.sb([128, 16, 64], F32, "gnw")
    gnb = P.sb([128, 16, 64], F32, "gnb")
    for h in range(2):
        for (dst, nm) in ((gnw, "rw_gn_w"), (gnb, "rw_gn_b")):
            srcv = A[nm][j:j + 1, :].rearrange("o (hp h v) -> o hp h v", h=2, v=64)[:, :, h, :]
            P.load(dst[h * 64:(h + 1) * 64, :, :], dst, srcv.partition_broadcast(64))
    if has_vres:
        vup = P.sb([64, D], F32, "vup")
        P.load(vup[:], vup, A["rw_vres_up"][j - 1])
        v0c = P.sb([128, 16], F32, "v0c")
        P.load(v0c[:], v0c, A["rw_v0col"][j - 1])
    H = [P.sb([128, 64], F32, "H") for _ in range(16)]
    for h_ in H:
        P.ev("pool", lambda e, h_=h_: e.memset(h_[:], 0.0), [], [h_])
    m4, ml, sel, bones, bdm, onesc = C["c_m4"], C["c_ml"], C["c_sel"], C["c_bones"], C["c_bdm"], C["onesc"]
    mu4 = P.sb([128, NC_, 128], F32, "mu4")
    mui4 = P.sb([128, NC_, 128], F32, "mui4")
    ml4 = P.sb([128, NC_, 128], F32, "ml4")
    for n in range(NC_):
        P.ev("pool", lambda e, n=n: e.tensor_copy(mu4[:, n, :], m4[:, 0, :]), [m4], [mu4])
        P.ev("pool", lambda e, n=n: e.tensor_copy(mui4[:, n, :], m4[:, 1, :]), [m4], [mui4])
        P.ev("pool", lambda e, n=n: e.tensor_copy(ml4[:, n, :], ml[:]), [ml], [ml4])

    W = TG
    tdw = [P.sb([96, W], F32, "tdw") for _ in range(2)]
    tda = [P.sb([96, W], F32, "tda") for _ in range(2)]
    tpv = [P.sb([64, W], F32, "tpv") for _ in range(2)] if has_vres else None
    tmpn = ["lw", "a", "kk", "sqk", "nrm", "kkn", "tm", "k2", "bb", "dcs", "eneg", "eprev", "at", "rt", "bt", "kt", "rk"]

    def make_set():
        S = {}
        for k in ["rT", "kT", "vT", "gT"] + (["vf"] if has_vres else []):
            S[k] = P.sb([128, W], F32, k)
        for k in tmpn:
            S[k] = P.sb([128, W], F32, k)
        S["lwtm"] = P.sb([128, W // 128, 128], F32, "lwtm")
        S["epos"] = P.sb([128, W], F32, "epos")
        S["sg"] = P.sb([128, W], F32, "sg")
        S["bdAR"] = P.sb([128, NC_, 2, 2, 64], F32, "bdAR")
        for k in ["bdB", "bdK", "bdV", "bdRK", "bdW", "bdY"]:
            S[k] = P.sb([128, NC_, 2, 64], F32, k)
        S["Nst"] = [P.sb([128, NC_, 128], F32, "Nst") for _ in range(2)]
        S["Ast"] = [P.sb([128, NC_, 128], F32, "Ast") for _ in range(2)]
        for k in ["Nk", "Nrb", "Nrk", "bdWT", "bdBtm", "bdKtm"]:
            S[k] = P.sb([128, NC_, 128], F32, k)
        S["Vtm"] = P.sb([128, NC_, 64], F32, "Vtm")
        S["sbon"] = P.sb([128, NC_], F32, "sbon")
        S["Z"] = [P.sb([128, NC_, 128], F32, "Z") for _ in range(3)]
        S["U"] = [P.sb([128, 64], F32, "U") for _ in range(2)]
        S["Ht"] = P.sb([128, 64], F32, "Ht")
        S["Ysb"] = P.sb([128, NC_, 64], F32, "Ysb")
        S["Ysq"] = P.sb([128, NC_, 64], F32, "Ysq")
        S["st"] = P.sb([128, 4, NC_], F32, "st")
        S["ycm"] = P.sb([128, W], F32, "ycm")
        S["ogT"] = P.sb([128, W], BF16, "ogT")
        return S

    SETS = [make_set(), make_set()]
    PTv = C["PT"][0]
    bPTv = C["bPT"][0]
    EXPM05 = math.exp(-0.5)

    def v3(t):
        return t[:].rearrange("p (n t) -> p n t", n=NC_)

    def expand(out4, src_tt):
        in0 = v3(src_tt).unsqueeze(2).broadcast_to([128, NC_, 2, 64])
        in1 = bdm[:].unsqueeze(1).unsqueeze(3).broadcast_to([128, NC_, 2, 64])
        return lambda e: e.tensor_tensor(out4, in0, in1, ALU.mult)

    def ncols(n):
        return slice(n * 128, (n + 1) * 128)

    def unit(tg, hp, par, dwt, dat, pvt):
        S = SETS[par]
        c0 = tg * TG
        b0, b1, b2, b3 = [P.ps[4 * par + i] for i in range(4)]
        rT, kT, vT, gT = S["rT"], S["kT"], S["vT"], S["gT"]
        for sec, t_ in enumerate((rT, kT, vT, gT)):
            r0 = (sec * 16 + hp) * 128
            P.load(t_[:], t_, PT[r0:r0 + 128, c0:c0 + W], dbuf=bPT)
        hs = slice(hp * 128, (hp + 1) * 128)
        col = lambda i: vecs[:, i, hp:hp + 1]
        lw, a_, kk, sqk, nrm, kkn, tmq, k2, bb, dcs, eneg, eprev, at, rt, bt, kt, rk = [S[k] for k in tmpn]
        ep, sg_, lwtm = S["epos"], S["sg"], S["lwtm"]
        AR, nrb, nrk, vtm, sbn = S["bdAR"], S["Nrb"], S["Nrk"], S["Vtm"], S["sbon"]
        bdB, bdK, bdV, bdRK, bdW, bdY = S["bdB"], S["bdK"], S["bdV"], S["bdRK"], S["bdW"], S["bdY"]
        Nst, Ast, Nk, Z, U = S["Nst"], S["Ast"], S["Nk"], S["Z"], S["U"]
        wT, btm, ktm = S["bdWT"], S["bdBtm"], S["bdKtm"]
        Ht, Ysb, Ysq, st, ycm, og_ = S["Ht"], S["Ysb"], S["Ysq"], S["st"], S["ycm"], S["ogT"]
        P.mm(b0[:, 0:W], b0, dup[0:96, hs], dup, dwt[0:96, :], dwt)
        P.mm(b1[:, 0:W], b1, iup[0:96, hs], iup, dat[0:96, :], dat)
        P.ev("act", lambda e, b=col(0): e.activation(lw[:], b0[:, 0:W], AF.Sigmoid, bias=b), [b0, vecs], [lw])
        P.ev("dve", lambda e: e.tensor_scalar(lw[:], lw[:], -EXPM05, None, ALU.mult), [lw], [lw])
        P.ev("act", lambda e, b=col(1): e.activation(a_[:], b1[:, 0:W], AF.Sigmoid, bias=b), [b1, vecs], [a_])
        P.ev("dve", lambda e, s_=col(2): e.tensor_scalar(kk[:], kT[:], s_, None, ALU.mult), [kT, vecs], [kk])
        P.ev("act", lambda e: e.activation(sqk[:], kk[:], AF.Square), [kk], [sqk])
        yield
        P.mm(b2[:, 0:W], b2, bones[:], bones, sqk[:], sqk)
        if has_vres:
            vf = S["vf"]
            r0 = (32 + hp) * 128
            P.load(vf[:], vf, PTv[r0:r0 + 128, c0:c0 + W], dbuf=bPTv)
            P.mm(b0[:, 256:256 + W], b0, vup[0:64, hs], vup, pvt[0:64, :], pvt)
        for i in range(W // 128):
            P.tr(b1[:, 256 + i * 128:256 + (i + 1) * 128], b1, lw[:, i * 128:(i + 1) * 128], lw, ident)
        P.ev("act", lambda e: e.activation(nrm[:], b2[:, 0:W], AF.Sqrt), [b2], [nrm])
        P.ev("dve", lambda e: e.tensor_scalar(nrm[:], nrm[:], 1e-12, None, ALU.max), [nrm], [nrm])
        P.ev("dve", lambda e: e.reciprocal(nrm[:], nrm[:]), [nrm], [nrm])
        P.ev("pool", lambda e: e.tensor_tensor(kkn[:], kk[:], nrm[:], ALU.mult), [kk, nrm], [kkn])
        P.ev("dve", lambda e, s1=col(3), s2=omka[:, hp:hp + 1]: e.tensor_scalar(tmq[:], a_[:], s1, s2, ALU.mult, ALU.add), [a_, vecs, omka], [tmq])
        P.ev("pool", lambda e: e.tensor_tensor(k2[:], kT[:], tmq[:], ALU.mult), [kT, tmq], [k2])
        P.ev("dve", lambda e: e.tensor_tensor(bb[:], kkn[:], a_[:], ALU.mult), [kkn, a_], [bb])
        if has_vres:
            P.ev("act", lambda e, b=v0c[:, hp:hp + 1]: e.activation(tmq[:], b0[:, 256:256 + W], AF.Sigmoid, bias=b), [b0, v0c], [tmq])
            P.ev("pool", lambda e: e.tensor_tensor(vf[:], vf[:], vT[:], ALU.subtract), [vf, vT], [vf])
            P.ev("dve", lambda e: e.tensor_tensor(vf[:], vf[:], tmq[:], ALU.mult), [vf, tmq], [vf])
            P.ev("pool", lambda e: e.tensor_tensor(vT[:], vT[:], vf[:], ALU.add), [vf, vT], [vT])
        P.ev("act", lambda e: e.activation(lwtm[:], b1[:, 256:256 + W].rearrange("p (a b) -> p a b", b=128), AF.Copy), [b1], [lwtm])
        yield
        for i in range(W // 128):
            P.mm(b2[:, 256 + i * 128:256 + (i + 1) * 128], b2, lwtm[:, i, :], lwtm, m4[:, 1, :], m4)
        cs = b2[:, 256:256 + W]
        P.ev("act", lambda e: e.activation(ep[:], cs, AF.Exp), [b2], [ep])
        P.ev("act", lambda e: e.activation(eneg[:], cs, AF.Exp, scale=-1.0), [b2], [eneg])
        P.ev("dve", lambda e: e.tensor_tensor(dcs[:], cs, lw[:], ALU.subtract), [b2, lw], [dcs])
        P.ev("act", lambda e: e.activation(eprev[:], dcs[:], AF.Exp), [dcs], [eprev])
        P.ev("act", lambda e: e.activation(sg_[:], gT[:], AF.Silu), [gT], [sg_])
        P.ev("dve", lambda e: e.scalar_tensor_tensor(at[:], kkn[:], -1.0, eprev[:], ALU.mult, ALU.mult), [kkn, eprev], [at])
        P.ev("pool", lambda e: e.tensor_tensor(rt[:], rT[:], ep[:], ALU.mult), [rT, ep], [rt])
        P.ev("dve", lambda e: e.tensor_tensor(bt[:], bb[:], eneg[:], ALU.mult), [bb, eneg], [bt])
        P.ev("pool", lambda e: e.tensor_tensor(kt[:], k2[:], eneg[:], ALU.mult), [k2, eneg], [kt])
        P.ev("dve", lambda e, s_=col(4): e.scalar_tensor_tensor(rk[:], rT[:], s_, k2[:], ALU.mult, ALU.mult), [rT, k2, vecs], [rk])
        P.ev("dve", expand(AR[:, :, 0, :, :], at), [at, bdm], [AR])
        P.ev("pool", expand(AR[:, :, 1, :, :], rt), [rt, bdm], [AR])
        P.ev("dve", expand(bdB[:], bt), [bt, bdm], [bdB])
        P.ev("pool", expand(bdK[:], kt), [kt, bdm], [bdK])
        P.ev("dve", expand(bdV[:], vT), [vT, bdm], [bdV])
        P.ev("pool", expand(bdRK[:], rk), [rk, bdm], [bdRK])
        yield
        f2 = lambda t, n: t[:, n, :, :].rearrange("p a b -> p (a b)")
        bA = lambda n: AR[:, n, 0, :, :].rearrange("p a b -> p (a b)")
        bR = lambda n: AR[:, n, 1, :, :].rearrange("p a b -> p (a b)")
        v4 = lambda q: q[:].rearrange("p (n t) -> p n t", n=NC_)
        N0, A0 = Nst[0], Ast[0]
        for n in range(NC_):
            P.mm(b3[:, ncols(n)], b3, f2(bdB, n), bdB, bA(n), AR)
        P.ev("dve", lambda e: e.tensor_tensor(N0[:], v4(b3), mu4[:], ALU.mult), [b3, mu4], [N0])
        for n in range(NC_):
            P.mm(b0[:, ncols(n)], b0, bA(n), AR, f2(bdB, n), bdB)
        P.ev("dve", lambda e: e.tensor_tensor(A0[:], v4(b0), ml4[:], ALU.mult), [b0, ml4], [A0])
        for n in range(NC_):
            P.mm(b1[:, ncols(n)], b1, f2(bdK, n), bdK, bA(n), AR)
        P.ev("dve", lambda e: e.tensor_tensor(Nk[:], v4(b1), mu4[:], ALU.mult), [b1, mu4], [Nk])
        yield
        for n in range(NC_):
            P.mm(b2[:, ncols(n)], b2, f2(bdB, n), bdB, bR(n), AR)
        P.ev("dve", lambda e: e.tensor_tensor(nrb[:], v4(b2), mui4[:], ALU.mult), [b2, mui4], [nrb])
        for n in range(NC_):
            P.mm(b3[:, ncols(n)], b3, f2(bdK, n), bdK, bR(n), AR)
        P.ev("dve", lambda e: e.tensor_tensor(nrk[:], v4(b3), mui4[:], ALU.mult), [b3, mui4], [nrk])
        Zc = Z[0]
        for n in range(NC_):
            P.mm(b0[:, n * 64:(n + 1) * 64], b0, f2(bdV, n), bdV, sel[:], sel)
        for n in range(NC_):
            P.mm(b0[:, 256 + n * 64:256 + (n + 1) * 64], b0, bA(n), AR, sel[:], sel)
        P.ev("act", lambda e: e.activation(vtm[:], b0[:, 0:256].rearrange("p (n v) -> p n v", n=NC_), AF.Copy), [b0], [vtm])
        P.ev("act", lambda e, Zc=Zc: e.activation(Zc[:, :, 0:64], b0[:, 256:512].rearrange("p (n v) -> p n v", n=NC_), AF.Copy), [b0], [Zc])
        for n in range(NC_):
            P.mm(b1[:, n:n + 1], b1, f2(bdRK, n), bdRK, onesc[:], onesc)
        P.ev("act", lambda e: e.activation(sbn[:], b1[:, 0:NC_], AF.Copy), [b1], [sbn])
        yield
        for n in range(NC_):
            P.mm(b2[:, n * 64:(n + 1) * 64], b2, Nk[:, n, :], Nk, vtm[:, n, :], vtm)
        P.ev("act", lambda e, Zc=Zc: e.activation(Zc[:, :, 64:128], b2[:, 0:256].rearrange("p (n v) -> p n v", n=NC_), AF.Copy), [b2], [Zc])
        yield
        Ncur, Acur = N0, A0
        zi = 0
        for k in range(6):
            qa = b0 if k % 2 == 0 else b1
            Zn = Z[(zi + 1) % 3]
            for n in range(NC_):
                P.mm(qa[:, ncols(n)], qa, Ncur[:, n, :], Ncur, Zc[:, n, :], Zc)
            P.ev("dve", lambda e, Zn=Zn, qa=qa, Zc=Zc: e.tensor_tensor(Zn[:], v4(qa), Zc[:], ALU.add), [qa, Zc], [Zn])
            Zc = Zn
            zi += 1
            if k < 5:
                Nn = Nst[(k + 1) % 2]
                for n in range(NC_):
                    P.mm(b2[:, ncols(n)], b2, Acur[:, n, :], Acur, Ncur[:, n, :], Ncur)
                if k < 4:
                    An = Ast[(k + 1) % 2]
                    for n in range(NC_):
                        P.mm(b3[:, ncols(n)], b3, Ncur[:, n, :], Ncur, Acur[:, n, :], Acur)
                P.ev("act", lambda e, Nn=Nn: e.activation(Nn[:], v4(b2), AF.Copy), [b2], [Nn])
                if k < 4:
                    P.ev("act", lambda e, An=An: e.activation(An[:], v4(b3), AF.Copy), [b3], [An])
                    Acur = An
                Ncur = Nn
            yield
        Zf = Zc
        P.ev("dve", lambda e: e.tensor_tensor(bdW[:], Zf[:, :, 0:64].unsqueeze(2).broadcast_to([128, NC_, 2, 64]),
                                              bdm[:].unsqueeze(1).unsqueeze(3).broadcast_to([128, NC_, 2, 64]), ALU.mult), [Zf, bdm], [bdW])
        for n in range(NC_):
            P.tr(b1[:, ncols(n)], b1, f2(bdB, n), bdB, ident)
        P.ev("act", lambda e: e.activation(btm[:], v4(b1), AF.Copy), [b1], [btm])
        for n in range(NC_):
            P.tr(b2[:, ncols(n)], b2, f2(bdK, n), bdK, ident)
        P.ev("act", lambda e: e.activation(ktm[:], v4(b2), AF.Copy), [b2], [ktm])
        yield
        for n in range(NC_):
            P.tr(b0[:, ncols(n)], b0, f2(bdW, n), bdW, ident)
        P.ev("act", lambda e: e.activation(wT[:], v4(b0), AF.Copy), [b0], [wT])
        yield
        Hh = H[hp]
        qy = b3
        for n in range(NC_):
            qu = b0 if n % 2 == 0 else b1
            U_ = U[n % 2]
            P.mm(qu[:, 0:64], qu, wT[:, n, :], wT, Hh[:], Hh)
            ys = qy[:, n * 64:(n + 1) * 64]
            P.mm(ys, qy, bR(n), AR, Hh[:], Hh, start=True, stop=False)
            P.ev("dve", lambda e, U_=U_, qu=qu, n=n: e.tensor_tensor(U_[:], qu[:, 0:64], Zf[:, n, 64:128], ALU.add), [qu, Zf], [U_])
            yield
            P.mm(ys, qy, nrb[:, n, :], nrb, U_[:], U_, start=False, stop=False)
            P.mm(ys, qy, nrk[:, n, :], nrk, vtm[:, n, :], vtm, start=False, stop=True)
            P.mm(qu[:, 64:128], qu, btm[:, n, :], btm, U_[:], U_, start=True, stop=False)
            P.mm(qu[:, 64:128], qu, ktm[:, n, :], ktm, vtm[:, n, :], vtm, start=False, stop=True)
            P.ev("dve", lambda e, qu=qu: e.tensor_tensor(Ht[:], qu[:, 64:128], Hh[:], ALU.add), [qu, Hh], [Ht])
            pc = ep[:, n * 64 + 63:n * 64 + 64]
            P.ev("dve", lambda e, pc=pc: e.tensor_scalar(Hh[:], Ht[:], pc, None, ALU.mult), [Ht, ep], [Hh])
            yield
        yv = qy[:, 0:NC_ * 64].rearrange("p (n v) -> p n v", n=NC_)
        P.ev("act", lambda e: e.activation(Ysb[:], yv, AF.Copy), [qy], [Ysb])
        P.ev("act", lambda e: e.activation(Ysq[:], yv, AF.Square), [qy], [Ysq])
        P.ev("dve", lambda e: e.reduce_sum(st[:, 0, :], Ysb[:], AX.X), [Ysb], [st])
        P.ev("dve", lambda e: e.reduce_sum(st[:, 1, :], Ysq[:], AX.X), [Ysq], [st])
        P.ev("dve", lambda e: e.tensor_scalar(st[:, 0, :], st[:, 0, :], 1.0 / 64, None, ALU.mult), [st], [st])
        P.ev("dve", lambda e: e.tensor_tensor(st[:, 2, :], st[:, 0, :], st[:, 0, :], ALU.mult), [st], [st])
        P.ev("dve", lambda e: e.scalar_tensor_tensor(st[:, 1, :], st[:, 1, :], 1.0 / 64, st[:, 2, :], ALU.mult, ALU.subtract), [st], [st])
        P.ev("act", lambda e: e.activation(st[:, 1, :], st[:, 1, :], AF.Sqrt, bias=GN_EPS, scale=1.0), [st], [st])
        P.ev("dve", lambda e: e.reciprocal(st[:, 1, :], st[:, 1, :]), [st], [st])
        yield
        P.ev("dve", lambda e: e.tensor_tensor(Ysb[:], Ysb[:], st[:, 0, :].unsqueeze(2).broadcast_to([128, NC_, 64]), ALU.subtract), [Ysb, st], [Ysb])
        P.ev("dve", lambda e: e.tensor_tensor(Ysb[:], Ysb[:], st[:, 1, :].unsqueeze(2).broadcast_to([128, NC_, 64]), ALU.mult), [Ysb, st], [Ysb])
        P.ev("pool", lambda e: e.tensor_tensor(Ysb[:], Ysb[:], gnw[:, hp, :].unsqueeze(1).broadcast_to([128, NC_, 64]), ALU.mult), [Ysb, gnw], [Ysb])
        P.ev("pool", lambda e: e.tensor_tensor(Ysb[:], Ysb[:], gnb[:, hp, :].unsqueeze(1).broadcast_to([128, NC_, 64]), ALU.add), [Ysb, gnb], [Ysb])
        P.ev("dve", lambda e: e.tensor_tensor(Ysq[:], vtm[:], sbn[:].unsqueeze(2).broadcast_to([128, NC_, 64]), ALU.mult), [vtm, sbn], [Ysq])
        P.ev("dve", lambda e: e.tensor_tensor(Ysb[:], Ysb[:], Ysq[:], ALU.add), [Ysb, Ysq], [Ysb])
        P.ev("pool", lambda e: e.tensor_tensor(bdY[:], Ysb[:].unsqueeze(2).broadcast_to([128, NC_, 2, 64]),
                                               bdm[:].unsqueeze(1).unsqueeze(3).broadcast_to([128, NC_, 2, 64]), ALU.mult), [Ysb, bdm], [bdY])
        yield
        for n in range(NC_):
            P.tr(b1[:, ncols(n)], b1, f2(bdY, n), bdY, ident)
        q3v = b1[:].rearrange("p (n h t) -> p n h t", n=NC_, h=2)
        P.ev("dve", lambda e: e.tensor_copy(v3(ycm), q3v[:, :, 0, :]), [b1], [ycm])
        P.ev("dve", lambda e: e.tensor_tensor(v3(ycm), v3(ycm), q3v[:, :, 1, :], ALU.add), [b1, ycm], [ycm])
        P.ev("pool", lambda e: e.tensor_tensor(og_[:], ycm[:], sg_[:], ALU.mult), [ycm, sg_], [og_])
        P.store(OGT[hp * 128:(hp + 1) * 128, c0:c0 + W], og_[:], og_, dbuf=bOGT)

    work = []
    for tg in range(NTG):
        for hp in range(16):
            work.append((tg, hp))
    tg_loaded = {}

    def tg_inputs(tg):
        if tg not in tg_loaded:
            c0 = tg * TG
            dwt, dat = tdw[tg % 2], tda[tg % 2]
            P.load(dwt[:], dwt, PT[64 * 128:64 * 128 + 96, c0:c0 + W], dbuf=bPT)
            P.load(dat[:], dat, PT[65 * 128:65 * 128 + 96, c0:c0 + W], dbuf=bPT)
            P.ev("act", lambda e, dwt=dwt: e.activation(dwt[:], dwt[:], AF.Tanh), [dwt], [dwt])
            pvt = None
            if has_vres:
                pvt = tpv[tg % 2]
                P.load(pvt[:], pvt, PT[66 * 128:66 * 128 + 64, c0:c0 + W], dbuf=bPT)
            tg_loaded[tg] = (dwt, dat, pvt)
        return tg_loaded[tg]

    nxt = 0
    active = [None, None]

    def start(par):
        nonlocal nxt
        if nxt >= len(work):
            return None
        tg, hp = work[nxt]
        nxt += 1
        dwt, dat, pvt = tg_inputs(tg)
        return unit(tg, hp, par, dwt, dat, pvt)

    active[0] = start(0)
    steps = 0
    while active[0] is not None or active[1] is not None:
        for par in range(2):
            g = active[par]
            if g is None:
                if par == 1 and steps == 8:
                    active[1] = start(1)
                continue
            try:
                next(g)
            except StopIteration:
                active[par] = start(par)
        steps += 1
        if steps > 8 and active[1] is None and nxt < len(work):
            active[1] = start(1)
    P.release(m3)
    stage_outproj(P, OGT, bOGT, A["rw_w_out"][j], x_d, x_buf, xo_d, xo_buf)


def host_consts(T):
    c = {}
    c["c_ident"] = np.eye(128, dtype=np.float32)
    import ml_dtypes
    k = np.arange(128)[:, None]
    q = np.arange(128)[None, :]
    c["c_mask"] = (k <= q).astype(np.float32)
    inv = (10000.0 ** (-np.arange(0, 128, 2, dtype=np.float32) / np.float32(128))).astype(np.float32)
    c["c_inv"] = np.ascontiguousarray(np.broadcast_to(inv[None, :], (128, 64))).astype(np.float32)
    hh = np.arange(128) // 64
    tt = np.arange(128) % 64
    same = (hh[:, None] == hh[None, :])
    mu_ = (same & (tt[:, None] < tt[None, :])).astype(np.float32)
    mui = (same & (tt[:, None] <= tt[None, :])).astype(np.float32)
    ml_ = (same & (tt[:, None] > tt[None, :])).astype(np.float32)
    c["c_m4"] = np.ascontiguousarray(np.stack([mu_, mui, mu_, mui], axis=1))
    c["c_ml"] = ml_
    c["c_sel"] = np.concatenate([np.eye(64), np.eye(64)], axis=0).astype(np.float32)
    c["c_bones"] = same.astype(np.float32)
    c["c_bdm"] = (hh[:, None] == np.arange(2)[None, :]).astype(np.float32)
    return c


def core_inputs(inputs, b, T):
    m = {}
    m["x"] = np.ascontiguousarray(inputs["x"][b, :T])
    pos = np.asarray(inputs["positions"][b, :T]).astype(np.int32)
    m["pos"] = np.ascontiguousarray(pos.reshape(T // 128, 128).T)
    for nm in ATTN_W + RWKV_W:
        m[nm] = np.ascontiguousarray(np.asarray(inputs[nm]))
    NR = inputs["rw_mu"].shape[0]
    mu = np.asarray(inputs["rw_mu"])
    mucol = np.zeros((NR, 128, 66), np.float32)
    mucol[:, :, :64] = mu[:, :8192].reshape(NR, 64, 128).transpose(0, 2, 1)
    mucol[:, :96, 64] = mu[:, 8192:8288]
    mucol[:, :96, 65] = mu[:, 8288:8384]
    m["rw_mucol"] = mucol
    vecs = np.stack([np.asarray(inputs[n]).reshape(NR, 16, 128).transpose(0, 2, 1)
                     for n in ["rw_w0", "rw_a0", "rw_k_k", "rw_k_a", "rw_r_k"]], axis=2)
    m["rw_vecs"] = np.ascontiguousarray(vecs.astype(np.float32))
    NV = inputs["rw_v0"].shape[0]
    m["rw_v0col"] = np.ascontiguousarray(np.asarray(inputs["rw_v0"]).reshape(NV, 16, 128).transpose(0, 2, 1))
    vm = np.zeros((NV, 128, 1), np.float32)
    vm[:, :64, 0] = np.asarray(inputs["rw_vres_mu"])
    m["rw_vmucol"] = vm
    m.update(host_consts(T))
    return m


def kernel(**inputs):
    T = inputs["x"].shape[1]
    B = inputs["x"].shape[0]
    maps = [core_inputs(inputs, c % B, T) for c in range(NCORES)]
    shapes = {k: (list(v.shape), I32 if v.dtype == np.int32 else F32) for k, v in maps[0].items()}
    P = build_program(T, [0, 1, 2, 3], shapes)
    res = run_bass_kernel_spmd(P.nc, maps, core_ids=list(range(NCORES)))
    out = np.stack([res.results[b]["out"] for b in range(B)], axis=0)
    return out.astype(np.float32)
```

```python
import math
import numpy as np
import concourse.bass as bass
import concourse.mybir as mybir
from concourse.bass_utils import run_bass_kernel_spmd

F32 = mybir.dt.float32
BF16 = mybir.dt.bfloat16
I32 = mybir.dt.int32
AF = mybir.ActivationFunctionType
ALU = mybir.AluOpType
AX = mybir.AxisListType

SEM_EPOCH = 30000
D = 2048
NORM_EPS = 1e-6
GN_EPS = 64e-5
NCORES = 8


class Buf:
    __slots__ = ("name", "writer", "readers")

    def __init__(self, name=""):
        self.name = name
        self.writer = None
        self.readers = {}


class KB:
    ENGS = ("pe", "act", "dve", "pool", "sp")

    def __init__(self, nc, n_dma_sems=40):
        self.nc = nc
        self.ops = {e: [] for e in self.ENGS}
        self.cnt = {e: 0 for e in self.ENGS}
        self.sems = {}
        self.seen = {e: {} for e in self.ENGS}
        self.n_dma_sems = n_dma_sems
        self.dma_next = 0
        self.dma_cum = [0] * n_dma_sems
        self._stack = []
        self.ninst = 0

    def _sem(self, key):
        if key not in self.sems:
            cm = self.nc.semaphore("s_%s_%s" % key)
            h = cm.__enter__()
            self._stack.append(cm)
            self.sems[key] = h
        return self.sems[key]

    def _wait(self, eng, dep):
        key, val = dep
        if self.seen[eng].get(key, 0) >= val:
            return
        self.seen[eng][key] = val
        h = self._sem(key)
        self.ops[eng].append(lambda e, h=h, val=val: e.wait_ge(h, val))

    def _deps(self, eng, reads, writes):
        deps = []
        for r in reads:
            if r.writer is not None:
                deps.append(r.writer)
        for w in writes:
            if w.writer is not None and w.writer[0][0] != eng:
                deps.append(w.writer)
            for k, v in w.readers.items():
                if k[0] != eng:
                    deps.append((k, v))
        return deps

    def op(self, eng, fn, reads=(), writes=()):
        for dep in self._deps(eng, reads, writes):
            if eng == "pe" and dep[0][0] == "pe":
                continue
            self._wait(eng, dep)
        self.cnt[eng] += 1
        n = self.cnt[eng]
        key = (eng, (n - 1) // SEM_EPOCH)
        val = (n - 1) % SEM_EPOCH + 1
        h = self._sem(key)
        self.ops[eng].append(lambda e, fn=fn, h=h: fn(e).then_inc(h, 1))
        tok = (key, val)
        for r in reads:
            r.readers[key] = val
        for w in writes:
            w.writer = tok
            w.readers = {}
        self.ninst += 1
        return tok

    def dma(self, q, out, in_, reads=(), writes=(), **kw):
        j = self.dma_next
        self.dma_next = (j + 1) % self.n_dma_sems
        key = ("dma", j)
        if self.dma_cum[j] > 0:
            self._wait(q, (key, self.dma_cum[j]))
        for dep in self._deps(q, reads, writes):
            self._wait(q, dep)
        self.dma_cum[j] += 16
        val = self.dma_cum[j]
        h = self._sem(key)
        self.ops[q].append(lambda e, h=h, out=out, in_=in_, kw=kw:
                           e.dma_start(out=out, in_=in_, **kw).then_inc(h, 16))
        tok = (key, val)
        for r in reads:
            r.readers[key] = val
        for w in writes:
            w.writer = tok
            w.readers = {}
        self.ninst += 1
        return tok

    def collective(self, kind, groups, in_ap, out_ap, reads=(), writes=()):
        q = "pool"
        j = self.dma_next
        self.dma_next = (j + 1) % self.n_dma_sems
        key = ("dma", j)
        if self.dma_cum[j] > 0:
            self._wait(q, (key, self.dma_cum[j]))
        for dep in self._deps(q, reads, writes):
            self._wait(q, dep)
        self.dma_cum[j] += 16
        val = self.dma_cum[j]
        h = self._sem(key)
        self.ops[q].append(lambda e, h=h: e.collective_compute(kind, ALU.bypass, replica_groups=groups,
                                                               ins=[in_ap], outs=[out_ap]).then_inc(h, 16))
        tok = (key, val)
        for r in reads:
            r.readers[key] = val
        for w in writes:
            w.writer = tok
            w.readers = {}
        return tok

    def barrier(self):
        toks = []
        for f in ("pe", "act", "dve", "pool"):
            n = self.cnt[f]
            if n > 0:
                toks.append(((f, (n - 1) // SEM_EPOCH), (n - 1) % SEM_EPOCH + 1))
        for j in range(self.n_dma_sems):
            if self.dma_cum[j] > 0:
                toks.append((("dma", j), self.dma_cum[j]))
        for e in self.ENGS:
            for t in toks:
                if t[0][0] == e:
                    continue
                self._wait(e, t)

    def finish(self):
        for j in range(self.n_dma_sems):
            if self.dma_cum[j] > 0:
                self._wait("sp", (("dma", j), self.dma_cum[j]))

    def emit(self):
        nc = self.nc
        with nc.Block() as block:
            @block.tensor
            def _(e):
                for f in self.ops["pe"]:
                    f(e)

            @block.scalar
            def _(e):
                for f in self.ops["act"]:
                    f(e)

            @block.vector
            def _(e):
                for f in self.ops["dve"]:
                    f(e)

            @block.gpsimd
            def _(e):
                for f in self.ops["pool"]:
                    f(e)

            @block.sync
            def _(e):
                for f in self.ops["sp"]:
                    f(e)
        for cm in reversed(self._stack):
            cm.__exit__(None, None, None)


class TT:
    __slots__ = ("t", "b")

    def __init__(self, t, name=""):
        self.t = t
        self.b = Buf(name)

    def __getitem__(self, k):
        return self.t[k]


_DT_BYTES = {F32: 4, BF16: 2, I32: 4}


class Prog:
    def __init__(self, T):
        self.T = T
        self.nc = bass.Bass("TRN2", target_bir_lowering=False)
        self.kb = KB(self.nc)
        self.off = 16640
        self.uid = 0
        self.ps = [TT(self.nc.alloc_psum_tensor("psb%d" % i, [128, 512], F32), "ps%d" % i) for i in range(8)]
        self.rr = 0
        self.SB_LIMIT = 229376

    def sb(self, shape, dt=F32, name="t"):
        n = 1
        for s in shape[1:]:
            n *= s
        nbytes = n * _DT_BYTES[dt]
        nbytes = (nbytes + 63) // 64 * 64
        self.uid += 1
        assert self.off + nbytes <= self.SB_LIMIT, ("SBUF overflow", name, self.off, nbytes)
        t = self.nc.alloc_sbuf_tensor_at("%s_%d" % (name, self.uid), list(shape), dt, offset=self.off)
        self.off += nbytes
        return TT(t, name)

    def mark(self):
        return self.off

    def release(self, m):
        self.kb.barrier()
        self.off = m

    def din(self, name, shape, dt=F32):
        return self.nc.dram_tensor(name, list(shape), dt, kind="ExternalInput").ap()

    def dout(self, name, shape, dt=F32):
        return self.nc.dram_tensor(name, list(shape), dt, kind="ExternalOutput").ap()

    def dscratch(self, name, shape, dt=F32):
        return self.nc.dram_tensor(name, list(shape), dt, kind="Internal").ap()

    def mm(self, out, o_tt, lhsT, l_tt, rhs, r_tt, start=True, stop=True):
        self.kb.op("pe", lambda e: e.matmul(out, lhsT, rhs, start=start, stop=stop),
                   reads=[l_tt.b, r_tt.b], writes=[o_tt.b])

    def tr(self, out, o_tt, in_, i_tt, ident):
        self.kb.op("pe", lambda e: e.transpose(out, in_, ident[:]), reads=[i_tt.b, ident.b], writes=[o_tt.b])

    def ev(self, eng, fn, reads, writes):
        self.kb.op(eng, fn, reads=[r.b for r in reads], writes=[w.b for w in writes])

    def alt(self):
        self.rr ^= 1
        return "act" if self.rr else "dve"

    def copy(self, eng, out, o_tt, in_, i_tt):
        if eng == "act":
            self.ev("act", lambda e: e.activation(out, in_, AF.Copy), [i_tt], [o_tt])
        else:
            self.ev(eng, lambda e: e.tensor_copy(out, in_), [i_tt], [o_tt])

    def load(self, out, o_tt, in_, q="sp", dbuf=None):
        self.kb.dma(q, out, in_, reads=[dbuf] if dbuf is not None else [], writes=[o_tt.b])

    def store(self, out, in_, i_tt, q="sp", dbuf=None):
        self.kb.dma(q, out, in_, reads=[i_tt.b], writes=[dbuf] if dbuf is not None else [])


def stage_norm_T(P, x_d, x_buf, normw_row_d, hnT, ident):
    T = P.T
    m = P.mark()
    nw = P.sb([128, D], F32, "nw")
    P.load(nw[:], nw, normw_row_d.partition_broadcast(128))
    xb = [P.sb([128, D], F32, "xb") for _ in range(2)]
    xs = [P.sb([128, D], F32, "xs") for _ in range(2)]
    junk = P.sb([128, D], BF16, "junk")
    ss = [P.sb([128, 1], F32, "ss") for _ in range(2)]
    for tb in range(T // 128):
        xt = xb[tb % 2]
        st = ss[tb % 2]
        xst = xs[tb % 2]
        P.load(xt[:], xt, x_d[tb * 128:(tb + 1) * 128, :], dbuf=x_buf)
        P.ev("act", lambda e, xt=xt, st=st: e.activation(junk[:], xt[:], AF.Square, accum_out=st[:]), [xt], [junk, st])
        P.ev("act", lambda e, st=st: e.activation(st[:], st[:], AF.Sqrt, bias=NORM_EPS, scale=1.0 / D), [st], [st])
        P.ev("dve", lambda e, st=st: e.reciprocal(st[:], st[:]), [st], [st])
        P.ev("dve", lambda e, xt=xt, st=st, xst=xst: e.scalar_tensor_tensor(xst[:], xt[:], st[:, 0:1], nw[:], ALU.mult, ALU.mult),
             [xt, st, nw], [xst])
        for g in range(4):
            pb = P.ps[g % 2]
            for j in range(4):
                c = g * 4 + j
                P.tr(pb[:, j * 128:(j + 1) * 128], pb, xst[:, c * 128:(c + 1) * 128], xst, ident)
            eng = P.alt()
            P.copy(eng, hnT[:, g * 4:(g + 1) * 4, tb * 128:(tb + 1) * 128],
                   hnT, pb[:].rearrange("p (a b) -> p a b", a=4), pb)
    P.release(m)


def stage_outproj(P, ogT_d, ogT_buf, wout_d, xin_d, xin_buf, xout_d, xout_buf):
    T = P.T
    m = P.mark()
    wo = P.sb([128, 16, D], BF16, "wo")
    wv = wout_d.rearrange("(kc p) n -> p kc n", p=128)
    for i in range(4):
        P.load(wo[:, i * 4:(i + 1) * 4, :], wo, wv[:, i * 4:(i + 1) * 4, :], q="pool")
    ogb = [P.sb([128, 16, 128], BF16, "ogb") for _ in range(2)]
    xb = [P.sb([128, D], F32, "xb") for _ in range(2)]
    ob = [P.sb([128, D], F32, "ob") for _ in range(2)]
    ogv = ogT_d.rearrange("(kc p) t -> p kc t", p=128)
    for tb in range(T // 128):
        og = ogb[tb % 2]
        xt = xb[tb % 2]
        ot = ob[tb % 2]
        P.load(og[:], og, ogv[:, :, tb * 128:(tb + 1) * 128], dbuf=ogT_buf)
        P.load(xt[:], xt, xin_d[tb * 128:(tb + 1) * 128, :], dbuf=xin_buf)
        for cg in range(4):
            pb = P.ps[(tb * 4 + cg) % 4]
            for kc in range(16):
                P.mm(pb[:], pb, og[:, kc, :], og, wo[:, kc, cg * 512:(cg + 1) * 512], wo, start=(kc == 0), stop=(kc == 15))
            P.ev("dve", lambda e, ot=ot, pb=pb, xt=xt, cg=cg: e.tensor_tensor(ot[:, cg * 512:(cg + 1) * 512], pb[:],
                                                                              xt[:, cg * 512:(cg + 1) * 512], ALU.add), [pb, xt], [ot])
        P.store(xout_d[tb * 128:(tb + 1) * 128, :], ot[:], ot, q="sp", dbuf=xout_buf)
    P.release(m)


def attn_layer(P, li, j, x_d, x_buf, xo_d, xo_buf, A, C):
    T = P.T
    NT = T // 128
    lambda_init = 0.8 - 0.6 * math.exp(-0.3 * li)
    ident = C["ident"]
    QT, KT, V, G, OGT = C["QT"], C["KT"], C["V"], C["G"], C["OGT"]
    bQT, bKT, bV, bG, bOGT = C["bQT"], C["bKT"], C["bV"], C["bG"], C["bOGT"]

    m0 = P.mark()
    setup_rope(P, A, C)
    hnT = P.sb([128, 16, T], BF16, "hnT")
    stage_norm_T(P, x_d, x_buf, A["norm_w"][li:li + 1, :], hnT, ident)

    m1 = P.mark()
    cos, sin = C["cos"], C["sin"]
    gq = P.sb([128, 128], F32, "gq")
    gk = P.sb([128, 128], F32, "gk")
    P.load(gq[:], gq, A["da_q_gain"][j:j + 1, :].partition_broadcast(128))
    P.load(gk[:], gk, A["da_k_gain"][j:j + 1, :].partition_broadcast(128))
    wb = [P.sb([128, 16, 512], BF16, "wb") for _ in range(2)]
    ND = 3
    pending = []
    sqs = [P.sb([128, 512], F32, "sq") for _ in range(2)]
    s4 = [P.sb([128, 4], F32, "s4") for _ in range(ND)]
    qn = [P.sb([128, 4, 128], F32, "qn") for _ in range(ND)]
    qr = [P.sb([128, 4, 128], F32, "qr") for _ in range(ND)]
    tmp = [P.sb([128, 4, 64], F32, "tmp") for _ in range(ND)]
    qT = [P.sb([128, 4, 128], BF16, "qT") for _ in range(ND)]
    wv = A["da_w_in"][j].rearrange("(kc p) n -> p kc n", p=128)
    it = 0

    def load_w(cg):
        w = wb[cg % 2]
        for h2 in range(2):
            P.load(w[:, h2 * 8:(h2 + 1) * 8, :], w, wv[:, h2 * 8:(h2 + 1) * 8, cg * 512:(cg + 1) * 512], q="pool")

    load_w(0)
    for cg in range(16):
        w = wb[cg % 2]
        if cg + 1 < 16:
            load_w(cg + 1)
        for tb in range(NT):
            pb = P.ps[it % 4]
            for kc in range(16):
                P.mm(pb[:], pb, hnT[:, kc, tb * 128:(tb + 1) * 128], hnT, w[:, kc, :], w, start=(kc == 0), stop=(kc == 15))
            while len(pending) > 2:
                pending.pop(0)()
            if cg < 8:
                gain = gq if cg < 4 else gk
                sq = sqs[it % 2]
                s4t, qnt, qrt, tmpt, qTt = s4[it % ND], qn[it % ND], qr[it % ND], tmp[it % ND], qT[it % ND]
                pv = pb[:].rearrange("p (a b) -> p a b", a=4)
                P.ev("act", lambda e, pb=pb, sq=sq: e.activation(sq[:], pb[:], AF.Square), [pb], [sq])
                P.ev("dve", lambda e, s4t=s4t, sq=sq: e.reduce_sum(s4t[:], sq[:].rearrange("p (a b) -> p a b", a=4), AX.X), [sq], [s4t])
                P.ev("act", lambda e, s4t=s4t: e.activation(s4t[:], s4t[:], AF.Sqrt, bias=NORM_EPS, scale=1.0 / 128), [s4t], [s4t])
                P.ev("dve", lambda e, s4t=s4t: e.reciprocal(s4t[:], s4t[:]), [s4t], [s4t])
                P.ev("dve", lambda e, qnt=qnt, pv=pv, s4t=s4t: e.tensor_tensor(qnt[:], pv, s4t[:].unsqueeze(2).broadcast_to([128, 4, 128]), ALU.mult),
                     [pb, s4t], [qnt])
                P.ev("pool", lambda e, qnt=qnt, gain=gain: e.tensor_tensor(qnt[:], qnt[:], gain[:].unsqueeze(1).broadcast_to([128, 4, 128]), ALU.mult),
                     [qnt, gain], [qnt])
                cb = cos[:, tb, :].unsqueeze(1).broadcast_to([128, 4, 64])
                sb_ = sin[:, tb, :].unsqueeze(1).broadcast_to([128, 4, 64])
                P.ev("dve", lambda e, qrt=qrt, qnt=qnt, cb=cb: e.tensor_tensor(qrt[:, :, 0:64], qnt[:, :, 0:64], cb, ALU.mult), [qnt, cos], [qrt])
                P.ev("pool", lambda e, tmpt=tmpt, qnt=qnt, sb_=sb_: e.tensor_tensor(tmpt[:], qnt[:, :, 64:128], sb_, ALU.mult), [qnt, sin], [tmpt])
                P.ev("dve", lambda e, qrt=qrt, tmpt=tmpt: e.tensor_tensor(qrt[:, :, 0:64], qrt[:, :, 0:64], tmpt[:], ALU.subtract), [qrt, tmpt], [qrt])
                P.ev("pool", lambda e, qrt=qrt, qnt=qnt, cb=cb: e.tensor_tensor(qrt[:, :, 64:128], qnt[:, :, 64:128], cb, ALU.mult), [qnt, cos], [qrt])
                P.ev("dve", lambda e, tmpt=tmpt, qnt=qnt, sb_=sb_: e.tensor_tensor(tmpt[:], qnt[:, :, 0:64], sb_, ALU.mult), [qnt, sin], [tmpt])
                P.ev("dve", lambda e, qrt=qrt, tmpt=tmpt: e.tensor_tensor(qrt[:, :, 64:128], qrt[:, :, 64:128], tmpt[:], ALU.add), [qrt, tmpt], [qrt])
                def fin(qrt=qrt, qTt=qTt, pt=P.ps[4 + it % 2], cg=cg, tb=tb):
                    for a in range(4):
                        P.tr(pt[:, a * 128:(a + 1) * 128], pt, qrt[:, a, :], qrt, ident)
                    P.ev("act", lambda e: e.activation(qTt[:], pt[:].rearrange("p (a b) -> p a b", a=4), AF.Copy), [pt], [qTt])
                    dst = QT if cg < 4 else KT
                    dbuf = bQT if cg < 4 else bKT
                    hc0 = (cg % 4) * 4
                    P.store(dst[hc0:hc0 + 4, :, tb * 128:(tb + 1) * 128].rearrange("a p t -> p a t"), qTt[:], qTt, dbuf=dbuf)
                pending.append(fin)
            elif cg < 12:
                vt = qT[it % ND]
                vflat = vt[:].rearrange("p a b -> p (a b)")
                P.copy(P.alt(), vflat, vt, pb[:], pb)
                c0 = (cg - 8) * 512
                P.store(V[tb * 128:(tb + 1) * 128, c0:c0 + 512], vflat, vt, dbuf=bV)
            else:
                gt = qn[it % ND]
                gflat = gt[:].rearrange("p a b -> p (a b)")
                P.ev("act", lambda e, gflat=gflat, pb=pb: e.activation(gflat, pb[:], AF.Silu), [pb], [gt])
                c0 = (cg - 12) * 512
                P.store(G[tb * 128:(tb + 1) * 128, c0:c0 + 512], gflat, gt, dbuf=bG)
            it += 1
        while pending:
            pending.pop(0)()
    P.release(m0)

    m2 = P.mark()
    lv = P.sb([1, 4, 128], F32, "lv")
    for i, nm in enumerate(["da_lam_q1", "da_lam_k1", "da_lam_q2", "da_lam_k2"]):
        P.load(lv[:, i, :], lv, A[nm][j:j + 1, :])
    lp = P.sb([1, 2, 128], F32, "lp")
    l2 = P.sb([1, 2], F32, "l2")
    P.ev("dve", lambda e: e.tensor_tensor(lp[:, 0, :], lv[:, 0, :], lv[:, 1, :], ALU.mult), [lv], [lp])
    P.ev("dve", lambda e: e.tensor_tensor(lp[:, 1, :], lv[:, 2, :], lv[:, 3, :], ALU.mult), [lv], [lp])
    P.ev("dve", lambda e: e.reduce_sum(l2[:], lp[:], AX.X), [lp], [l2])
    P.ev("act", lambda e: e.activation(l2[:], l2[:], AF.Exp), [l2], [l2])
    l1 = P.sb([1, 1], F32, "l1")
    P.ev("dve", lambda e: e.scalar_tensor_tensor(l1[:], l2[:, 1:2], -lambda_init, l2[:, 0:1], ALU.add, ALU.subtract), [l2], [l1])
    ones1 = C["ones1"]
    pb = P.ps[7]
    P.mm(pb[:, 0:1], pb, ones1[0:1, :], ones1, l1[:], l1)
    nlam = P.sb([128, 1], F32, "nlam")
    P.copy("dve", nlam[:], nlam, pb[:, 0:1], pb)
    subw = P.sb([128, 256], F32, "subw")
    P.load(subw[:], subw, A["da_subln_w"][j:j + 1, :].partition_broadcast(128))
    P.ev("dve", lambda e: e.tensor_scalar(subw[:], subw[:], 1.0 - lambda_init, None, ALU.mult), [subw], [subw])
    mask = C["mask"]

    qkb = [[P.sb([128, T], BF16, "qk") for _ in range(4)] for _ in range(2)]
    vh = [P.sb([128, NT, 257], BF16, "vh") for _ in range(2)]
    for v_ in vh:
        P.ev("pool", lambda e, v_=v_: e.memset(v_[:, :, 256:257], 1.0), [], [v_])
    ptb = [P.sb([128, 512], BF16, "pt") for _ in range(3)]
    gt2 = [P.sb([128, 256], F32, "gt2") for _ in range(4)]
    o1 = [P.sb([128, 256], F32, "o1") for _ in range(4)]
    dd = [P.sb([128, 256], F32, "dd") for _ in range(4)]
    rs = [P.sb([128, 4], F32, "rs") for _ in range(4)]
    pend2 = []
    jk = P.sb([128, 256], BF16, "jk")
    ogT = [P.sb([128, 2, 128], BF16, "ogT") for _ in range(4)]
    scale = 128 ** -0.5
    nsb = T // 256
    ipair = 0
    ifin = 0
    for h in range(8):
        qk = qkb[h % 2]
        vt = vh[h % 2]
        P.load(qk[0][:], qk[0], QT[2 * h, :, :], dbuf=bQT)
        P.load(qk[1][:], qk[1], QT[2 * h + 1, :, :], dbuf=bQT)
        P.load(qk[2][:], qk[2], KT[2 * h, :, :], dbuf=bKT)
        P.load(qk[3][:], qk[3], KT[2 * h + 1, :, :], dbuf=bKT)
        P.load(vt[:, :, 0:256], vt, V[:, h * 256:(h + 1) * 256].rearrange("(n p) c -> p n c", p=128), dbuf=bV)
        acc = [[P.ps[2 + c * 2 + s_] for s_ in range(2)] for c in range(2)]
        pairs = [(qs, kb_) for qs in range(nsb) for kb_ in range(2 * qs + 2)]
        slots = {}

        def emit_S(idx):
            nonlocal ipair
            qs, kb_ = pairs[idx]
            nkb = 2 * qs + 2
            last = (kb_ == nkb - 1)
            psb = P.ps[ipair % 2]
            pt = ptb[ipair % 3]
            ipair += 1
            slots[idx] = pt
            q0 = 128 if last else 0
            for c in range(2):
                P.mm(psb[:, c * 256 + q0:(c + 1) * 256], psb, qk[2 + c][:, kb_ * 128:(kb_ + 1) * 128], qk[2 + c],
                     qk[c][:, qs * 256 + q0:(qs + 1) * 256], qk[c])
            if last:
                pin = psb[:].rearrange("p (c q) -> p c q", c=2)[:, :, 128:256]
                pout = pt[:].rearrange("p (c q) -> p c q", c=2)[:, :, 128:256]
            else:
                pin = psb[:]
                pout = pt[:]
            P.ev("act", lambda e, pin=pin, pout=pout: e.activation(pout, pin, AF.Exp, scale=scale), [psb], [pt])
            if kb_ >= nkb - 2:
                sub = kb_ - (nkb - 2)
                pm = pt[:].rearrange("p (c q) -> p c q", c=2)[:, :, sub * 128:(sub + 1) * 128]
                P.ev("pool", lambda e, pm=pm: e.tensor_tensor(pm, pm, mask[:].unsqueeze(1).broadcast_to([128, 2, 128]), ALU.mult),
                     [pt, mask], [pt])

        def emit_AV(idx):
            qs, kb_ = pairs[idx]
            pt = slots.pop(idx)
            for s_ in range(2):
                if kb_ > 2 * qs + s_:
                    continue
                for c in range(2):
                    a_ = acc[c][s_]
                    P.mm(a_[:, 0:257], a_, pt[:, c * 256 + s_ * 128:c * 256 + (s_ + 1) * 128], pt, vt[:, kb_, :], vt,
                         start=(kb_ == 0), stop=(kb_ == 2 * qs + s_))

        def finalize(qs):
            nonlocal ifin
            for s_ in range(2):
                tb = qs * 2 + s_
                a1, a2 = acc[0][s_], acc[1][s_]
                g_, o_, d_, r_, og_ = gt2[ifin % 4], o1[ifin % 4], dd[ifin % 4], rs[ifin % 4], ogT[ifin % 4]
                P.load(g_[:], g_, G[tb * 128:(tb + 1) * 128, h * 256:(h + 1) * 256], dbuf=bG)
                P.ev("dve", lambda e, r_=r_, a1=a1: e.reciprocal(r_[:, 0:1], a1[:, 256:257]), [a1], [r_])
                P.ev("dve", lambda e, r_=r_, a2=a2: e.reciprocal(r_[:, 1:2], a2[:, 256:257]), [a2], [r_])
                P.ev("dve", lambda e, r_=r_: e.tensor_tensor(r_[:, 1:2], r_[:, 1:2], nlam[:], ALU.mult), [r_, nlam], [r_])
                P.ev("act", lambda e, o_=o_, a1=a1, r_=r_: e.activation(o_[:], a1[:, 0:256], AF.Copy, scale=r_[:, 0:1]), [a1, r_], [o_])
                P.ev("dve", lambda e, d_=d_, a2=a2, r_=r_, o_=o_: e.scalar_tensor_tensor(d_[:], a2[:, 0:256], r_[:, 1:2], o_[:], ALU.mult, ALU.add),
                     [a2, r_, o_], [d_])
                P.ev("act", lambda e, d_=d_, r_=r_: e.activation(jk[:], d_[:], AF.Square, accum_out=r_[:, 2:3]), [d_], [jk, r_])
                P.ev("act", lambda e, r_=r_: e.activation(r_[:, 2:3], r_[:, 2:3], AF.Sqrt, bias=NORM_EPS, scale=1.0 / 256), [r_], [r_])
                P.ev("dve", lambda e, r_=r_: e.reciprocal(r_[:, 2:3], r_[:, 2:3]), [r_], [r_])
                P.ev("dve", lambda e, d_=d_, r_=r_: e.scalar_tensor_tensor(d_[:], d_[:], r_[:, 2:3], subw[:], ALU.mult, ALU.mult), [d_, r_, subw], [d_])
                P.ev("pool", lambda e, d_=d_, g_=g_: e.tensor_tensor(d_[:], d_[:], g_[:], ALU.mult), [d_, g_], [d_])
                def finb(d_=d_, og_=og_, pt_=P.ps[6 + ifin % 2], tb=tb, h=h):
                    for a in range(2):
                        P.tr(pt_[:, a * 128:(a + 1) * 128], pt_, d_[:, a * 128:(a + 1) * 128], d_, ident)
                    P.ev("act", lambda e: e.activation(og_[:], pt_[:, 0:256].rearrange("p (a b) -> p a b", a=2), AF.Copy), [pt_], [og_])
                    P.store(OGT[h * 256:(h + 1) * 256, tb * 128:(tb + 1) * 128].rearrange("(a p) t -> p a t", p=128), og_[:], og_, dbuf=bOGT)
                pend2.append([3, finb])
                ifin += 1

        npairs = len(pairs)
        emit_S(0)
        for idx in range(npairs):
            if idx + 1 < npairs:
                emit_S(idx + 1)
            emit_AV(idx)
            for p_ in pend2:
                p_[0] -= 1
            while pend2 and pend2[0][0] <= 0:
                pend2.pop(0)[1]()
            qs, kb_ = pairs[idx]
            if kb_ == 2 * qs + 1:
                finalize(qs)
        while pend2:
            pend2.pop(0)[1]()
    P.release(m2)

    stage_outproj(P, OGT, bOGT, A["da_w_out"][j], x_d, x_buf, xo_d, xo_buf)


def setup_consts(P, A, C):
    T = P.T
    NT = T // 128
    ident = P.sb([128, 128], F32, "ident")
    P.load(ident[:], ident, A["c_ident"])
    C["ident"] = ident
    ones1 = P.sb([1, 128], F32, "ones1")
    P.ev("dve", lambda e: e.memset(ones1[:], 1.0), [], [ones1])
    C["ones1"] = ones1
    mask = P.sb([128, 128], BF16, "mask")
    P.load(mask[:], mask, A["c_mask"], q="pool")
    C["mask"] = mask
    for nm, shp in (("c_m4", [128, 4, 128]), ("c_ml", [128, 128]), ("c_sel", [128, 64]), ("c_bones", [128, 128]), ("c_bdm", [128, 2])):
        t = P.sb(shp, F32, nm)
        P.load(t[:], t, A[nm])
        C[nm] = t
    onesc = P.sb([128, 1], F32, "onesc")
    P.ev("dve", lambda e: e.memset(onesc[:], 1.0), [], [onesc])
    C["onesc"] = onesc


def setup_rope(P, A, C):
    T = P.T
    NT = T // 128
    cos = P.sb([128, NT, 64], F32, "cos")
    sin = P.sb([128, NT, 64], F32, "sin")
    m = P.mark()
    posi = P.sb([128, NT], I32, "posi")
    posf = P.sb([128, NT], F32, "posf")
    inv = P.sb([128, 64], F32, "inv")
    ang = P.sb([128, NT, 64], F32, "ang")
    n_ = P.sb([128, NT, 64], F32, "n_")
    P.load(posi[:], posi, A["pos"])
    P.load(inv[:], inv, A["c_inv"])
    P.ev("dve", lambda e: e.tensor_copy(posf[:], posi[:]), [posi], [posf])
    P.ev("dve", lambda e: e.tensor_tensor(ang[:], posf[:].unsqueeze(2).broadcast_to([128, NT, 64]),
                                          inv[:].unsqueeze(1).broadcast_to([128, NT, 64]), ALU.mult), [posf, inv], [ang])
    TWO_PI = 2.0 * math.pi
    MAGIC = 12582912.0
    for tab, shift in ((sin, 0.0), (cos, math.pi / 2)):
        P.ev("dve", lambda e, shift=shift: e.tensor_scalar(n_[:], ang[:], shift, 1.0 / TWO_PI, ALU.add, ALU.mult), [ang], [n_])
        P.ev("dve", lambda e: e.tensor_scalar(n_[:], n_[:], MAGIC, -MAGIC, ALU.add, ALU.add), [n_], [n_])
        P.ev("dve", lambda e, tab=tab: e.scalar_tensor_tensor(tab[:], n_[:], -TWO_PI, ang[:], ALU.mult, ALU.add), [n_, ang], [tab])
        P.ev("dve", lambda e, tab=tab, shift=shift: e.tensor_scalar(tab[:], tab[:], shift, 3.14159, ALU.add, ALU.min), [tab], [tab])
        P.ev("dve", lambda e, tab=tab: e.tensor_scalar(tab[:], tab[:], -3.14159, None, ALU.max), [tab], [tab])
        P.ev("act", lambda e, tab=tab: e.activation(tab[:], tab[:], AF.Sin), [tab], [tab])
    P.release(m)
    C["cos"], C["sin"] = cos, sin


RWKV_W = ["rw_w_in", "rw_decay_up", "rw_iclr_up", "rw_gn_w", "rw_gn_b", "rw_w_out", "rw_vres_down", "rw_vres_up"]
ATTN_W = ["norm_w", "da_w_in", "da_q_gain", "da_k_gain", "da_lam_q1", "da_lam_k1", "da_lam_q2", "da_lam_k2",
          "da_subln_w", "da_w_out"]


def build_program(T, layers, shapes):
    P = Prog(T)
    A = {}
    for nm, (shp, dt) in shapes.items():
        A[nm] = P.din(nm, shp, dt)
    out_d = P.dout("out", [T, D], F32)
    C = {}
    C["QT"] = P.dscratch("QT", [16, 128, T], BF16)
    C["KT"] = P.dscratch("KT", [16, 128, T], BF16)
    C["V"] = P.dscratch("V", [T, D], BF16)
    C["G"] = P.dscratch("G", [T, D], F32)
    C["OGT"] = P.dscratch("OGT", [D, T], BF16)
    for nm in ["QT", "KT", "V", "G", "OGT"]:
        C["b" + nm] = Buf(nm)
    C["PT"] = [P.dscratch("PT%d" % i, [67 * 128, T], F32) for i in range(2)]
    C["bPT"] = [Buf("PT0"), Buf("PT1")]
    xs = [P.dscratch("XA", [T, D], F32), P.dscratch("XB", [T, D], F32)]
    xbufs = [Buf("XA"), Buf("XB")]
    setup_consts(P, A, C)
    cur, cur_b = A["x"], Buf("x")
    for n, li in enumerate(layers):
        if n == len(layers) - 1:
            nxt, nxt_b = out_d, Buf("out")
        else:
            nxt, nxt_b = xs[n % 2], xbufs[n % 2]
        if li % 2 == 0:
            attn_layer(P, li, li // 2, cur, cur_b, nxt, nxt_b, A, C)
        else:
            rwkv_layer(P, li, li // 2, cur, cur_b, nxt, nxt_b, A, C)
        cur, cur_b = nxt, nxt_b
    P.kb.finish()
    P.kb.emit()
    return P


def rwkv_layer(P, li, j, x_d, x_buf, xo_d, xo_buf, A, C):
    T = P.T
    ident = C["ident"]
    PT, bPT = C["PT"][j], C["bPT"][j]
    has_vres = j > 0
    OGT, bOGT = C["OGT"], C["bOGT"]
    m0 = P.mark()
    hnT = P.sb([128, 16, T], BF16, "hnT")
    stage_norm_T(P, x_d, x_buf, A["norm_w"][li:li + 1, :], hnT, ident)

    TH = min(2048, T)
    NHALF = T // TH
    mucol = P.sb([128, 66], F32, "mucol")
    P.load(mucol[:], mucol, A["rw_mucol"][j])
    if has_vres:
        vmucol = P.sb([128, 1], F32, "vmucol")
        P.load(vmucol[:], vmucol, A["rw_vmucol"][j - 1])
    wb = [P.sb([128, 16, 512], BF16, "wb") for _ in range(2)]
    pT = [P.sb([128, TH + 1], F32, "pT") for _ in range(2)]
    dT = P.sb([128, TH], F32, "dT")
    res = [P.sb([128, TH], F32, "res") for _ in range(2)]
    wv = A["rw_w_in"][j].rearrange("(kc p) n -> p kc n", p=128)
    groups = []
    for g in range(16):
        groups.append((wv, g * 512, 512, [(g * 4 + i, 128, i * 128, mucol[:, g * 4 + i:g * 4 + i + 1]) for i in range(4)]))
    groups.append((wv, 8192, 192, [(64, 96, 0, mucol[0:96, 64:65]), (65, 96, 96, mucol[0:96, 65:66])]))
    if has_vres:
        groups.append((A["rw_vres_down"][j - 1].rearrange("(kc p) n -> p kc n", p=128), 0, 64, [(66, 64, 0, vmucol[0:64, 0:1])]))
    ib = it = ir = 0

    def load_wg(gi):
        src, c0, ncg, blks = groups[gi]
        w = wb[gi % 2]
        for h2 in range(2):
            P.load(w[:, h2 * 8:(h2 + 1) * 8, 0:ncg], w, src[:, h2 * 8:(h2 + 1) * 8, c0:c0 + ncg], q="pool")

    load_wg(0)
    for gi, (src, c0, ncg, blks) in enumerate(groups):
        w = wb[gi % 2]
        if gi + 1 < len(groups):
            load_wg(gi + 1)
        for (cb, ncols, wc, mu_ap) in blks:
            p_ = pT[ib % 2]
            ib += 1
            for half in range(NHALF):
                if half == 0:
                    P.ev("pool", lambda e, p_=p_: e.memset(p_[:, 0:1], 0.0), [], [p_])
                else:
                    P.ev("pool", lambda e, p_=p_: e.tensor_copy(p_[:, 0:1], p_[:, TH:TH + 1]), [p_], [p_])
                for tg in range(TH // 512):
                    t0 = half * TH + tg * 512
                    pb = P.ps[it % 2]
                    it += 1
                    for kc in range(16):
                        P.mm(pb[0:ncols, :], pb, w[:, kc, wc:wc + ncols], w, hnT[:, kc, t0:t0 + 512], hnT, start=(kc == 0), stop=(kc == 15))
                    P.ev("act", lambda e, p_=p_, pb=pb, tg=tg, ncols=ncols: e.activation(p_[0:ncols, 1 + tg * 512:1 + (tg + 1) * 512], pb[0:ncols, :], AF.Copy), [pb], [p_])
                r_ = res[ir % 2]
                ir += 1
                P.ev("dve", lambda e, p_=p_, ncols=ncols: e.tensor_tensor(dT[0:ncols, :], p_[0:ncols, 0:TH], p_[0:ncols, 1:TH + 1], ALU.subtract), [p_], [dT])
                P.ev("dve", lambda e, p_=p_, r_=r_, ncols=ncols, mu_ap=mu_ap: e.scalar_tensor_tensor(r_[0:ncols, :], dT[0:ncols, :], mu_ap, p_[0:ncols, 1:TH + 1], ALU.mult, ALU.add),
                     [dT, p_, mucol], [r_])
                P.store(PT[cb * 128:cb * 128 + ncols, half * TH:(half + 1) * TH], r_[0:ncols, :], r_, dbuf=bPT)
    P.release(m0)

    m3 = P.mark()
    TG = 256
    NC_ = 4
    NTG = T // TG
    dup = P.sb([96, D], F32, "dup")
    iup = P.sb([96, D], F32, "iup")
    P.load(dup[:], dup, A["rw_decay_up"][j])
    P.load(iup[:], iup, A["rw_iclr_up"][j])
    vecs = P.sb([128, 5, 16], F32, "vecs")
    P.load(vecs[:], vecs, A["rw_vecs"][j])
    omka = P.sb([128, 16], F32, "omka")
    P.ev("dve", lambda e: e.tensor_scalar(omka[:], vecs[:, 3, :], -1.0, 1.0, ALU.mult, ALU.add), [vecs], [omka])
    gnw = P.sb([128, 16, 64], F32, "gnw")
    gnb = P.sb([128, 16, 64], F32, "gnb")
    for h in range(2):
        for (dst, nm) in ((gnw, "rw_gn_w"), (gnb, "rw_gn_b")):
            srcv = A[nm][j:j + 1, :].rearrange("o (hp h v) -> o hp h v", h=2, v=64)[:, :, h, :]
            P.load(dst[h * 64:(h + 1) * 64, :, :], dst, srcv.partition_broadcast(64))
    if has_vres:
        vup = P.sb([64, D], F32, "vup")
        P.load(vup[:], vup, A["rw_vres_up"][j - 1])
        v0c = P.sb([128, 16], F32, "v0c")
        P.load(v0c[:], v0c, A["rw_v0col"][j - 1])
    H = [P.sb([128, 64], F32, "H") for _ in range(16)]
    for h_ in H:
        P.ev("pool", lambda e, h_=h_: e.memset(h_[:], 0.0), [], [h_])
    m4, ml, sel, bones, bdm, onesc = C["c_m4"], C["c_ml"], C["c_sel"], C["c_bones"], C["c_bdm"], C["onesc"]
    mu4 = P.sb([128, NC_, 128], F32, "mu4")
    mui4 = P.sb([128, NC_, 128], F32, "mui4")
    ml4 = P.sb([128, NC_, 128], F32, "ml4")
    for n in range(NC_):
        P.ev("pool", lambda e, n=n: e.tensor_copy(mu4[:, n, :], m4[:, 0, :]), [m4], [mu4])
        P.ev("pool", lambda e, n=n: e.tensor_copy(mui4[:, n, :], m4[:, 1, :]), [m4], [mui4])
        P.ev("pool", lambda e, n=n: e.tensor_copy(ml4[:, n, :], ml[:]), [ml], [ml4])

    W = TG
    tdw = [P.sb([96, W], F32, "tdw") for _ in range(2)]
    tda = [P.sb([96, W], F32, "tda") for _ in range(2)]
    tpv = [P.sb([64, W], F32, "tpv") for _ in range(2)] if has_vres else None
    tmpn = ["lw", "a", "kk", "sqk", "nrm", "kkn", "tm", "k2", "bb", "dcs", "eneg", "eprev", "at", "rt", "bt", "kt", "rk"]

    def make_set():
        S = {}
        for k in ["rT", "kT", "vT", "gT"] + (["vf"] if has_vres else []):
            S[k] = P.sb([128, W], F32, k)
        for k in tmpn:
            S[k] = P.sb([128, W], F32, k)
        S["lwtm"] = P.sb([128, W // 128, 128], F32, "lwtm")
        S["epos"] = P.sb([128, W], F32, "epos")
        S["sg"] = P.sb([128, W], F32, "sg")
        S["bdAR"] = P.sb([128, NC_, 2, 2, 64], F32, "bdAR")
        for k in ["bdB", "bdK", "bdV", "bdRK", "bdW", "bdY"]:
            S[k] = P.sb([128, NC_, 2, 64], F32, k)
        S["Nst"] = [P.sb([128, NC_, 128], F32, "Nst") for _ in range(2)]
        S["Ast"] = [P.sb([128, NC_, 128], F32, "Ast") for _ in range(2)]
        for k in ["Nk", "Nrb", "Nrk", "bdWT", "bdBtm", "bdKtm"]:
            S[k] = P.sb([128, NC_, 128], F32, k)
        S["Vtm"] = P.sb([128, NC_, 64], F32, "Vtm")
        S["sbon"] = P.sb([128, NC_], F32, "sbon")
        S["Z"] = [P.sb([128, NC_, 128], F32, "Z") for _ in range(3)]
        S["U"] = [P.sb([128, 64], F32, "U") for _ in range(2)]
        S["Ht"] = P.sb([128, 64], F32, "Ht")
        S["Ysb"] = P.sb([128, NC_, 64], F32, "Ysb")
        S["Ysq"] = P.sb([128, NC_, 64], F32, "Ysq")
        S["st"] = P.sb([128, 4, NC_], F32, "st")
        S["ycm"] = P.sb([128, W], F32, "ycm")
        S["ogT"] = P.sb([128, W], BF16, "ogT")
        return S

    SETS = [make_set(), make_set()]
    PTv = C["PT"][0]
    bPTv = C["bPT"][0]
    EXPM05 = math.exp(-0.5)

    def v3(t):
        return t[:].rearrange("p (n t) -> p n t", n=NC_)

    def expand(out4, src_tt):
        in0 = v3(src_tt).unsqueeze(2).broadcast_to([128, NC_, 2, 64])
        in1 = bdm[:].unsqueeze(1).unsqueeze(3).broadcast_to([128, NC_, 2, 64])
        return lambda e: e.tensor_tensor(out4, in0, in1, ALU.mult)

    def ncols(n):
        return slice(n * 128, (n + 1) * 128)

    def unit(tg, hp, par, dwt, dat, pvt):
        S = SETS[par]
        c0 = tg * TG
        b0, b1, b2, b3 = [P.ps[4 * par + i] for i in range(4)]
        rT, kT, vT, gT = S["rT"], S["kT"], S["vT"], S["gT"]
        for sec, t_ in enumerate((rT, kT, vT, gT)):
            r0 = (sec * 16 + hp) * 128
            P.load(t_[:], t_, PT[r0:r0 + 128, c0:c0 + W], dbuf=bPT)
        hs = slice(hp * 128, (hp + 1) * 128)
        col = lambda i: vecs[:, i, hp:hp + 1]
        lw, a_, kk, sqk, nrm, kkn, tmq, k2, bb, dcs, eneg, eprev, at, rt, bt, kt, rk = [S[k] for k in tmpn]
        ep, sg_, lwtm = S["epos"], S["sg"], S["lwtm"]
        AR, nrb, nrk, vtm, sbn = S["bdAR"], S["Nrb"], S["Nrk"], S["Vtm"], S["sbon"]
        bdB, bdK, bdV, bdRK, bdW, bdY = S["bdB"], S["bdK"], S["bdV"], S["bdRK"], S["bdW"], S["bdY"]
        Nst, Ast, Nk, Z, U = S["Nst"], S["Ast"], S["Nk"], S["Z"], S["U"]
        wT, btm, ktm = S["bdWT"], S["bdBtm"], S["bdKtm"]
        Ht, Ysb, Ysq, st, ycm, og_ = S["Ht"], S["Ysb"], S["Ysq"], S["st"], S["ycm"], S["ogT"]
        P.mm(b0[:, 0:W], b0, dup[0:96, hs], dup, dwt[0:96, :], dwt)
        P.mm(b1[:, 0:W], b1, iup[0:96, hs], iup, dat[0:96, :], dat)
        P.ev("act", lambda e, b=col(0): e.activation(lw[:], b0[:, 0:W], AF.Sigmoid, bias=b), [b0, vecs], [lw])
        P.ev("dve", lambda e: e.tensor_scalar(lw[:], lw[:], -EXPM05, None, ALU.mult), [lw], [lw])
        P.ev("act", lambda e, b=col(1): e.activation(a_[:], b1[:, 0:W], AF.Sigmoid, bias=b), [b1, vecs], [a_])
        P.ev("dve", lambda e, s_=col(2): e.tensor_scalar(kk[:], kT[:], s_, None, ALU.mult), [kT, vecs], [kk])
        P.ev("act", lambda e: e.activation(sqk[:], kk[:], AF.Square), [kk], [sqk])
        yield
        P.mm(b2[:, 0:W], b2, bones[:], bones, sqk[:], sqk)
        if has_vres:
            vf = S["vf"]
            r0 = (32 + hp) * 128
            P.load(vf[:], vf, PTv[r0:r0 + 128, c0:c0 + W], dbuf=bPTv)
            P.mm(b0[:, 256:256 + W], b0, vup[0:64, hs], vup, pvt[0:64, :], pvt)
        for i in range(W // 128):
            P.tr(b1[:, 256 + i * 128:256 + (i + 1) * 128], b1, lw[:, i * 128:(i + 1) * 128], lw, ident)
        P.ev("act", lambda e: e.activation(nrm[:], b2[:, 0:W], AF.Sqrt), [b2], [nrm])
        P.ev("dve", lambda e: e.tensor_scalar(nrm[:], nrm[:], 1e-12, None, ALU.max), [nrm], [nrm])
        P.ev("dve", lambda e: e.reciprocal(nrm[:], nrm[:]), [nrm], [nrm])
        P.ev("pool", lambda e: e.tensor_tensor(kkn[:], kk[:], nrm[:], ALU.mult), [kk, nrm], [kkn])
        P.ev("dve", lambda e, s1=col(3), s2=omka[:, hp:hp + 1]: e.tensor_scalar(tmq[:], a_[:], s1, s2, ALU.mult, ALU.add), [a_, vecs, omka], [tmq])
        P.ev("pool", lambda e: e.tensor_tensor(k2[:], kT[:], tmq[:], ALU.mult), [kT, tmq], [k2])
        P.ev("dve", lambda e: e.tensor_tensor(bb[:], kkn[:], a_[:], ALU.mult), [kkn, a_], [bb])
        if has_vres:
            P.ev("act", lambda e, b=v0c[:, hp:hp + 1]: e.activation(tmq[:], b0[:, 256:256 + W], AF.Sigmoid, bias=b), [b0, v0c], [tmq])
            P.ev("pool", lambda e: e.tensor_tensor(vf[:], vf[:], vT[:], ALU.subtract), [vf, vT], [vf])
            P.ev("dve", lambda e: e.tensor_tensor(vf[:], vf[:], tmq[:], ALU.mult), [vf, tmq], [vf])
            P.ev("pool", lambda e: e.tensor_tensor(vT[:], vT[:], vf[:], ALU.add), [vf, vT], [vT])
        P.ev("act", lambda e: e.activation(lwtm[:], b1[:, 256:256 + W].rearrange("p (a b) -> p a b", b=128), AF.Copy), [b1], [lwtm])
        yield
        for i in range(W // 128):
            P.mm(b2[:, 256 + i * 128:256 + (i + 1) * 128], b2, lwtm[:, i, :], lwtm, m4[:, 1, :], m4)
        cs = b2[:, 256:256 + W]
        P.ev("act", lambda e: e.activation(ep[:], cs, AF.Exp), [b2], [ep])
        P.ev("act", lambda e: e.activation(eneg[:], cs, AF.Exp, scale=-1.0), [b2], [eneg])
        P.ev("dve", lambda e: e.tensor_tensor(dcs[:], cs, lw[:], ALU.subtract), [b2, lw], [dcs])
        P.ev("act", lambda e: e.activation(eprev[:], dcs[:], AF.Exp), [dcs], [eprev])
        P.ev("act", lambda e: e.activation(sg_[:], gT[:], AF.Silu), [gT], [sg_])
        P.ev("dve", lambda e: e.scalar_tensor_tensor(at[:], kkn[:], -1.0, eprev[:], ALU.mult, ALU.mult), [kkn, eprev], [at])
        P.ev("pool", lambda e: e.tensor_tensor(rt[:], rT[:], ep[:], ALU.mult), [rT, ep], [rt])
        P.ev("dve", lambda e: e.tensor_tensor(bt[:], bb[:], eneg[:], ALU.mult), [bb, eneg], [bt])
        P.ev("pool", lambda e: e.tensor_tensor(kt[:], k2[:], eneg[:], ALU.mult), [k2, eneg], [kt])
        P.ev("dve", lambda e, s_=col(4): e.scalar_tensor_tensor(rk[:], rT[:], s_, k2[:], ALU.mult, ALU.mult), [rT, k2, vecs], [rk])
        P.ev("dve", expand(AR[:, :, 0, :, :], at), [at, bdm], [AR])
        P.ev("pool", expand(AR[:, :, 1, :, :], rt), [rt, bdm], [AR])
        P.ev("dve", expand(bdB[:], bt), [bt, bdm], [bdB])
        P.ev("pool", expand(bdK[:], kt), [kt, bdm], [bdK])
        P.ev("dve", expand(bdV[:], vT), [vT, bdm], [bdV])
        P.ev("pool", expand(bdRK[:], rk), [rk, bdm], [bdRK])
        yield
        f2 = lambda t, n: t[:, n, :, :].rearrange("p a b -> p (a b)")
        bA = lambda n: AR[:, n, 0, :, :].rearrange("p a b -> p (a b)")
        bR = lambda n: AR[:, n, 1, :, :].rearrange("p a b -> p (a b)")
        v4 = lambda q: q[:].rearrange("p (n t) -> p n t", n=NC_)
        N0, A0 = Nst[0], Ast[0]
        for n in range(NC_):
            P.mm(b3[:, ncols(n)], b3, f2(bdB, n), bdB, bA(n), AR)
        P.ev("dve", lambda e: e.tensor_tensor(N0[:], v4(b3), mu4[:], ALU.mult), [b3, mu4], [N0])
        for n in range(NC_):
            P.mm(b0[:, ncols(n)], b0, bA(n), AR, f2(bdB, n), bdB)
        P.ev("dve", lambda e: e.tensor_tensor(A0[:], v4(b0), ml4[:], ALU.mult), [b0, ml4], [A0])
        for n in range(NC_):
            P.mm(b1[:, ncols(n)], b1, f2(bdK, n), bdK, bA(n), AR)
        P.ev("dve", lambda e: e.tensor_tensor(Nk[:], v4(b1), mu4[:], ALU.mult), [b1, mu4], [Nk])
        yield
        for n in range(NC_):
            P.mm(b2[:, ncols(n)], b2, f2(bdB, n), bdB, bR(n), AR)
        P.ev("dve", lambda e: e.tensor_tensor(nrb[:], v4(b2), mui4[:], ALU.mult), [b2, mui4], [nrb])
        for n in range(NC_):
            P.mm(b3[:, ncols(n)], b3, f2(bdK, n), bdK, bR(n), AR)
        P.ev("dve", lambda e: e.tensor_tensor(nrk[:], v4(b3), mui4[:], ALU.mult), [b3, mui4], [nrk])
        Zc = Z[0]
        for n in range(NC_):
            P.mm(b0[:, n * 64:(n + 1) * 64], b0, f2(bdV, n), bdV, sel[:], sel)
        for n in range(NC_):
            P.mm(b0[:, 256 + n * 64:256 + (n + 1) * 64], b0, bA(n), AR, sel[:], sel)
        P.ev("act", lambda e: e.activation(vtm[:], b0[:, 0:256].rearrange("p (n v) -> p n v", n=NC_), AF.Copy), [b0], [vtm])
        P.ev("act", lambda e, Zc=Zc: e.activation(Zc[:, :, 0:64], b0[:, 256:512].rearrange("p (n v) -> p n v", n=NC_), AF.Copy), [b0], [Zc])
        for n in range(NC_):
            P.mm(b1[:, n:n + 1], b1, f2(bdRK, n), bdRK, onesc[:], onesc)
        P.ev("act", lambda e: e.activation(sbn[:], b1[:, 0:NC_], AF.Copy), [b1], [sbn])
        yield
        for n in range(NC_):
            P.mm(b2[:, n * 64:(n + 1) * 64], b2, Nk[:, n, :], Nk, vtm[:, n, :], vtm)
        P.ev("act", lambda e, Zc=Zc: e.activation(Zc[:, :, 64:128], b2[:, 0:256].rearrange("p (n v) -> p n v", n=NC_), AF.Copy), [b2], [Zc])
        yield
        Ncur, Acur = N0, A0
        zi = 0
        for k in range(6):
            qa = b0 if k % 2 == 0 else b1
            Zn = Z[(zi + 1) % 3]
            for n in range(NC_):
                P.mm(qa[:, ncols(n)], qa, Ncur[:, n, :], Ncur, Zc[:, n, :], Zc)
            P.ev("dve", lambda e, Zn=Zn, qa=qa, Zc=Zc: e.tensor_tensor(Zn[:], v4(qa), Zc[:], ALU.add), [qa, Zc], [Zn])
            Zc = Zn
            zi += 1
            if k < 5:
                Nn = Nst[(k + 1) % 2]
                for n in range(NC_):
                    P.mm(b2[:, ncols(n)], b2, Acur[:, n, :], Acur, Ncur[:, n, :], Ncur)
                if k < 4:
                    An = Ast[(k + 1) % 2]
                    for n in range(NC_):
                        P.mm(b3[:, ncols(n)], b3, Ncur[:, n, :], Ncur, Acur[:, n, :], Acur)
                P.ev("act", lambda e, Nn=Nn: e.activation(Nn[:], v4(b2), AF.Copy), [b2], [Nn])
                if k < 4:
                    P.ev("act", lambda e, An=An: e.activation(An[:], v4(b3), AF.Copy), [b3], [An])
                    Acur = An
                Ncur = Nn
            yield
        Zf = Zc
        P.ev("dve", lambda e: e.tensor_tensor(bdW[:], Zf[:, :, 0:64].unsqueeze(2).broadcast_to([128, NC_, 2, 64]),
                                              bdm[:].unsqueeze(1).unsqueeze(3).broadcast_to([128, NC_, 2, 64]), ALU.mult), [Zf, bdm], [bdW])
        for n in range(NC_):
            P.tr(b1[:, ncols(n)], b1, f2(bdB, n), bdB, ident)
        P.ev("act", lambda e: e.activation(btm[:], v4(b1), AF.Copy), [b1], [btm])
        for n in range(NC_):
            P.tr(b2[:, ncols(n)], b2, f2(bdK, n), bdK, ident)
        P.ev("act", lambda e: e.activation(ktm[:], v4(b2), AF.Copy), [b2], [ktm])
        yield
        for n in range(NC_):
            P.tr(b0[:, ncols(n)], b0, f2(bdW, n), bdW, ident)
        P.ev("act", lambda e: e.activation(wT[:], v4(b0), AF.Copy), [b0], [wT])
        yield
        Hh = H[hp]
        qy = b3
        for n in range(NC_):
            qu = b0 if n % 2 == 0 else b1
            U_ = U[n % 2]
            P.mm(qu[:, 0:64], qu, wT[:, n, :], wT, Hh[:], Hh)
            ys = qy[:, n * 64:(n + 1) * 64]
            P.mm(ys, qy, bR(n), AR, Hh[:], Hh, start=True, stop=False)
            P.ev("dve", lambda e, U_=U_, qu=qu, n=n: e.tensor_tensor(U_[:], qu[:, 0:64], Zf[:, n, 64:128], ALU.add), [qu, Zf], [U_])
            yield
            P.mm(ys, qy, nrb[:, n, :], nrb, U_[:], U_, start=False, stop=False)
            P.mm(ys, qy, nrk[:, n, :], nrk, vtm[:, n, :], vtm, start=False, stop=True)
            P.mm(qu[:, 64:128], qu, btm[:, n, :], btm, U_[:], U_, start=True, stop=False)
            P.mm(qu[:, 64:128], qu, ktm[:, n, :], ktm, vtm[:, n, :], vtm, start=False, stop=True)
            P.ev("dve", lambda e, qu=qu: e.tensor_tensor(Ht[:], qu[:, 64:128], Hh[:], ALU.add), [qu, Hh], [Ht])
            pc = ep[:, n * 64 + 63:n * 64 + 64]
            P.ev("dve", lambda e, pc=pc: e.tensor_scalar(Hh[:], Ht[:], pc, None, ALU.mult), [Ht, ep], [Hh])
            yield
        yv = qy[:, 0:NC_ * 64].rearrange("p (n v) -> p n v", n=NC_)
        P.ev("act", lambda e: e.activation(Ysb[:], yv, AF.Copy), [qy], [Ysb])
        P.ev("act", lambda e: e.activation(Ysq[:], yv, AF.Square), [qy], [Ysq])
        P.ev("dve", lambda e: e.reduce_sum(st[:, 0, :], Ysb[:], AX.X), [Ysb], [st])
        P.ev("dve", lambda e: e.reduce_sum(st[:, 1, :], Ysq[:], AX.X), [Ysq], [st])
        P.ev("dve", lambda e: e.tensor_scalar(st[:, 0, :], st[:, 0, :], 1.0 / 64, None, ALU.mult), [st], [st])
        P.ev("dve", lambda e: e.tensor_tensor(st[:, 2, :], st[:, 0, :], st[:, 0, :], ALU.mult), [st], [st])
        P.ev("dve", lambda e: e.scalar_tensor_tensor(st[:, 1, :], st[:, 1, :], 1.0 / 64, st[:, 2, :], ALU.mult, ALU.subtract), [st], [st])
        P.ev("act", lambda e: e.activation(st[:, 1, :], st[:, 1, :], AF.Sqrt, bias=GN_EPS, scale=1.0), [st], [st])
        P.ev("dve", lambda e: e.reciprocal(st[:, 1, :], st[:, 1, :]), [st], [st])
        yield
        P.ev("dve", lambda e: e.tensor_tensor(Ysb[:], Ysb[:], st[:, 0, :].unsqueeze(2).broadcast_to([128, NC_, 64]), ALU.subtract), [Ysb, st], [Ysb])
        P.ev("dve", lambda e: e.tensor_tensor(Ysb[:], Ysb[:], st[:, 1, :].unsqueeze(2).broadcast_to([128, NC_, 64]), ALU.mult), [Ysb, st], [Ysb])
        P.ev("pool", lambda e: e.tensor_tensor(Ysb[:], Ysb[:], gnw[:, hp, :].unsqueeze(1).broadcast_to([128, NC_, 64]), ALU.mult), [Ysb, gnw], [Ysb])
        P.ev("pool", lambda e: e.tensor_tensor(Ysb[:], Ysb[:], gnb[:, hp, :].unsqueeze(1).broadcast_to([128, NC_, 64]), ALU.add), [Ysb, gnb], [Ysb])
        P.ev("dve", lambda e: e.tensor_tensor(Ysq[:], vtm[:], sbn[:].unsqueeze(2).broadcast_to([128, NC_, 64]), ALU.mult), [vtm, sbn], [Ysq])
        P.ev("dve", lambda e: e.tensor_tensor(Ysb[:], Ysb[:], Ysq[:], ALU.add), [Ysb, Ysq], [Ysb])
        P.ev("pool", lambda e: e.tensor_tensor(bdY[:], Ysb[:].unsqueeze(2).broadcast_to([128, NC_, 2, 64]),
                                               bdm[:].unsqueeze(1).unsqueeze(3).broadcast_to([128, NC_, 2, 64]), ALU.mult), [Ysb, bdm], [bdY])
        yield
        for n in range(NC_):
            P.tr(b1[:, ncols(n)], b1, f2(bdY, n), bdY, ident)
        q3v = b1[:].rearrange("p (n h t) -> p n h t", n=NC_, h=2)
        P.ev("dve", lambda e: e.tensor_copy(v3(ycm), q3v[:, :, 0, :]), [b1], [ycm])
        P.ev("dve", lambda e: e.tensor_tensor(v3(ycm), v3(ycm), q3v[:, :, 1, :], ALU.add), [b1, ycm], [ycm])
        P.ev("pool", lambda e: e.tensor_tensor(og_[:], ycm[:], sg_[:], ALU.mult), [ycm, sg_], [og_])
        P.store(OGT[hp * 128:(hp + 1) * 128, c0:c0 + W], og_[:], og_, dbuf=bOGT)

    work = []
    for tg in range(NTG):
        for hp in range(16):
            work.append((tg, hp))
    tg_loaded = {}

    def tg_inputs(tg):
        if tg not in tg_loaded:
            c0 = tg * TG
            dwt, dat = tdw[tg % 2], tda[tg % 2]
            P.load(dwt[:], dwt, PT[64 * 128:64 * 128 + 96, c0:c0 + W], dbuf=bPT)
            P.load(dat[:], dat, PT[65 * 128:65 * 128 + 96, c0:c0 + W], dbuf=bPT)
            P.ev("act", lambda e, dwt=dwt: e.activation(dwt[:], dwt[:], AF.Tanh), [dwt], [dwt])
            pvt = None
            if has_vres:
                pvt = tpv[tg % 2]
                P.load(pvt[:], pvt, PT[66 * 128:66 * 128 + 64, c0:c0 + W], dbuf=bPT)
            tg_loaded[tg] = (dwt, dat, pvt)
        return tg_loaded[tg]

    nxt = 0
    active = [None, None]

    def start(par):
        nonlocal nxt
        if nxt >= len(work):
            return None
        tg, hp = work[nxt]
        nxt += 1
        dwt, dat, pvt = tg_inputs(tg)
        return unit(tg, hp, par, dwt, dat, pvt)

    active[0] = start(0)
    steps = 0
    while active[0] is not None or active[1] is not None:
        for par in range(2):
            g = active[par]
            if g is None:
                if par == 1 and steps == 8:
                    active[1] = start(1)
                continue
            try:
                next(g)
            except StopIteration:
                active[par] = start(par)
        steps += 1
        if steps > 8 and active[1] is None and nxt < len(work):
            active[1] = start(1)
    P.release(m3)
    stage_outproj(P, OGT, bOGT, A["rw_w_out"][j], x_d, x_buf, xo_d, xo_buf)


def host_consts(T):
    c = {}
    c["c_ident"] = np.eye(128, dtype=np.float32)
    import ml_dtypes
    k = np.arange(128)[:, None]
    q = np.arange(128)[None, :]
    c["c_mask"] = (k <= q).astype(np.float32)
    inv = (10000.0 ** (-np.arange(0, 128, 2, dtype=np.float32) / np.float32(128))).astype(np.float32)
    c["c_inv"] = np.ascontiguousarray(np.broadcast_to(inv[None, :], (128, 64))).astype(np.float32)
    hh = np.arange(128) // 64
    tt = np.arange(128) % 64
    same = (hh[:, None] == hh[None, :])
    mu_ = (same & (tt[:, None] < tt[None, :])).astype(np.float32)
    mui = (same & (tt[:, None] <= tt[None, :])).astype(np.float32)
    ml_ = (same & (tt[:, None] > tt[None, :])).astype(np.float32)
    c["c_m4"] = np.ascontiguousarray(np.stack([mu_, mui, mu_, mui], axis=1))
    c["c_ml"] = ml_
    c["c_sel"] = np.concatenate([np.eye(64), np.eye(64)], axis=0).astype(np.float32)
    c["c_bones"] = same.astype(np.float32)
    c["c_bdm"] = (hh[:, None] == np.arange(2)[None, :]).astype(np.float32)
    return c


def core_inputs(inputs, b, T):
    m = {}
    m["x"] = np.ascontiguousarray(inputs["x"][b, :T])
    pos = np.asarray(inputs["positions"][b, :T]).astype(np.int32)
    m["pos"] = np.ascontiguousarray(pos.reshape(T // 128, 128).T)
    for nm in ATTN_W + RWKV_W:
        m[nm] = np.ascontiguousarray(np.asarray(inputs[nm]))
    NR = inputs["rw_mu"].shape[0]
    mu = np.asarray(inputs["rw_mu"])
    mucol = np.zeros((NR, 128, 66), np.float32)
    mucol[:, :, :64] = mu[:, :8192].reshape(NR, 64, 128).transpose(0, 2, 1)
    mucol[:, :96, 64] = mu[:, 8192:8288]
    mucol[:, :96, 65] = mu[:, 8288:8384]
    m["rw_mucol"] = mucol
    vecs = np.stack([np.asarray(inputs[n]).reshape(NR, 16, 128).transpose(0, 2, 1)
                     for n in ["rw_w0", "rw_a0", "rw_k_k", "rw_k_a", "rw_r_k"]], axis=2)
    m["rw_vecs"] = np.ascontiguousarray(vecs.astype(np.float32))
    NV = inputs["rw_v0"].shape[0]
    m["rw_v0col"] = np.ascontiguousarray(np.asarray(inputs["rw_v0"]).reshape(NV, 16, 128).transpose(0, 2, 1))
    vm = np.zeros((NV, 128, 1), np.float32)
    vm[:, :64, 0] = np.asarray(inputs["rw_vres_mu"])
    m["rw_vmucol"] = vm
    m.update(host_consts(T))
    return m


def kernel(**inputs):
    T = inputs["x"].shape[1]
    B = inputs["x"].shape[0]
    maps = [core_inputs(inputs, c % B, T) for c in range(NCORES)]
    shapes = {k: (list(v.shape), I32 if v.dtype == np.int32 else F32) for k, v in maps[0].items()}
    P = build_program(T, [0, 1, 2, 3], shapes)
    res = run_bass_kernel_spmd(P.nc, maps, core_ids=list(range(NCORES)))
    out = np.stack([res.results[b]["out"] for b in range(B)], axis=0)
    return out.astype(np.float32)
```
